# Optimizing a Trainium2 kernel written in Bass

```python
import math
import jax, jax.numpy as jnp
from jax import lax
import numpy as np

D_MODEL = 1024
BATCH = 4
SEQ = 4096
DEPTH = 1

MEM_LEN = 256
A_HEADS = 8
A_HEAD_DIM = 64
A_WIDTH = A_HEADS * A_HEAD_DIM
MOBA_BLOCK = 256
MOBA_TOPK = 3
MOBA_QCHUNK = 32
B_HEADS = 4
B_KEY_DIM = 64
B_VAL_DIM = 128
B_KEY_WIDTH = B_HEADS * B_KEY_DIM
B_WIDTH = B_HEADS * B_VAL_DIM
GLA_RANK = 16
GLA_TAU = 16.0
GLA_CHUNK = 64
C_HEADS = 4
C_HEAD_DIM = 128
C_WIDTH = C_HEADS * C_HEAD_DIM
MIX_WIDTH = A_WIDTH + B_WIDTH + C_WIDTH
REL_BUCKETS = 32
REL_MAX_DIST = 128
RMS_EPS = 1e-6
NEG = -1e30
IN_COLS = 4 * A_WIDTH + 2 * B_KEY_WIDTH + 2 * B_WIDTH + GLA_RANK + 2 * C_WIDTH

kernel_name = "hymba_moba_gla_xattn_layer"


def rmsnorm(x, w):
    xf = x.astype(jnp.float32)
    y = xf * lax.rsqrt(jnp.mean(xf * xf, axis=-1, keepdims=True) + RMS_EPS)
    return (y * w.astype(jnp.float32)).astype(x.dtype)


def t5_bucket(rel):
    n = jnp.maximum(rel, 0)
    max_exact = REL_BUCKETS // 2
    is_small = n < max_exact
    nf = jnp.maximum(n, max_exact).astype(jnp.float32)
    large = max_exact + (jnp.log(nf / max_exact) / math.log(REL_MAX_DIST / max_exact)
                         * (REL_BUCKETS - max_exact)).astype(jnp.int32)
    large = jnp.minimum(large, REL_BUCKETS - 1)
    return jnp.where(is_small, n, large)


def moba_attention(q, k, v, rel_bias):
    B, S, H, dh = q.shape
    s_pad = -(-S // MOBA_BLOCK) * MOBA_BLOCK
    pad = ((0, 0), (0, s_pad - S), (0, 0), (0, 0))
    q, k, v = jnp.pad(q, pad), jnp.pad(k, pad), jnp.pad(v, pad)
    nb = s_pad // MOBA_BLOCK
    topk = min(MOBA_TOPK, nb)
    n_chunks = s_pad // MOBA_QCHUNK
    scale = dh ** -0.5
    kbh = k.reshape(B, nb, MOBA_BLOCK, H, dh).transpose(0, 3, 1, 2, 4)
    vbh = v.reshape(B, nb, MOBA_BLOCK, H, dh).transpose(0, 3, 1, 2, 4)
    kmean = jnp.mean(kbh.astype(jnp.float32), axis=3)
    qch = q.reshape(B, n_chunks, MOBA_QCHUNK, H, dh).transpose(1, 0, 3, 2, 4)
    table_t = rel_bias.T.astype(jnp.float32)
    head_idx = jnp.arange(H)[None, :, None, None, None]
    blk_ar = jnp.arange(MOBA_BLOCK, dtype=jnp.int32)
    gather = jax.vmap(jax.vmap(lambda kk, ii: kk[ii]))

    def chunk_fn(args):
        qh, ci = args
        pos = ci * MOBA_QCHUNK + jnp.arange(MOBA_QCHUNK, dtype=jnp.int32)
        own = (ci * MOBA_QCHUNK) // MOBA_BLOCK
        gate = jnp.einsum('bhqd,bhnd->bhqn', qh.astype(jnp.float32), kmean)
        past = jnp.arange(nb) < own
        gate = jnp.where(past[None, None, None], gate, NEG)
        _, idx = lax.top_k(gate, topk)
        valid = idx < own
        k_sel = gather(kbh, idx)
        v_sel = gather(vbh, idx)
        key_pos = idx[..., None] * MOBA_BLOCK + blk_ar
        bias_sel = table_t[head_idx, t5_bucket(pos[None, None, :, None, None] - key_pos)]
        s_sel = jnp.einsum('bhqd,bhqnjd->bhqnj', qh, k_sel).astype(jnp.float32) * scale + bias_sel
        s_sel = jnp.where(valid[..., None], s_sel, NEG)
        k_own = lax.dynamic_index_in_dim(kbh, own, axis=2, keepdims=False)
        v_own = lax.dynamic_index_in_dim(vbh, own, axis=2, keepdims=False)
        own_pos = own * MOBA_BLOCK + blk_ar
        bias_own = table_t[:, t5_bucket(pos[:, None] - own_pos[None, :])]
        s_own = jnp.einsum('bhqd,bhjd->bhqj', qh, k_own).astype(jnp.float32) * scale + bias_own[None]
        causal = own_pos[None, :] <= pos[:, None]
        s_own = jnp.where(causal[None, None], s_own, NEG)
        logits = jnp.concatenate([s_own, s_sel.reshape(B, H, MOBA_QCHUNK, topk * MOBA_BLOCK)], axis=-1)
        p = jax.nn.softmax(logits, axis=-1).astype(v.dtype)
        p_own = p[..., :MOBA_BLOCK]
        p_sel = p[..., MOBA_BLOCK:].reshape(B, H, MOBA_QCHUNK, topk, MOBA_BLOCK)
        o = (jnp.einsum('bhqj,bhjd->bhqd', p_own, v_own)
             + jnp.einsum('bhqnj,bhqnjd->bhqd', p_sel, v_sel))
        return o.transpose(0, 2, 1, 3)

    out = lax.map(chunk_fn, (qch, jnp.arange(n_chunks, dtype=jnp.int32)))
    out = out.transpose(1, 0, 2, 3, 4).reshape(B, s_pad, H, dh)
    return out[:, :S]


def gla_attention(q, k, v, g):
    B, S, H, dk = q.shape
    dv = v.shape[-1]
    n = S // GLA_CHUNK

    def to_chunks(t):
        return t.astype(jnp.float32).reshape(B, n, GLA_CHUNK, H, t.shape[-1]).transpose(1, 0, 3, 2, 4)

    qs = to_chunks(q) * (dk ** -0.5)
    ks, vs, gs = to_chunks(k), to_chunks(v), to_chunks(g)
    mask = jnp.tril(jnp.ones((GLA_CHUNK, GLA_CHUNK), dtype=bool))[None, None, :, :, None]

    def step(state, inp):
        qc, kc, vc, gc = inp
        b = jnp.cumsum(gc, axis=2)
        o_inter = jnp.einsum('bhtk,bhkv->bhtv', qc * jnp.exp(b), state)
        diff = b[:, :, :, None, :] - b[:, :, None, :, :]
        decay = jnp.where(mask, jnp.exp(jnp.where(mask, diff, 0.0)), 0.0)
        att = jnp.einsum('bhtk,bhsk,bhtsk->bhts', qc, kc, decay)
        o_intra = jnp.einsum('bhts,bhsv->bhtv', att, vc)
        b_last = b[:, :, -1:, :]
        new_state = (jnp.exp(b_last[:, :, 0, :])[..., None] * state
                     + jnp.einsum('bhsk,bhsv->bhkv', kc * jnp.exp(b_last - b), vc))
        return new_state, o_inter + o_intra

    state0 = jnp.zeros((B, H, dk, dv), jnp.float32)
    _, ys = lax.scan(step, state0, (qs, ks, vs, gs))
    return ys.transpose(1, 0, 3, 2, 4).reshape(B, S, H, dv)


def memory_cross_attention(q, km, vm):
    dh = q.shape[-1]
    s = jnp.einsum('bshd,bmhd->bhsm', q, km).astype(jnp.float32) * (dh ** -0.5)
    p = jax.nn.softmax(s, axis=-1).astype(vm.dtype)
    return jnp.einsum('bhsm,bmhd->bshd', p, vm)


def setup_inputs(seed: int = 0) -> dict:
    key = jax.random.key(seed)
    ks = jax.random.split(key, 14)
    f32 = jnp.float32
    x = jax.random.normal(ks[0], (BATCH, SEQ, D_MODEL), f32)
    mem = jax.random.normal(ks[1], (BATCH, MEM_LEN, D_MODEL), f32)
    norm_w = 1.0 + 0.01 * jax.random.normal(ks[2], (DEPTH, D_MODEL), f32)
    w_in = jax.random.normal(ks[3], (DEPTH, D_MODEL, IN_COLS), f32) * D_MODEL ** -0.5
    w_alpha2 = jax.random.normal(ks[4], (DEPTH, GLA_RANK, B_KEY_WIDTH), f32) * GLA_RANK ** -0.5
    b_alpha = 0.1 * jax.random.normal(ks[5], (DEPTH, B_KEY_WIDTH), f32)
    gla_norm_w = 1.0 + 0.01 * jax.random.normal(ks[6], (DEPTH, B_VAL_DIM), f32)
    mem_norm_w = 1.0 + 0.01 * jax.random.normal(ks[7], (DEPTH, D_MODEL), f32)
    w_mem_kv = jax.random.normal(ks[8], (DEPTH, D_MODEL, 2 * C_WIDTH), f32) * D_MODEL ** -0.5
    w_out = jax.random.normal(ks[9], (DEPTH, MIX_WIDTH, D_MODEL), f32) * MIX_WIDTH ** -0.5
    rel_bias = 0.5 * jax.random.normal(ks[10], (REL_BUCKETS, A_HEADS), f32)
    final_norm_w = 1.0 + 0.01 * jax.random.normal(ks[11], (D_MODEL,), f32)
    return {"x": x, "mem": mem, "norm_w": norm_w, "w_in": w_in, "w_alpha2": w_alpha2,
            "b_alpha": b_alpha, "gla_norm_w": gla_norm_w, "mem_norm_w": mem_norm_w,
            "w_mem_kv": w_mem_kv, "w_out": w_out, "rel_bias": rel_bias,
            "final_norm_w": final_norm_w}


def reference(x, mem, norm_w, w_in, w_alpha2, b_alpha, gla_norm_w, mem_norm_w,
              w_mem_kv, w_out, rel_bias, final_norm_w):
    B, S, _ = x.shape
    sizes = (A_WIDTH, A_WIDTH, A_WIDTH, A_WIDTH,
             B_KEY_WIDTH, B_KEY_WIDTH, B_WIDTH, B_WIDTH, GLA_RANK,
             C_WIDTH, C_WIDTH)
    points = [int(p) for p in np.cumsum(sizes)[:-1]]
    h = x
    for l in range(DEPTH):
        u = rmsnorm(h, norm_w[l])
        proj = u @ w_in[l]
        qa, ka, va, ga, qb, kb, vb, gb, zb, qc, gc = jnp.split(proj, points, axis=-1)
        oa = moba_attention(qa.reshape(B, S, A_HEADS, A_HEAD_DIM),
                            ka.reshape(B, S, A_HEADS, A_HEAD_DIM),
                            va.reshape(B, S, A_HEADS, A_HEAD_DIM),
                            rel_bias).reshape(B, S, A_WIDTH)
        g_log = jax.nn.log_sigmoid((zb @ w_alpha2[l] + b_alpha[l]).astype(jnp.float32)) / GLA_TAU
        ob = gla_attention(qb.reshape(B, S, B_HEADS, B_KEY_DIM),
                           kb.reshape(B, S, B_HEADS, B_KEY_DIM),
                           vb.reshape(B, S, B_HEADS, B_VAL_DIM),
                           g_log.reshape(B, S, B_HEADS, B_KEY_DIM))
        ob = rmsnorm(ob, gla_norm_w[l]).reshape(B, S, B_WIDTH).astype(x.dtype)
        mkv = rmsnorm(mem, mem_norm_w[l]) @ w_mem_kv[l]
        km, vm = jnp.split(mkv, 2, axis=-1)
        M = mem.shape[1]
        oc = memory_cross_attention(qc.reshape(B, S, C_HEADS, C_HEAD_DIM),
                                    km.reshape(B, M, C_HEADS, C_HEAD_DIM),
                                    vm.reshape(B, M, C_HEADS, C_HEAD_DIM)).reshape(B, S, C_WIDTH)
        mixed = jnp.concatenate([oa * jax.nn.silu(ga), ob * jax.nn.silu(gb),
                                 oc * jax.nn.silu(gc)], axis=-1)
        h = h + mixed @ w_out[l]
    return rmsnorm(h, final_norm_w)
```

```python
import math
import os
import contextlib
import numpy as np
import concourse.bass as bass
import concourse.mybir as mybir
from concourse.bass_utils import run_bass_kernel_spmd

F32 = mybir.dt.float32
BF16 = mybir.dt.bfloat16
ALU = mybir.AluOpType
AF = mybir.ActivationFunctionType
AX = mybir.AxisListType

COMPUTE = ("pe", "act", "dve", "pool")
N_DMA_SEMS = int(os.environ.get("DBG_NDS", "12"))
NEGM = -30000.0
NOP_AFTER_WAIT = int(os.environ.get('DBG_NAW', '1'))
NAW_SKIP = tuple(os.environ.get('DBG_NAWSKIP', 'pe,act,pool,sp,dve').split(','))
EPS = 1e-6


class _Op:
    __slots__ = ("eng", "fn", "deps", "idx", "signal", "count", "dsem", "is_dma", "epoch", "where", "waits", "snap")


class Sched:
    def __init__(self, nc):
        self.nc = nc
        self.ops = {e: [] for e in COMPUTE + ("sp",)}
        self.last_write = {}
        self.readers = {}
        self.dma_rr = 0
        self.dma_last = [None] * N_DMA_SEMS
        self.dma_count = [0] * N_DMA_SEMS
        self.all_ops = []
        self.pending = {}

    def barrier(self):
        lasts = [self.ops[e][-1] for e in self.ops if self.ops[e]]
        lasts += [d for d in self.dma_last if d is not None]
        for e in self.ops:
            self.pending[e] = list(lasts)

    def add(self, eng, fn, reads=(), writes=(), dma=False):
        lim = int(os.environ.get("DBG_MAXOPS", "0"))
        if lim and not getattr(self, "nolimit", False) and len(self.all_ops) >= lim:
            return None
        op = _Op()
        op.eng, op.fn, op.is_dma = eng, fn, dma
        op.signal, op.count, op.dsem = False, None, None
        deps = []
        if self.pending.get(eng):
            deps.extend(self.pending[eng])
            self.pending[eng] = None
        for r in reads:
            w = self.last_write.get(r)
            if w is not None:
                deps.append(w)
        for w_ in writes:
            w = self.last_write.get(w_)
            if w is not None:
                deps.append(w)
            deps.extend(self.readers.get(w_, ()))
        if dma:
            k = self.dma_rr
            self.dma_rr = (self.dma_rr + 1) % N_DMA_SEMS
            op.dsem = k
            if self.dma_last[k] is not None:
                deps.append(self.dma_last[k])
            self.dma_last[k] = op
            self.dma_count[k] += 1
            op.count = 16 * self.dma_count[k]
            op.signal = True
        op.deps = [d for d in deps if d is not op]
        op.idx = len(self.ops[eng])
        if os.environ.get("DBG_DUMP"):
            import sys as _sys
            f = _sys._getframe(1)
            while f.f_code.co_name in ("A", "dma", "add"):
                f = f.f_back
            op.where = "%s:%d" % (f.f_code.co_name, f.f_lineno)
        self.ops[eng].append(op)
        self.all_ops.append(op)
        for r in reads:
            self.readers.setdefault(r, []).append(op)
        for w_ in writes:
            self.last_write[w_] = op
            self.readers[w_] = []
        return op

    @staticmethod
    def _skip(d, op):
        if d.is_dma or op.is_dma or d.eng != op.eng:
            return False
        if d.eng == "pe":
            return True
        return d.idx < op.idx - 3

    def emit(self, final_waits=()):
        nc = self.nc
        for op in self.all_ops:
            for d in op.deps:
                if not d.is_dma and not self._skip(d, op):
                    d.signal = True
        for op in final_waits:
            op.signal = True
        EPOCH = 2000
        nep = {}
        for e in self.ops:
            c = 0
            for op in self.ops[e]:
                if not op.is_dma and op.signal:
                    op.epoch = c // EPOCH
                    op.count = c % EPOCH + 1
                    c += 1
            nep[e] = c // EPOCH + 1
        know = {e: {} for e in self.ops}
        for op in self.all_ops:
            K = know[op.eng]
            cand = {}
            for d in op.deps:
                if d.is_dma:
                    key = ("d", d.dsem)
                else:
                    if self._skip(d, op):
                        continue
                    key = ("c", d.eng, d.epoch)
                if key not in cand or d.count > cand[key].count:
                    cand[key] = d
            order = sorted(cand.items(), key=lambda kv: -len(kv[1].snap))
            op.waits = []
            for key, d in order:
                if K.get(key, 0) >= d.count:
                    continue
                op.waits.append((key, d.count))
                K[key] = d.count
                for k2, v2 in d.snap.items():
                    if v2 > K.get(k2, 0):
                        K[k2] = v2
            op.snap = dict(K)
        with contextlib.ExitStack() as st:
            dsems = [st.enter_context(nc.semaphore("d_%d" % i)) for i in range(N_DMA_SEMS)]
            sems = {(e, k): st.enter_context(nc.semaphore("s_%s_%d" % (e, k))) for e in self.ops for k in range(nep[e])}
            block = st.enter_context(nc.Block())

            def run(e, handle):
                seen = {}
                for _ in range(int(os.environ.get("DBG_NOP_" + e.upper(), "0"))):
                    handle.engine_nop()
                for op in self.ops[e]:
                    wl = op.waits
                    for key, val in wl:
                        handle.wait_ge(dsems[key[1]] if key[0] == "d" else sems[(key[1], key[2])], val)
                    if wl and NOP_AFTER_WAIT and e not in NAW_SKIP:
                        handle.nop()
                    if os.environ.get("DBG_DUMP"):
                        with open(os.environ["DBG_DUMP"], "a") as fh:
                            fh.write("%s %d %s waits=%s sig=%s\n" % (e, op.idx, op.where, wl,
                                     (("d", op.dsem, op.count) if op.is_dma else ((e, op.epoch, op.count) if op.signal else None))))
                    inst = op.fn(handle)
                    if op.is_dma:
                        inst.then_inc(dsems[op.dsem], 16)
                    elif op.signal:
                        inst.then_inc(sems[(e, op.epoch)], 1)
                if e == "sp":
                    for op in final_waits:
                        handle.wait_ge(dsems[op.dsem] if op.is_dma else sems[(op.eng, op.epoch)], op.count)

            block.sync(lambda h: run("sp", h))
            block.tensor(lambda h: run("pe", h))
            block.scalar(lambda h: run("act", h))
            block.vector(lambda h: run("dve", h))
            block.gpsimd(lambda h: run("pool", h))


QA, KA, VA, GA, QB, KB, VB, GB, ZB, QC, GC = 0, 512, 1024, 1536, 2048, 2304, 2560, 3072, 3584, 3600, 4112
T_OWN = 2048
T_ALL = 4096
SCR_BYTES = 73 * 1024


def build_program(phases=(1, 4, 2, 3, 5), mode=None):
    nc = bass.Bass("TRN2", target_bir_lowering=False)

    def din(name, shape):
        return nc.dram_tensor(name, list(shape), F32, kind="ExternalInput").ap()

    xall = din("xall", [T_ALL, 1024])
    memb = din("memb", [256, 1024])
    w_in = din("w_in", [1024, 4624])
    w_mem = din("w_mem", [1024, 1024])
    w_out = din("w_out", [1536, 1024])
    nwc_d = din("nwc", [128, 8])
    mnwc_d = din("mnwc", [128, 8])
    fnw_d = din("fnw", [128, 1024])
    balc_d = din("balc", [128, 2])
    wa2_d = din("wa2", [16, 256])
    gnwc_d = din("gnwc", [128, 1])
    biasDP_d = din("biasDP", [8, 128, 1024])
    c31_d = din("c31", [128, 8])
    ident_d = din("ident", [128, 128])
    diagM_d = din("diagM", [128, 512])
    gmask_d = din("gmask", [128, 256])
    dforce_d = din("dforce", [128, 256])
    onehot_d = din("onehot", [16, T_ALL])
    tri8_d = din("tri8", [128, 1024])
    rmask_d = din("rmask", [128, 512])
    out_d = nc.dram_tensor("out", [T_OWN, 1024], F32, kind="ExternalOutput").ap() if mode != "A" else None
    mixo_d = nc.dram_tensor("mixo", [128, 8 * T_OWN], F32, kind="ExternalOutput").ap() if mode == "A" else None
    mixi_d = din("mixi", [128, 8 * T_OWN]) if mode == "B" else None

    with contextlib.ExitStack() as st:
        def sb(name, shape, dt):
            return st.enter_context(nc.sbuf_tensor(name, list(shape), dt))

        def ps(name, shape, dt):
            return st.enter_context(nc.psum_tensor(name, list(shape), dt))

        uT = sb("uT", [128, 8, T_ALL], BF16)
        mixT = sb("mixT", [128, 12, T_OWN], BF16)
        wst = sb("wst", [128, 8, 256], F32)
        wb = [sb("wb0", [128, 8, 256], BF16), sb("wb1", [128, 8, 256], BF16)]
        identb = sb("identb", [128, 128], BF16)
        onesb = sb("onesb", [128, 128], BF16)
        onesf = sb("onesf", [128, 128], F32)
        nwc = sb("nwc_s", [128, 8], F32)
        mnwc = sb("mnwc_s", [128, 8], F32)
        c31 = sb("c31_s", [128, 8], F32)
        balc = sb("balc_s", [128, 2], F32)
        nbal = sb("nbal_s", [128, 2], F32)
        wa2 = sb("wa2_s", [16, 256], F32)
        gnwc = sb("gnwc_s", [128, 1], F32)
        epsc = sb("epsc", [128, 1], F32)
        kmTx = sb("kmTx", [128, 4, 256], BF16)
        vmx = sb("vmx", [128, 2, 512], BF16)
        stat = sb("stat", [128, 3, 40], F32)
        scr = sb("scr", [128, SCR_BYTES // 4], F32)

        psAB = ps("psAB", [128, 1024], F32)
        psA = psAB[:, 0:512]
        psB = psAB[:, 512:1024]
        psS = [ps("psS0", [128, 1024], F32), ps("psS1", [128, 1024], F32)]
        psO = ps("psO", [128, 512], F32)
        psT = ps("psT", [128, 1024], BF16)
        psO_bf = psO[:].bitcast(BF16)

        S = Sched(nc)
        scr_off = [0]

        def scr_reset():
            S.barrier()
            scr_off[0] = 0

        def carve(shape, dt):
            n = 1
            for s_ in shape[1:]:
                n *= s_
            nbytes = n * (2 if dt == BF16 else 4)
            nbytes = (nbytes + 63) // 64 * 64
            o = scr_off[0]
            assert o + nbytes <= SCR_BYTES, ("scratch overflow", o, nbytes)
            scr_off[0] = o + nbytes
            v = scr[:, o // 4:(o + nbytes) // 4]
            if dt == BF16:
                v = v.bitcast(BF16)
            v = v[:, 0:n]
            if len(shape) == 3:
                v = v.rearrange("p (a b) -> p a b", a=shape[1])
            elif len(shape) == 4:
                v = v.rearrange("p (a b c) -> p a b c", a=shape[1], b=shape[2])
            if shape[0] < 128:
                v = v[0:shape[0]]
            return v

        def dma(out, in_, reads=(), writes=(), eng="sp"):
            return S.add(eng, lambda e, o=out, i=in_: e.dma_start(out=o, in_=i), reads=reads, writes=writes, dma=True)

        def A(eng, fn, reads=(), writes=()):
            return S.add(eng, fn, reads=reads, writes=writes)

        identf = scr[:, 0:128]
        dma(identf, ident_d, writes=["identf"])
        A("dve", lambda e: e.tensor_copy(out=identb[:], in_=identf), reads=["identf"], writes=["identb"])
        A("dve", lambda e: e.memset(onesb[:], 1.0), writes=["onesb"])
        A("dve", lambda e: e.memset(onesf[:], 1.0), writes=["onesf"])
        A("dve", lambda e: e.memset(epsc[:], EPS), writes=["epsc"])
        dma(nwc[:], nwc_d, writes=["nwc"])
        dma(mnwc[:], mnwc_d, writes=["mnwc"])
        dma(c31[:], c31_d, writes=["c31"])
        dma(balc[:], balc_d, writes=["balc"])
        dma(wa2[:], wa2_d, writes=["wa2"])
        dma(gnwc[:], gnwc_d, writes=["gnwc"])
        A("dve", lambda e: e.tensor_scalar(out=nbal[:], in0=balc[:], scalar1=-1.0, scalar2=None, op0=ALU.mult),
          reads=["balc"], writes=["nbal"])

        wcount = [0]

        def load_w(dram, pieces, scale, kc0=0, nkc=8, dst=None, dkey=None):
            if dst is None:
                k = wcount[0] % 2
                wcount[0] += 1
                dst, dkey = wb[k], "wb%d" % k
            off = 0
            for (c0, W) in pieces:
                src = dram[kc0 * 128:(kc0 + nkc) * 128, c0:c0 + W].rearrange("(kc p) c -> p kc c", p=128)
                dma(wst[:, 0:nkc, off:off + W], src, writes=["wst"])
                off += W
            for kc in range(nkc):
                if scale is not None:
                    A("dve", lambda e, kc=kc, off=off: e.tensor_scalar(
                        out=dst[:, kc, 0:off], in0=wst[:, kc, 0:off], scalar1=scale[:, kc0 + kc:kc0 + kc + 1],
                        scalar2=None, op0=ALU.mult), reads=["wst", "nwc", "mnwc"], writes=[dkey])
                else:
                    A("dve", lambda e, kc=kc, off=off: e.tensor_copy(out=dst[:, kc, 0:off], in_=wst[:, kc, 0:off]),
                      reads=["wst"], writes=[dkey])
            return dst, dkey

        ntile = [0]

        pend_B = []

        def norm_transpose(src2, dstTs, dkey, xs, ub, junk):
            ip = ntile[0] // 2
            bx = ip % len(xs)
            dma(xs[bx], src2.rearrange("(t p) d -> p t d", p=128), writes=[("xs", bx)], eng=("sp" if ip % 2 == 0 else "pool"))
            for t in range(2):
                partB = norm_partA(xs[bx][:, t, :], ("xs", bx), dstTs[t], dkey, ub, junk)
                if len(pend_B) >= 2:
                    pend_B.pop(0)()
                pend_B.append(partB)

        def norm_flush():
            while pend_B:
                pend_B.pop(0)()

        def norm_partA(xt, xkey, dstT, dkey, ub, junk):
            i = ntile[0]
            ntile[0] += 1
            b2 = i % len(ub)
            col = i % 40
            pT, pTk = (psT, "psT") if i % 2 == 0 else (psO_bf, "psO")
            A("act", lambda e: e.activation(out=junk, in_=xt, func=AF.Square, accum_out=stat[:, 0, col:col + 1]),
              reads=[xkey], writes=["junk", ("ss", col)])
            A("act", lambda e: e.activation(out=stat[:, 1, col:col + 1], in_=stat[:, 0, col:col + 1], func=AF.Sqrt,
                                            scale=1.0 / 1024, bias=epsc[:]), reads=[("ss", col), "epsc"], writes=[("sd", col)])
            A("dve", lambda e: e.reciprocal(out=stat[:, 2, col:col + 1], in_=stat[:, 1, col:col + 1]),
              reads=[("sd", col)], writes=[("rs", col)])
            A("dve", lambda e: e.tensor_scalar(out=ub[b2], in0=xt, scalar1=stat[:, 2, col:col + 1], scalar2=None,
                                               op0=ALU.mult), reads=[xkey, ("rs", col)], writes=[("ub", b2)])

            def partB():
                for kc in range(8):
                    A("pe", lambda e, kc=kc: e.transpose(out=pT[:, kc * 128:(kc + 1) * 128], in_=ub[b2][:, kc * 128:(kc + 1) * 128],
                                                         identity=identb[:]), reads=[("ub", b2), "identb"], writes=[pTk])
                src = pT[:, 0:1024].rearrange("p (a b) -> p a b", a=8)
                if i % 2 == 0:
                    A("act", lambda e: e.activation(out=dstT, in_=src, func=AF.Copy), reads=[pTk], writes=[dkey])
                else:
                    A("dve", lambda e: e.tensor_copy(out=dstT, in_=src), reads=[pTk], writes=[dkey])
            return partB

        def proj_F(wt, wkey, coff, M, srcT, skey, t0, N, pst, pkey):
            for kc in range(8):
                A("pe", lambda e, kc=kc: e.matmul(pst, lhsT=wt[:, kc, coff:coff + M], rhs=srcT[:, kc, t0:t0 + N],
                                                  start=(kc == 0), stop=(kc == 7)), reads=[wkey, skey], writes=[pkey])

        def proj_T(wt, wkey, coff, NC, srcT, skey, t0, pst, pkey):
            for kc in range(8):
                A("pe", lambda e, kc=kc: e.matmul(pst, lhsT=srcT[:, kc, t0:t0 + 128], rhs=wt[:, kc, coff:coff + NC],
                                                  start=(kc == 0), stop=(kc == 7)), reads=[wkey, skey], writes=[pkey])

        scr_reset()
        _ = carve([128, 128], F32)
        xs = [carve([128, 2, 1024], F32) for _ in range(4)]
        ub = [carve([128, 1024], BF16) for _ in range(3)]
        junk = carve([128, 1024], BF16)
        mT = carve([128, 8, 256], BF16)
        norm_transpose(memb[0:256, :], [mT[:, :, 0:128], mT[:, :, 128:256]], "mT", xs, ub, junk)
        norm_flush()
        for half in range(2):
            for cc in range(2):
                wt, wk = load_w(w_mem, [(half * 512 + cc * 256, 256)], mnwc)
                if half == 0:
                    for c2 in range(2):
                        c = cc * 2 + c2
                        proj_F(wt, wk, c2 * 128, 128, mT, "mT", 0, 256, psA[:, 0:256], "psA")
                        A("dve", lambda e, c=c: e.tensor_copy(out=kmTx[:, c, :], in_=psA[:, 0:256]), reads=["psA"], writes=["kmTx"])
                else:
                    for mt in range(2):
                        proj_T(wt, wk, 0, 256, mT, "mT", mt * 128, psB[:, 0:256], "psB")
                        A("dve", lambda e, mt=mt, cc=cc: e.tensor_copy(out=vmx[:, mt, cc * 256:(cc + 1) * 256], in_=psB[:, 0:256]),
                          reads=["psB"], writes=["vmx"])
        for i in range(16):
            norm_transpose(xall[i * 256:(i + 1) * 256, :], [uT[:, :, i * 256:i * 256 + 128], uT[:, :, i * 256 + 128:(i + 1) * 256]],
                           "uT", xs, ub, junk)
        norm_flush()

        def phase2():
            scr_reset()
            qcT = carve([128, T_OWN], BF16)
            gcT = carve([128, T_OWN], BF16)
            ptx = [carve([128, 2, 512], BF16) for _ in range(2)]
            recx = carve([128, 512], F32)
            t1x = carve([128, 512], F32)
            XS = 1.0 / math.sqrt(128.0)
            for c in range(int(os.environ.get('DBG_NXH', '4'))):
                wt, wk = load_w(w_in, [(QC + c * 128, 128), (GC + c * 128, 128)], nwc)
                for g in range(int(os.environ.get('DBG_NXPG', '4'))):
                    if 'q' in os.environ.get('DBG_XPARTS', 'qg'):
                        proj_F(wt, wk, 0, 128, uT, "uT", T_OWN + g * 512, 512, psA[:], "psA")
                        A("dve", lambda e, g=g: e.tensor_copy(out=qcT[:, g * 512:(g + 1) * 512], in_=psA[:]), reads=["psA"], writes=[("qcT", g)])
                    if 'g' not in os.environ.get('DBG_XPARTS', 'qg'):
                        continue
                    proj_F(wt, wk, 128, 128, uT, "uT", T_OWN + g * 512, 512, psB[:], "psB")
                    A("act", lambda e, g=g: e.activation(out=gcT[:, g * 512:(g + 1) * 512], in_=psB[:], func=(AF.Copy if os.environ.get("DBG_NOSILU") else AF.Silu)),
                      reads=["psB"], writes=[("gcT", g)])
                for g in range(int(os.environ.get('DBG_NXG', '4'))):
                    pS = psS[g % 2]
                    pk = "psS%d" % (g % 2)
                    pt = ptx[g % 2]
                    ptk = ("ptx", g % 2)
                    for mt in range(2):
                        A("pe", lambda e, mt=mt, pS=pS, g=g, c=c: e.matmul(pS[:, mt * 512:(mt + 1) * 512], lhsT=kmTx[:, c, mt * 128:(mt + 1) * 128],
                                                                          rhs=qcT[:, g * 512:(g + 1) * 512], start=True, stop=True),
                          reads=["kmTx", ("qcT", g)], writes=[pk])
                    A("act", lambda e, pS=pS, pt=pt: e.activation(out=pt.rearrange("p a b -> p (a b)"), in_=pS[:], func=AF.Exp, scale=XS),
                      reads=[pk], writes=[ptk])
                    for mt in range(2):
                        A("pe", lambda e, mt=mt, pt=pt, c=c: e.matmul(psO[:], lhsT=vmx[:, mt, c * 128:(c + 1) * 128], rhs=pt[:, mt, :],
                                                                     start=(mt == 0), stop=(mt == 1)), reads=["vmx", ptk], writes=["psO"])
                    for mt in range(2):
                        A("pe", lambda e, mt=mt, pt=pt: e.matmul(psB[:], lhsT=onesb[:], rhs=pt[:, mt, :], start=(mt == 0), stop=(mt == 1)),
                          reads=["onesb", ptk], writes=["psB"])
                    A("dve", lambda e: e.reciprocal(out=recx, in_=psB[:]), reads=["psB"], writes=["recx"])
                    A("dve", lambda e: e.tensor_tensor(out=t1x, in0=psO[:], in1=recx, op=ALU.mult), reads=["psO", "recx"], writes=["t1x"])
                    A("pool", lambda e, g=g, c=c: e.tensor_tensor(out=mixT[:, 8 + c, g * 512:(g + 1) * 512], in0=t1x,
                                                                 in1=gcT[:, g * 512:(g + 1) * 512], op=ALU.mult),
                      reads=["t1x", ("gcT", g)], writes=[("mixT", 8 + c)])

        def phase3():
            scr_reset()
            tri8 = carve([128, 1024], BF16)
            rmask = carve([128, 512], F32)
            tri8f = carve([128, 1024], F32)
            dma(tri8f, tri8_d, writes=["tri8f"])
            A("dve", lambda e: e.tensor_copy(out=tri8, in_=tri8f), reads=["tri8f"], writes=["tri8"])
            dma(rmask, rmask_d, writes=["rmask"])
            scr_off[0] -= 4096
            S.barrier()
            bufs = []
            for pr in range(2):
                b = {}
                b["wg"] = carve([128, 8, 784], BF16)
                b["zt"] = carve([16, 512], F32)
                b["tmpA"] = carve([128, 512], F32)
                b["Bg"] = carve([128, 512], F32)
                b["tmpE"] = carve([128, 512], F32)
                b["sdg"] = carve([128, 512], F32)
                b["ktg"] = carve([128, 512], BF16)
                b["khg"] = carve([128, 512], BF16)
                b["qtg"] = carve([128, 512], BF16)
                b["dg"] = carve([128, 4], F32)
                b["khtok"] = carve([128, 4, 128], BF16)
                b["vbg"] = carve([128, 4, 256], BF16)
                b["Sbg"] = carve([128, 4, 128], BF16)
                b["Sst"] = carve([128, 128], F32)
                b["attm"] = carve([128, 2, 4, 128], BF16)
                b["gbg"] = carve([128, 512], BF16)
                bufs.append(b)
            for pr in range(2):
                wg = bufs[pr]["wg"]
                wk_ = ("wg", pr)
                load_w(w_in, [(QB + pr * 128, 128), (KB + pr * 128, 128)], nwc, dst=wg[:, :, 0:256], dkey=wk_)
                load_w(w_in, [(VB + pr * 256, 256)], nwc, dst=wg[:, :, 256:512], dkey=wk_)
                load_w(w_in, [(GB + pr * 256, 256)], nwc, dst=wg[:, :, 512:768], dkey=wk_)
                load_w(w_in, [(ZB, 16)], nwc, dst=wg[:, :, 768:784], dkey=wk_)
                A("dve", lambda e, pr=pr: e.memset(bufs[pr]["Sst"], 0.0), writes=[("Sst", pr)])

            def group_body(pr, g):
                b = bufs[pr]
                wg, zt, tmpA, Bg, tmpE, sdg = b["wg"], b["zt"], b["tmpA"], b["Bg"], b["tmpE"], b["sdg"]
                ktg, khg, qtg, dg, khtok, vbg, Sbg, Sst, attm, gbg = (b["ktg"], b["khg"], b["qtg"], b["dg"], b["khtok"], b["vbg"],
                                                                     b["Sbg"], b["Sst"], b["attm"], b["gbg"])

                def k(name):
                    return (name, pr)
                wk_ = k("wg")
                own = g >= 4
                t0 = g * 512
                zt = bufs[g % 2]["zt"]
                zkey = ("ztS", g % 2)
                if pr == 0:
                    proj_F(wg, wk_, 768, 16, uT, "uT", t0, 512, psA[0:16, :], "psA")
                    A("act", lambda e: e.activation(out=zt, in_=psA[0:16, :], func=AF.Copy), reads=["psA"], writes=[zkey])
                yield
                A("pe", lambda e: e.matmul(psB[:], lhsT=wa2[0:16, pr * 128:(pr + 1) * 128], rhs=zt, start=True, stop=True),
                  reads=["wa2", zkey], writes=["psB"])
                A("act", lambda e: e.activation(out=tmpA, in_=psB[:], func=AF.Exp, scale=-1.0, bias=nbal[:, pr:pr + 1]),
                  reads=["psB", "nbal"], writes=[k("tmpA")])
                yield
                A("act", lambda e: e.activation(out=tmpA, in_=tmpA, func=AF.Ln, bias=1.0), reads=[k("tmpA")], writes=[k("tmpA")])
                A("dve", lambda e: e.tensor_tensor_scan(out=Bg, data0=rmask, data1=tmpA, initial=0.0, op0=ALU.mult, op1=ALU.add),
                  reads=[k("tmpA"), "rmask"], writes=[k("Bg")])
                yield
                proj_F(wg, wk_, 128, 128, uT, "uT", t0, 512, psA[:], "psA")
                A("act", lambda e: e.activation(out=tmpE, in_=Bg, func=AF.Exp, scale=1.0 / 16), reads=[k("Bg")], writes=[k("tmpE")])
                A("dve", lambda e: e.tensor_tensor(out=ktg, in0=psA[:], in1=tmpE, op=ALU.mult), reads=["psA", k("tmpE")], writes=[k("ktg")])
                A("act", lambda e: e.activation(out=dg, in_=Bg.rearrange("p (a b) -> p a b", a=4)[:, :, 127], func=AF.Exp, scale=-1.0 / 16),
                  reads=[k("Bg")], writes=[k("dg")])
                yield
                for ch in range(4):
                    A("dve", lambda e, ch=ch: e.tensor_scalar(out=khg[:, ch * 128:(ch + 1) * 128], in0=ktg[:, ch * 128:(ch + 1) * 128],
                                                              scalar1=dg[:, ch:ch + 1], scalar2=None, op0=ALU.mult),
                      reads=[k("ktg"), k("dg")], writes=[k("khg")])
                for ch in range(4):
                    A("pe", lambda e, ch=ch: e.transpose(out=psT[:, ch * 128:(ch + 1) * 128], in_=khg[:, ch * 128:(ch + 1) * 128],
                                                         identity=identb[:]), reads=[k("khg"), "identb"], writes=["psT"])
                A("dve", lambda e: e.tensor_copy(out=khtok, in_=psT[:, 0:512].rearrange("p (a b) -> p a b", a=4)),
                  reads=["psT"], writes=[k("khtok")])
                yield
                for ch in range(4):
                    proj_T(wg, wk_, 256, 256, uT, "uT", t0 + ch * 128, psS[0][:, ch * 256:(ch + 1) * 256], "psS0")
                A("act", lambda e: e.activation(out=vbg.rearrange("p a b -> p (a b)"), in_=psS[0][:], func=AF.Copy),
                  reads=["psS0"], writes=[k("vbg")])
                yield
                for ch in range(4):
                    A("pe", lambda e, ch=ch: e.matmul(psS[1][:, ch * 256:(ch + 1) * 256], lhsT=khtok[:, ch, :], rhs=vbg[:, ch, :],
                                                      start=True, stop=True), reads=[k("khtok"), k("vbg")], writes=["psS1"])
                for ch in range(4):
                    if own:
                        A("pool", lambda e, ch=ch: e.tensor_copy(out=Sbg[:, ch, :], in_=Sst), reads=[k("Sst")], writes=[k("Sbg")])
                    for hh in range(2):
                        r0 = hh * 64
                        A("dve", lambda e, ch=ch, hh=hh, r0=r0: e.scalar_tensor_tensor(
                            out=Sst[r0:r0 + 64, :], in0=Sst[r0:r0 + 64, :], scalar=dg[r0:r0 + 64, ch:ch + 1],
                            in1=psS[1][r0:r0 + 64, ch * 256 + hh * 128:ch * 256 + (hh + 1) * 128], op0=ALU.mult, op1=ALU.add),
                          reads=[k("Sst"), k("dg"), "psS1"], writes=[k("Sst")])
                yield
                if not own:
                    return
                proj_F(wg, wk_, 0, 128, uT, "uT", t0, 512, psA[:], "psA")
                A("act", lambda e: e.activation(out=tmpE, in_=Bg, func=AF.Exp, scale=-1.0 / 16), reads=[k("Bg")], writes=[k("tmpE")])
                A("dve", lambda e: e.scalar_tensor_tensor(out=qtg, in0=psA[:], scalar=0.125, in1=tmpE, op0=ALU.mult, op1=ALU.mult),
                  reads=["psA", k("tmpE")], writes=[k("qtg")])
                yield
                for hh in range(2):
                    r0 = hh * 64
                    for ch in range(4):
                        A("pe", lambda e, hh=hh, ch=ch, r0=r0: e.matmul(
                            psS[0][:, (hh * 4 + ch) * 128:(hh * 4 + ch + 1) * 128], lhsT=ktg[r0:r0 + 64, ch * 128:(ch + 1) * 128],
                            rhs=qtg[r0:r0 + 64, ch * 128:(ch + 1) * 128], start=True, stop=True),
                          reads=[k("ktg"), k("qtg")], writes=["psS0"])
                A("dve", lambda e: e.tensor_tensor(out=attm.rearrange("p a b c -> p (a b c)"), in0=psS[0][:], in1=tri8, op=ALU.mult),
                  reads=["psS0", "tri8"], writes=[k("attm")])
                yield
                for hh in range(2):
                    r0 = hh * 64
                    hd = pr * 2 + hh
                    for ch in range(4):
                        A("pe", lambda e, hh=hh, ch=ch, r0=r0: e.matmul(
                            psO[:, ch * 128:(ch + 1) * 128], lhsT=Sbg[r0:r0 + 64, ch, :], rhs=qtg[r0:r0 + 64, ch * 128:(ch + 1) * 128],
                            start=True, stop=False), reads=[k("Sbg"), k("qtg")], writes=["psO"])
                        A("pe", lambda e, hh=hh, ch=ch: e.matmul(
                            psO[:, ch * 128:(ch + 1) * 128], lhsT=vbg[:, ch, hh * 128:(hh + 1) * 128], rhs=attm[:, hh, ch, :],
                            start=False, stop=True), reads=[k("vbg"), k("attm")], writes=["psO"])
                    A("act", lambda e: e.activation(out=tmpA, in_=psO[:], func=AF.Square), reads=["psO"], writes=[k("tmpA")])
                    A("pe", lambda e: e.matmul(psB[:], lhsT=onesf[:], rhs=tmpA, start=True, stop=True), reads=["onesf", k("tmpA")], writes=["psB"])
                    A("act", lambda e: e.activation(out=sdg, in_=psB[:], func=AF.Sqrt, scale=1.0 / 128, bias=epsc[:]),
                      reads=["psB", "epsc"], writes=[k("sdg")])
                    A("dve", lambda e: e.reciprocal(out=sdg, in_=sdg), reads=[k("sdg")], writes=[k("sdg")])
                    A("dve", lambda e: e.scalar_tensor_tensor(out=tmpE, in0=psO[:], scalar=gnwc[:, 0:1], in1=sdg, op0=ALU.mult, op1=ALU.mult),
                      reads=["psO", "gnwc", k("sdg")], writes=[k("tmpE")])
                    proj_F(wg, wk_, 512 + hh * 128, 128, uT, "uT", t0, 512, psA[:], "psA")
                    A("act", lambda e: e.activation(out=gbg, in_=psA[:], func=AF.Silu), reads=["psA"], writes=[k("gbg")])
                    A("pool", lambda e, hd=hd: e.tensor_tensor(out=mixT[:, 4 + hd, (g - 4) * 512:(g - 3) * 512], in0=tmpE, in1=gbg, op=ALU.mult),
                      reads=[k("tmpE"), k("gbg")], writes=[("mixT", 4 + hd)])
                    yield

            for g in range(8):
                gens = [group_body(0, g), group_body(1, g)]
                while gens:
                    for gen in list(gens):
                        try:
                            next(gen)
                        except StopIteration:
                            gens.remove(gen)

        def phase4():
            scr_reset()
            Kaug = [carve([80, T_ALL], BF16) for _ in range(2)]
            Qaug = [carve([80, T_OWN], BF16) for _ in range(2)]
            V3 = carve([128, 32, 192], BF16)
            gaT = carve([128, T_OWN], BF16)
            ptm = [carve([128, 2, 512], BF16) for _ in range(3)]
            sbias = carve([128, 2, 512], F32)
            bDP = [carve([128, 4, 256], F32) for _ in range(2)]
            bD = [t_[:, 0:2, :] for t_ in bDP]
            bP = [t_[:, 2:4, :] for t_ in bDP]
            diagM = carve([128, 2, 256], F32)
            gmask = carve([128, 16, 16], F32)
            dforce = carve([128, 16, 16], F32)
            gm = carve([128, 16, 16], F32)
            m8 = carve([128, 16, 8], F32)
            thrc = carve([128, 16], F32)
            Fsel = carve([128, 16, 16], F32)
            FTin = carve([128, 16, 80], BF16)
            recm = carve([128, 512], F32)
            t1m = carve([128, 512], F32)
            ksum = carve([128, 16], F32)
            ksumB = carve([128, 16], F32)
            kmT = [carve([64, 16], BF16) for _ in range(2)]
            ohst = sbias.rearrange("p a b -> p (a b)")[0:80]
            dma(diagM.rearrange("p a b -> p (a b)"), diagM_d, writes=["diagM"])
            dma(gmask.rearrange("p a b -> p (a b)"), gmask_d, writes=["gmask"])
            dma(dforce.rearrange("p a b -> p (a b)"), dforce_d, writes=["dforce"])
            for q4 in range(4):
                dma(ohst[64:80, :], onehot_d[:, q4 * 1024:(q4 + 1) * 1024], writes=[("sbias", 0), ("sbias", 256)])
                for hh in range(2):
                    A("dve", lambda e, hh=hh, q4=q4: e.tensor_copy(out=Kaug[hh][64:80, q4 * 1024:(q4 + 1) * 1024], in_=ohst[64:80, :]),
                      reads=[("sbias", 0), ("sbias", 256)], writes=[("Koh", hh)])
            MS = os.environ.get("DBG_MSENG", "pool")
            A(MS, lambda e: e.memset(V3[:, :, 64:128], 1.0), writes=["V3ones"])
            A(MS, lambda e: e.memset(FTin.rearrange("p a b -> p (a b)"), 0.0), writes=["FTin"])
            for p in range(int(os.environ.get('DBG_NPAIR', '4'))):
                PARTS = os.environ.get('DBG_P4PARTS', 'kqvg')
                wt, wk = load_w(w_in, [(KA + p * 128, 128), (QA + p * 128, 128)], nwc)
                for g in range(8 if 'k' in PARTS else 0):
                    pst, pk = (psA, "psA") if g % 2 == 0 else (psB, "psB")
                    proj_F(wt, wk, 0, 128, uT, "uT", g * 512, 512, pst[:], pk)
                    for blk in range(2):
                        cs = slice(blk * 256, (blk + 1) * 256)
                        ts = slice(g * 512 + blk * 256, g * 512 + (blk + 1) * 256)
                        col = g * 2 + blk
                        A("act", lambda e, pst=pst, cs=cs, ts=ts, col=col: e.activation(
                            out=Kaug[0][0:64, ts], in_=pst[0:64, cs], func=AF.Copy, accum_out=ksum[0:64, col:col + 1]),
                          reads=[pk], writes=[("Kaug", 0), ("ksA", col)])
                        A("dve", lambda e, pst=pst, cs=cs, ts=ts, col=col: e.tensor_scalar(
                            out=Kaug[1][0:64, ts], in0=pst[64:128, cs], scalar1=1.0, scalar2=0.0, op0=ALU.mult, op1=ALU.add,
                            accum_out=ksumB[0:64, col:col + 1]), reads=[pk], writes=[("Kaug", 1), ("ksB", col)])
                A("dve", lambda e: e.tensor_scalar(out=kmT[0][:], in0=ksum[0:64, :], scalar1=1.0 / 256, scalar2=None, op0=ALU.mult),
                  reads=[("ksA", c_) for c_ in range(16)], writes=[("kmT", 0)])
                A("dve", lambda e: e.tensor_scalar(out=kmT[1][:], in0=ksumB[0:64, :], scalar1=1.0 / 256, scalar2=None, op0=ALU.mult),
                  reads=[("ksB", c_) for c_ in range(16)], writes=[("kmT", 1)])
                for g in range(4 if 'q' in PARTS else 0):
                    pst, pk = (psA, "psA") if g % 2 == 0 else (psB, "psB")
                    proj_F(wt, wk, 128, 128, uT, "uT", T_OWN + g * 512, 512, pst[:], pk)
                    A("act", lambda e, g=g, pst=pst: e.activation(out=Qaug[0][0:64, g * 512:(g + 1) * 512], in_=pst[0:64, :], func=AF.Copy, scale=0.125),
                      reads=[pk], writes=[("Qaug", 0)])
                    A("dve", lambda e, g=g, pst=pst: e.tensor_scalar(out=Qaug[1][0:64, g * 512:(g + 1) * 512], in0=pst[64:128, :], scalar1=0.125,
                                                                     scalar2=None, op0=ALU.mult), reads=[pk], writes=[("Qaug", 1)])
                wt, wk = load_w(w_in, [(VA + p * 128, 128), (GA + p * 128, 128)], nwc)
                def gen_vg():
                    for g4 in range(8 if 'v' in PARTS else 0):
                        pst, pk = (psA, "psA") if g4 % 2 == 0 else (psB, "psB")
                        for ti in range(4):
                            proj_T(wt, wk, 0, 128, uT, "uT", (g4 * 4 + ti) * 128, pst[:, ti * 128:(ti + 1) * 128], pk)
                        pv = pst[:].rearrange("p (a b) -> p a b", a=4)
                        A("act", lambda e, g4=g4, pv=pv: e.activation(out=V3[:, g4 * 4:(g4 + 1) * 4, 0:64], in_=pv[:, :, 0:64], func=AF.Copy),
                          reads=[pk], writes=["V3"])
                        A("dve", lambda e, g4=g4, pv=pv: e.tensor_copy(out=V3[:, g4 * 4:(g4 + 1) * 4, 128:192], in_=pv[:, :, 64:128]),
                          reads=[pk], writes=["V3"])
                        yield
                    for g in range(4 if 'g' in PARTS else 0):
                        pst, pk = (psA, "psA") if g % 2 == 0 else (psB, "psB")
                        proj_F(wt, wk, 128, 128, uT, "uT", T_OWN + g * 512, 512, pst[:], pk)
                        A("act", lambda e, g=g, pst=pst: e.activation(out=gaT[:, g * 512:(g + 1) * 512], in_=pst[:], func=AF.Silu),
                          reads=[pk], writes=["gaT"])
                        yield
                        yield
                def gen_gate():
                    for hh in range(int(os.environ.get('DBG_NHH', '2'))):
                        h = p * 2 + hh
                        dma(bDP[hh].rearrange("p a b -> p (a b)"), biasDP_d[h], writes=[("bD", hh), ("bP", hh)])
                        A("dve", lambda e, hh=hh, h=h: e.scalar_tensor_tensor(
                            out=bD[hh], in0=bD[hh], scalar=c31[:, h:h + 1], in1=diagM, op0=ALU.subtract, op1=ALU.add),
                          reads=[("bD", hh), "c31", "diagM"], writes=[("bD", hh)])
                        A("dve", lambda e, hh=hh, h=h: e.tensor_scalar(
                            out=bP[hh], in0=bP[hh], scalar1=c31[:, h:h + 1], scalar2=None, op0=ALU.subtract), reads=[("bP", hh), "c31"], writes=[("bP", hh)])
                        yield
                        gps = psS[0][:, 0:256].rearrange("p (a b) -> p a b", a=16)
                        for ti in range(16):
                            A("pe", lambda e, ti=ti, hh=hh: e.matmul(psS[0][:, ti * 16:(ti + 1) * 16], lhsT=Qaug[hh][0:64, ti * 128:(ti + 1) * 128],
                                                                     rhs=kmT[hh][:], start=True, stop=True),
                              reads=[("Qaug", hh), ("kmT", hh)], writes=["psS0"])
                        A("dve", lambda e: e.tensor_tensor(out=gm, in0=gps, in1=gmask, op=ALU.add), reads=["psS0", "gmask"], writes=["gm"])
                        for ti in range(16):
                            A("dve", lambda e, ti=ti: e.max(out=m8[:, ti, :], in_=gm[:, ti, :]), reads=["gm"], writes=["m8"])
                        A("dve", lambda e: e.tensor_scalar(out=thrc, in0=m8[:, :, 2], scalar1=-1e29, scalar2=None, op0=ALU.max),
                          reads=["m8"], writes=["thrc"])
                        yield
                        for ti in range(16):
                            A("dve", lambda e, ti=ti: e.tensor_scalar(out=Fsel[:, ti, :], in0=gm[:, ti, :], scalar1=thrc[:, ti:ti + 1],
                                                                      scalar2=None, op0=ALU.is_ge), reads=["gm", "thrc"], writes=["Fsel"])
                        A("dve", lambda e: e.tensor_scalar(out=Fsel, in0=Fsel, scalar1=-1.0, scalar2=-NEGM, op0=ALU.add, op1=ALU.mult),
                          reads=["Fsel"], writes=["Fsel"])
                        A("dve", lambda e: e.tensor_tensor(out=Fsel, in0=Fsel, in1=dforce, op=ALU.add), reads=["Fsel", "dforce"], writes=["Fsel"])
                        A("dve", lambda e, h=h: e.tensor_scalar(out=FTin[:, :, 64:80], in0=Fsel, scalar1=0.0, scalar2=c31[:, h:h + 1],
                                                                op0=ALU.min, op1=ALU.add), reads=["Fsel", "c31"], writes=["FTin"])
                        for half in range(2):
                            for t8 in range(8):
                                ti = half * 8 + t8
                                A("pe", lambda e, ti=ti, t8=t8: e.transpose(out=psT[0:80, t8 * 128:(t8 + 1) * 128], in_=FTin[:, ti, :],
                                                                            identity=identb[:]), reads=["FTin", "identb"], writes=["psT"])
                            A("dve", lambda e, half=half, hh=hh: e.tensor_copy(out=Qaug[hh][64:80, half * 1024:(half + 1) * 1024], in_=psT[64:80, :]),
                              reads=["psT"], writes=[("Qaug", hh)])
                            yield
                            yield
                gens_ = [gen_vg(), gen_gate()]
                while gens_:
                    for gen_ in list(gens_):
                        try:
                            next(gen_)
                        except StopIteration:
                            gens_.remove(gen_)
                for hh in range(int(os.environ.get('DBG_NHH', '2'))):
                    h = p * 2 + hh
                    orow = hh * 64
                    drow = 64 - hh * 64
                    slots_all = []
                    for qt in range(int(os.environ.get('DBG_NQT', '4'))):
                        l0 = 2 * qt
                        slots = list(range(8)) + [8 + m for m in range(l0 + 2)]
                        for si, n in enumerate(slots):
                            last = (n == 8 + l0 + 1)
                            c0 = 256 if last else 0
                            kind_l = kind_r = "far"
                            if n == 8 + l0 + 1:
                                kind_l, kind_r = "skip", "diag"
                            elif n == 8 + l0:
                                kind_l, kind_r = "diag", "prev"
                            elif n == 8 + l0 - 1:
                                kind_l = "prev"
                            gi = len(slots_all)
                            slots_all.append(dict(qt=qt, si=si, n=n, nslot=len(slots), c0=c0, NQ=512 - c0, q0=qt * 512 + c0,
                                                  kl=kind_l, kr=kind_r, b=gi % 3, bt=gi % 3))

                    psS3 = [psS[0], psS[1], psAB]
                    psK3 = [["psS0"], ["psS1"], ["psA", "psB"]]

                    def emit_S(sl):
                        pS, pk = psS3[sl["b"]], psK3[sl["b"]]
                        n, c0, q0, NQ = sl["n"], sl["c0"], sl["q0"], sl["NQ"]
                        for kt in range(2):
                            A("pe", lambda e, kt=kt, pS=pS, n=n, q0=q0, NQ=NQ, c0=c0, hh=hh: e.matmul(
                                pS[:, kt * 512 + c0:kt * 512 + 512], lhsT=Kaug[hh][0:80, n * 256 + kt * 128:n * 256 + (kt + 1) * 128],
                                rhs=Qaug[hh][0:80, q0:q0 + NQ], start=True, stop=True),
                              reads=[("Kaug", hh), ("Koh", hh), ("Qaug", hh)], writes=pk)

                    def emit_exp(sl):
                        pS, pk = psS3[sl["b"]], psK3[sl["b"]]
                        pt, ptk = ptm[sl["bt"]], ("ptm", sl["bt"])
                        pS3 = pS[:].rearrange("p (a b) -> p a b", a=2)
                        if sl["kl"] == "far" and sl["kr"] == "far":
                            A("act", lambda e, pS=pS, pt=pt: e.activation(out=pt.rearrange("p a b -> p (a b)"), in_=pS[:], func=AF.Exp),
                              reads=pk, writes=[ptk])
                            return
                        for (kind, cs) in ((sl["kl"], 0), (sl["kr"], 256)):
                            if kind == "skip":
                                continue
                            if kind == "far":
                                A("act", lambda e, pS3=pS3, pt=pt, cs=cs: e.activation(out=pt[:, :, cs:cs + 256], in_=pS3[:, :, cs:cs + 256],
                                                                                       func=AF.Exp), reads=pk, writes=[ptk])
                                continue
                            bt, bk = (bD[hh], ("bD", hh)) if kind == "diag" else (bP[hh], ("bP", hh))
                            A("dve", lambda e, pS3=pS3, cs=cs, bt=bt: e.tensor_tensor(out=sbias[:, :, cs:cs + 256], in0=pS3[:, :, cs:cs + 256],
                                                                                      in1=bt, op=ALU.add), reads=pk + [bk], writes=[("sbias", cs)])
                            A("act", lambda e, pt=pt, cs=cs: e.activation(out=pt[:, :, cs:cs + 256], in_=sbias[:, :, cs:cs + 256], func=AF.Exp),
                              reads=[("sbias", cs)], writes=[ptk])

                    def emit_PV(sl):
                        pt, ptk = ptm[sl["bt"]], ("ptm", sl["bt"])
                        n, c0, si, nslot, qt = sl["n"], sl["c0"], sl["si"], sl["nslot"], sl["qt"]
                        for kt in range(2):
                            A("pe", lambda e, kt=kt, pt=pt, n=n, c0=c0, si=si, nslot=nslot, hh=hh: e.matmul(
                                psO[:, c0:512], lhsT=V3[:, n * 2 + kt, hh * 64:hh * 64 + 128], rhs=pt[:, kt, c0:512],
                                start=(si == 0 and kt == 0), stop=(si == nslot - 1 and kt == 1)),
                              reads=["V3", "V3ones", ptk], writes=["psO"])
                        if si != nslot - 1:
                            return
                        A("dve", lambda e, orow=orow, drow=drow: e.reciprocal(out=recm[orow:orow + 64, :], in_=psO[drow:drow + 64, :]),
                          reads=["psO"], writes=["recm"])
                        A("dve", lambda e, orow=orow: e.tensor_tensor(out=t1m[orow:orow + 64, :], in0=psO[orow:orow + 64, :],
                                                                      in1=recm[orow:orow + 64, :], op=ALU.mult), reads=["psO", "recm"], writes=["t1m"])
                        A("pool", lambda e, orow=orow, qt=qt, p=p: e.tensor_tensor(
                            out=mixT[orow:orow + 64, p, qt * 512:(qt + 1) * 512], in0=t1m[orow:orow + 64, :],
                            in1=gaT[orow:orow + 64, qt * 512:(qt + 1) * 512], op=ALU.mult), reads=["t1m", "gaT"], writes=[("mixT", p)])

                    for sl in slots_all[0:3]:
                        emit_S(sl)
                    for gi, sl in enumerate(slots_all):
                        emit_exp(sl)
                        if gi + 3 < len(slots_all):
                            emit_S(slots_all[gi + 3])
                        emit_PV(sl)

        def phase5():
            scr_reset()
            wo = carve([128, 12, 1024], BF16)
            fnw = carve([128, 1024], F32)
            xs5 = [carve([128, 2, 1024], F32) for _ in range(2)]
            hres = carve([128, 1024], F32)
            junk5 = carve([128, 1024], BF16)
            ot = [carve([128, 2, 1024], F32) for _ in range(2)]
            dma(fnw, fnw_d, writes=["fnw"])
            for cc in range(4):
                load_w(w_out, [(cc * 256, 256)], None, kc0=0, nkc=8, dst=wo[:, 0:8, cc * 256:(cc + 1) * 256], dkey="wo")
                load_w(w_out, [(cc * 256, 256)], None, kc0=8, nkc=4, dst=wo[:, 8:12, cc * 256:(cc + 1) * 256], dkey="wo")
            mix_keys = [("mixT", k) for k in range(12)]
            finals = []
            for ip in range(8):
                bx = ip % 2
                dma(xs5[bx], xall[T_OWN + ip * 256:T_OWN + (ip + 1) * 256, :].rearrange("(t p) d -> p t d", p=128), writes=[("xs5", bx)])
                for t in range(2):
                    i = ip * 2 + t
                    for half in range(2):
                        pst, pk = (psA, "psA") if half == 0 else (psB, "psB")
                        for kt in range(12):
                            A("pe", lambda e, kt=kt, half=half, pst=pst, i=i: e.matmul(
                                pst[:], lhsT=mixT[:, kt, i * 128:(i + 1) * 128], rhs=wo[:, kt, half * 512:(half + 1) * 512],
                                start=(kt == 0), stop=(kt == 11)), reads=mix_keys + ["wo"], writes=[pk])
                        A("dve", lambda e, half=half, pst=pst, bx=bx, t=t: e.tensor_tensor(
                            out=hres[:, half * 512:(half + 1) * 512], in0=pst[:], in1=xs5[bx][:, t, half * 512:(half + 1) * 512], op=ALU.add),
                          reads=[pk, ("xs5", bx)], writes=["hres"])
                    col = i
                    A("act", lambda e, col=col: e.activation(out=junk5, in_=hres, func=AF.Square, accum_out=stat[:, 0, col:col + 1]),
                      reads=["hres"], writes=["junk5", ("ss", col)])
                    A("act", lambda e, col=col: e.activation(out=stat[:, 1, col:col + 1], in_=stat[:, 0, col:col + 1], func=AF.Sqrt,
                                                             scale=1.0 / 1024, bias=epsc[:]), reads=[("ss", col), "epsc"], writes=[("sd", col)])
                    A("dve", lambda e, col=col: e.reciprocal(out=stat[:, 2, col:col + 1], in_=stat[:, 1, col:col + 1]),
                      reads=[("sd", col)], writes=[("rs", col)])
                    A("dve", lambda e, bx=bx, t=t, col=col: e.scalar_tensor_tensor(out=ot[bx][:, t, :], in0=hres, scalar=stat[:, 2, col:col + 1],
                                                                                  in1=fnw, op0=ALU.mult, op1=ALU.mult),
                      reads=["hres", ("rs", col), "fnw"], writes=[("ot", bx)])
                finals.append(dma(out_d[ip * 256:(ip + 1) * 256, :].rearrange("(t p) d -> p t d", p=128), ot[bx], reads=[("ot", bx)]))
            return finals

        def phase6():
            S.nolimit = True
            scr_reset()
            stg = [carve([128, 512], F32) for _ in range(2)]
            fin = []
            for kt in range(4, 12):
                for c in range(4):
                    k = (kt * 4 + c) % 2
                    A("dve", lambda e, kt=kt, c=c, k=k: e.tensor_copy(out=stg[k], in_=mixT[:, kt, c * 512:(c + 1) * 512]),
                      reads=[("mixT", kt)], writes=[("stg", k)])
                    fin.append(dma(mixo_d[:, (kt - 4) * T_OWN + c * 512:(kt - 4) * T_OWN + (c + 1) * 512], stg[k], reads=[("stg", k)]))
            return fin

        def phase7():
            scr_reset()
            stg = [carve([128, 512], F32) for _ in range(2)]
            for kt in range(4, 12):
                for c in range(4):
                    k = (kt * 4 + c) % 2
                    dma(stg[k], mixi_d[:, (kt - 4) * T_OWN + c * 512:(kt - 4) * T_OWN + (c + 1) * 512], writes=[("stg", k)])
                    A("dve", lambda e, kt=kt, c=c, k=k: e.tensor_copy(out=mixT[:, kt, c * 512:(c + 1) * 512], in_=stg[k]),
                      reads=[("stg", k)], writes=[("mixT", kt)])

        def phase8():
            for k in range(int(os.environ.get('DBG_PAD', '1000'))):
                if os.environ.get('DBG_PADENG', 'pool') == 'pe':
                    A("pe", lambda e: e.transpose(out=psT[:, 0:128], in_=identb[:], identity=identb[:]), writes=[("padjunk", k)])
                elif os.environ.get('DBG_PADENG', 'pool') == 'act':
                    A("act", lambda e: e.activation(out=stat[:, 2, 39:40], in_=stat[:, 2, 38:39], func=AF.Copy), writes=[("padjunk", k)])
                else:
                    A(os.environ.get('DBG_PADENG', 'pool'), lambda e: e.memset(stat[:, 2, 39:40], 0.0), writes=[("padjunk", k)])

        fns = {2: phase2, 3: phase3, 4: phase4, 5: phase5, 6: phase6, 7: phase7, 8: phase8}
        finals = None
        for ph in phases:
            if ph in fns:
                r_ = fns[ph]()
                if ph in (5, 6):
                    finals = r_
        S.emit(final_waits=finals)
    return nc


def _t5_bucket(rel):
    n = np.maximum(rel, 0).astype(np.int32)
    is_small = n < 16
    nf = np.maximum(n, 16).astype(np.float32)
    large = 16 + (np.log(nf / np.float32(16.0)) / np.float32(math.log(128 / 16)) * np.float32(16)).astype(np.int32)
    large = np.minimum(large, 31)
    return np.where(is_small, n, large)


_NC_CACHE = {}


def _prep(x, mem, norm_w, w_in, w_alpha2, b_alpha, gla_norm_w, mem_norm_w, w_mem_kv, w_out, rel_bias, final_norm_w):
    f32 = np.float32
    x = np.asarray(x, f32)
    mem = np.asarray(mem, f32)
    rel_bias = np.asarray(rel_bias, f32)

    p = np.arange(128)[:, None, None]
    kt = np.arange(2)[None, :, None]
    q = np.arange(256)[None, None, :]
    kloc = kt * 128 + p
    relD = q - kloc
    bktD = _t5_bucket(relD)
    bktP = _t5_bucket(256 + relD)
    biasD = np.ascontiguousarray(np.transpose(rel_bias[bktD], (3, 0, 1, 2))).reshape(8, 128, 512).astype(f32)
    biasP = np.ascontiguousarray(np.transpose(rel_bias[bktP], (3, 0, 1, 2))).reshape(8, 128, 512).astype(f32)
    diagM = np.where(relD >= 0, 0.0, NEGM).astype(f32).reshape(128, 512)
    c31 = np.ascontiguousarray(np.broadcast_to(rel_bias[31][None, :], (128, 8))).astype(f32)
    ident = np.eye(128, dtype=f32)
    onehot = (np.arange(T_ALL)[None, :] // 256 == np.arange(16)[:, None]).astype(f32)
    s_ = np.arange(128)[:, None]
    t_ = np.arange(128)[None, :]
    tri8 = np.tile((s_ <= t_).astype(f32), (1, 8))
    rmask = np.tile((np.arange(512) % 128 != 0).astype(f32)[None, :], (128, 1))
    common = {
        "w_in": np.ascontiguousarray(np.asarray(w_in, f32)[0]),
        "w_mem": np.ascontiguousarray(np.asarray(w_mem_kv, f32)[0]),
        "w_out": np.ascontiguousarray(np.asarray(w_out, f32)[0]),
        "nwc": np.ascontiguousarray(np.asarray(norm_w, f32)[0].reshape(8, 128).T),
        "mnwc": np.ascontiguousarray(np.asarray(mem_norm_w, f32)[0].reshape(8, 128).T),
        "fnw": np.ascontiguousarray(np.broadcast_to(np.asarray(final_norm_w, f32)[None, :], (128, 1024))),
        "balc": np.ascontiguousarray(np.asarray(b_alpha, f32)[0].reshape(2, 128).T),
        "wa2": np.ascontiguousarray(np.asarray(w_alpha2, f32)[0]),
        "gnwc": np.ascontiguousarray(np.asarray(gla_norm_w, f32)[0].reshape(128, 1)),
        "biasDP": np.ascontiguousarray(np.concatenate([biasD, biasP], axis=2)), "c31": c31, "ident": ident, "diagM": diagM,
        "onehot": onehot, "tri8": tri8, "rmask": rmask,
    }
    in_maps = []
    for core in range(8):
        b, j = core // 2, core % 2
        own = x[b, j * T_OWN:(j + 1) * T_OWN]
        prev = x[b, 0:T_OWN] if j == 1 else np.zeros((T_OWN, 1024), f32)
        gm = np.full((16, 16), -1e30, f32)
        df = np.zeros((16, 16), f32)
        for ti in range(16):
            l = ti // 2
            for n in range(16):
                valid = (n < 8 and j == 1) or (n >= 8 and (n - 8) < l)
                if valid:
                    gm[ti, n] = 0.0
            df[ti, 8 + l] = -NEGM
        m = dict(common)
        m["xall"] = np.ascontiguousarray(np.concatenate([prev, own], axis=0))
        m["memb"] = np.ascontiguousarray(mem[b])
        m["gmask"] = np.ascontiguousarray(np.broadcast_to(gm.reshape(1, 256), (128, 256)))
        m["dforce"] = np.ascontiguousarray(np.broadcast_to(df.reshape(1, 256), (128, 256)))
        in_maps.append(m)
    return in_maps


def kernel(x, mem, norm_w, w_in, w_alpha2, b_alpha, gla_norm_w, mem_norm_w, w_mem_kv, w_out, rel_bias, final_norm_w):
    f32 = np.float32
    in_maps = _prep(x, mem, norm_w, w_in, w_alpha2, b_alpha, gla_norm_w, mem_norm_w, w_mem_kv, w_out, rel_bias, final_norm_w)
    if "nc" not in _NC_CACHE:
        _NC_CACHE["nc"] = build_program((1, 2, 3, 4, 5))
    res = run_bass_kernel_spmd(_NC_CACHE["nc"], in_maps, core_ids=list(range(8)))
    out = np.empty((4, 4096, 1024), f32)
    for core in range(8):
        b, j = core // 2, core % 2
        out[b, j * T_OWN:(j + 1) * T_OWN] = res.results[core]["out"]
    return out
```

```python
import math
import os
import contextlib
import numpy as np
import concourse.bass as bass
import concourse.mybir as mybir
from concourse.bass_utils import run_bass_kernel_spmd

F32 = mybir.dt.float32
BF16 = mybir.dt.bfloat16
ALU = mybir.AluOpType
AF = mybir.ActivationFunctionType
AX = mybir.AxisListType

COMPUTE = ("pe", "act", "dve", "pool")
N_DMA_SEMS = int(os.environ.get("DBG_NDS", "12"))
NEGM = -30000.0
NOP_AFTER_WAIT = int(os.environ.get('DBG_NAW', '1'))
NAW_SKIP = tuple(os.environ.get('DBG_NAWSKIP', 'pe,act,pool,sp,dve').split(','))
EPS = 1e-6


class _Op:
    __slots__ = ("eng", "fn", "deps", "idx", "signal", "count", "dsem", "is_dma", "epoch", "where", "waits", "snap")


class Sched:
    def __init__(self, nc):
        self.nc = nc
        self.ops = {e: [] for e in COMPUTE + ("sp",)}
        self.last_write = {}
        self.readers = {}
        self.dma_rr = 0
        self.dma_last = [None] * N_DMA_SEMS
        self.dma_count = [0] * N_DMA_SEMS
        self.all_ops = []
        self.pending = {}

    def barrier(self):
        lasts = [self.ops[e][-1] for e in self.ops if self.ops[e]]
        lasts += [d for d in self.dma_last if d is not None]
        for e in self.ops:
            self.pending[e] = list(lasts)

    def add(self, eng, fn, reads=(), writes=(), dma=False):
        lim = int(os.environ.get("DBG_MAXOPS", "0"))
        if lim and not getattr(self, "nolimit", False) and len(self.all_ops) >= lim:
            return None
        op = _Op()
        op.eng, op.fn, op.is_dma = eng, fn, dma
        op.signal, op.count, op.dsem = False, None, None
        deps = []
        if self.pending.get(eng):
            deps.extend(self.pending[eng])
            self.pending[eng] = None
        for r in reads:
            w = self.last_write.get(r)
            if w is not None:
                deps.append(w)
        for w_ in writes:
            w = self.last_write.get(w_)
            if w is not None:
                deps.append(w)
            deps.extend(self.readers.get(w_, ()))
        if dma:
            k = self.dma_rr
            self.dma_rr = (self.dma_rr + 1) % N_DMA_SEMS
            op.dsem = k
            if self.dma_last[k] is not None:
                deps.append(self.dma_last[k])
            self.dma_last[k] = op
            self.dma_count[k] += 1
            op.count = 16 * self.dma_count[k]
            op.signal = True
        op.deps = [d for d in deps if d is not op]
        op.idx = len(self.ops[eng])
        if os.environ.get("DBG_DUMP"):
            import sys as _sys
            f = _sys._getframe(1)
            while f.f_code.co_name in ("A", "dma", "add"):
                f = f.f_back
            op.where = "%s:%d" % (f.f_code.co_name, f.f_lineno)
        self.ops[eng].append(op)
        self.all_ops.append(op)
        for r in reads:
            self.readers.setdefault(r, []).append(op)
        for w_ in writes:
            self.last_write[w_] = op
            self.readers[w_] = []
        return op

    @staticmethod
    def _skip(d, op):
        if d.is_dma or op.is_dma or d.eng != op.eng:
            return False
        if d.eng == "pe":
            return True
        return d.idx < op.idx - 3

    def emit(self, final_waits=()):
        nc = self.nc
        for op in self.all_ops:
            for d in op.deps:
                if not d.is_dma and not self._skip(d, op):
                    d.signal = True
        for op in final_waits:
            op.signal = True
        EPOCH = 2000
        nep = {}
        for e in self.ops:
            c = 0
            for op in self.ops[e]:
                if not op.is_dma and op.signal:
                    op.epoch = c // EPOCH
                    op.count = c % EPOCH + 1
                    c += 1
            nep[e] = c // EPOCH + 1
        know = {e: {} for e in self.ops}
        for op in self.all_ops:
            K = know[op.eng]
            cand = {}
            for d in op.deps:
                if d.is_dma:
                    key = ("d", d.dsem)
                else:
                    if self._skip(d, op):
                        continue
                    key = ("c", d.eng, d.epoch)
                if key not in cand or d.count > cand[key].count:
                    cand[key] = d
            order = sorted(cand.items(), key=lambda kv: -len(kv[1].snap))
            op.waits = []
            for key, d in order:
                if K.get(key, 0) >= d.count:
                    continue
                op.waits.append((key, d.count))
                K[key] = d.count
                for k2, v2 in d.snap.items():
                    if v2 > K.get(k2, 0):
                        K[k2] = v2
            op.snap = dict(K)
        with contextlib.ExitStack() as st:
            dsems = [st.enter_context(nc.semaphore("d_%d" % i)) for i in range(N_DMA_SEMS)]
            sems = {(e, k): st.enter_context(nc.semaphore("s_%s_%d" % (e, k))) for e in self.ops for k in range(nep[e])}
            block = st.enter_context(nc.Block())

            def run(e, handle):
                seen = {}
                for _ in range(int(os.environ.get("DBG_NOP_" + e.upper(), "0"))):
                    handle.engine_nop()
                for op in self.ops[e]:
                    wl = op.waits
                    for key, val in wl:
                        handle.wait_ge(dsems[key[1]] if key[0] == "d" else sems[(key[1], key[2])], val)
                    if wl and NOP_AFTER_WAIT and e not in NAW_SKIP:
                        handle.nop()
                    if os.environ.get("DBG_DUMP"):
                        with open(os.environ["DBG_DUMP"], "a") as fh:
                            fh.write("%s %d %s waits=%s sig=%s\n" % (e, op.idx, op.where, wl,
                                     (("d", op.dsem, op.count) if op.is_dma else ((e, op.epoch, op.count) if op.signal else None))))
                    inst = op.fn(handle)
                    if op.is_dma:
                        inst.then_inc(dsems[op.dsem], 16)
                    elif op.signal:
                        inst.then_inc(sems[(e, op.epoch)], 1)
                if e == "sp":
                    for op in final_waits:
                        handle.wait_ge(dsems[op.dsem] if op.is_dma else sems[(op.eng, op.epoch)], op.count)

            block.sync(lambda h: run("sp", h))
            block.tensor(lambda h: run("pe", h))
            block.scalar(lambda h: run("act", h))
            block.vector(lambda h: run("dve", h))
            block.gpsimd(lambda h: run("pool", h))


QA, KA, VA, GA, QB, KB, VB, GB, ZB, QC, GC = 0, 512, 1024, 1536, 2048, 2304, 2560, 3072, 3584, 3600, 4112
T_OWN = 2048
T_ALL = 4096
SCR_BYTES = 73 * 1024


def build_program(phases=(1, 4, 2, 3, 5), mode=None):
    nc = bass.Bass("TRN2", target_bir_lowering=False)

    def din(name, shape):
        return nc.dram_tensor(name, list(shape), F32, kind="ExternalInput").ap()

    xall = din("xall", [T_ALL, 1024])
    memb = din("memb", [256, 1024])
    w_in = din("w_in", [1024, 4624])
    w_mem = din("w_mem", [1024, 1024])
    w_out = din("w_out", [1536, 1024])
    nwc_d = din("nwc", [128, 8])
    mnwc_d = din("mnwc", [128, 8])
    fnw_d = din("fnw", [128, 1024])
    balc_d = din("balc", [128, 2])
    wa2_d = din("wa2", [16, 256])
    gnwc_d = din("gnwc", [128, 1])
    biasDP_d = din("biasDP", [8, 128, 1024])
    c31_d = din("c31", [128, 8])
    ident_d = din("ident", [128, 128])
    diagM_d = din("diagM", [128, 512])
    gmask_d = din("gmask", [128, 256])
    dforce_d = din("dforce", [128, 256])
    onehot_d = din("onehot", [16, T_ALL])
    tri8_d = din("tri8", [128, 1024])
    rmask_d = din("rmask", [128, 512])
    out_d = nc.dram_tensor("out", [T_OWN, 1024], F32, kind="ExternalOutput").ap() if mode != "A" else None
    mixo_d = nc.dram_tensor("mixo", [128, 8 * T_OWN], F32, kind="ExternalOutput").ap() if mode == "A" else None
    mixi_d = din("mixi", [128, 8 * T_OWN]) if mode == "B" else None

    with contextlib.ExitStack() as st:
        def sb(name, shape, dt):
            return st.enter_context(nc.sbuf_tensor(name, list(shape), dt))

        def ps(name, shape, dt):
            return st.enter_context(nc.psum_tensor(name, list(shape), dt))

        uT = sb("uT", [128, 8, T_ALL], BF16)
        mixT = sb("mixT", [128, 12, T_OWN], BF16)
        wst = sb("wst", [128, 8, 256], F32)
        wb = [sb("wb0", [128, 8, 256], BF16), sb("wb1", [128, 8, 256], BF16)]
        identb = sb("identb", [128, 128], BF16)
        onesb = sb("onesb", [128, 128], BF16)
        onesf = sb("onesf", [128, 128], F32)
        nwc = sb("nwc_s", [128, 8], F32)
        mnwc = sb("mnwc_s", [128, 8], F32)
        c31 = sb("c31_s", [128, 8], F32)
        balc = sb("balc_s", [128, 2], F32)
        nbal = sb("nbal_s", [128, 2], F32)
        wa2 = sb("wa2_s", [16, 256], F32)
        gnwc = sb("gnwc_s", [128, 1], F32)
        epsc = sb("epsc", [128, 1], F32)
        kmTx = sb("kmTx", [128, 4, 256], BF16)
        vmx = sb("vmx", [128, 2, 512], BF16)
        stat = sb("stat", [128, 3, 40], F32)
        scr = sb("scr", [128, SCR_BYTES // 4], F32)

        psAB = ps("psAB", [128, 1024], F32)
        psA = psAB[:, 0:512]
        psB = psAB[:, 512:1024]
        psS = [ps("psS0", [128, 1024], F32), ps("psS1", [128, 1024], F32)]
        psO = ps("psO", [128, 512], F32)
        psT = ps("psT", [128, 1024], BF16)
        psO_bf = psO[:].bitcast(BF16)

        S = Sched(nc)
        scr_off = [0]

        def scr_reset():
            S.barrier()
            scr_off[0] = 0

        def carve(shape, dt):
            n = 1
            for s_ in shape[1:]:
                n *= s_
            nbytes = n * (2 if dt == BF16 else 4)
            nbytes = (nbytes + 63) // 64 * 64
            o = scr_off[0]
            assert o + nbytes <= SCR_BYTES, ("scratch overflow", o, nbytes)
            scr_off[0] = o + nbytes
            v = scr[:, o // 4:(o + nbytes) // 4]
            if dt == BF16:
                v = v.bitcast(BF16)
            v = v[:, 0:n]
            if len(shape) == 3:
                v = v.rearrange("p (a b) -> p a b", a=shape[1])
            elif len(shape) == 4:
                v = v.rearrange("p (a b c) -> p a b c", a=shape[1], b=shape[2])
            if shape[0] < 128:
                v = v[0:shape[0]]
            return v

        def dma(out, in_, reads=(), writes=(), eng="sp"):
            return S.add(eng, lambda e, o=out, i=in_: e.dma_start(out=o, in_=i), reads=reads, writes=writes, dma=True)

        def A(eng, fn, reads=(), writes=()):
            return S.add(eng, fn, reads=reads, writes=writes)

        identf = scr[:, 0:128]
        dma(identf, ident_d, writes=["identf"])
        A("dve", lambda e: e.tensor_copy(out=identb[:], in_=identf), reads=["identf"], writes=["identb"])
        A("dve", lambda e: e.memset(onesb[:], 1.0), writes=["onesb"])
        A("dve", lambda e: e.memset(onesf[:], 1.0), writes=["onesf"])
        A("dve", lambda e: e.memset(epsc[:], EPS), writes=["epsc"])
        dma(nwc[:], nwc_d, writes=["nwc"])
        dma(mnwc[:], mnwc_d, writes=["mnwc"])
        dma(c31[:], c31_d, writes=["c31"])
        dma(balc[:], balc_d, writes=["balc"])
        dma(wa2[:], wa2_d, writes=["wa2"])
        dma(gnwc[:], gnwc_d, writes=["gnwc"])
        A("dve", lambda e: e.tensor_scalar(out=nbal[:], in0=balc[:], scalar1=-1.0, scalar2=None, op0=ALU.mult),
          reads=["balc"], writes=["nbal"])

        wcount = [0]

        def load_w(dram, pieces, scale, kc0=0, nkc=8, dst=None, dkey=None):
            if dst is None:
                k = wcount[0] % 2
                wcount[0] += 1
                dst, dkey = wb[k], "wb%d" % k
            off = 0
            for (c0, W) in pieces:
                src = dram[kc0 * 128:(kc0 + nkc) * 128, c0:c0 + W].rearrange("(kc p) c -> p kc c", p=128)
                dma(wst[:, 0:nkc, off:off + W], src, writes=["wst"])
                off += W
            for kc in range(nkc):
                if scale is not None:
                    A("dve", lambda e, kc=kc, off=off: e.tensor_scalar(
                        out=dst[:, kc, 0:off], in0=wst[:, kc, 0:off], scalar1=scale[:, kc0 + kc:kc0 + kc + 1],
                        scalar2=None, op0=ALU.mult), reads=["wst", "nwc", "mnwc"], writes=[dkey])
                else:
                    A("dve", lambda e, kc=kc, off=off: e.tensor_copy(out=dst[:, kc, 0:off], in_=wst[:, kc, 0:off]),
                      reads=["wst"], writes=[dkey])
            return dst, dkey

        ntile = [0]

        pend_B = []

        def norm_transpose(src2, dstTs, dkey, xs, ub, junk):
            ip = ntile[0] // 2
            bx = ip % len(xs)
            dma(xs[bx], src2.rearrange("(t p) d -> p t d", p=128), writes=[("xs", bx)])
            for t in range(2):
                partB = norm_partA(xs[bx][:, t, :], ("xs", bx), dstTs[t], dkey, ub, junk)
                if len(pend_B) >= 2:
                    pend_B.pop(0)()
                pend_B.append(partB)

        def norm_flush():
            while pend_B:
                pend_B.pop(0)()

        def norm_partA(xt, xkey, dstT, dkey, ub, junk):
            i = ntile[0]
            ntile[0] += 1
            b2 = i % len(ub)
            col = i % 40
            pT, pTk = (psT, "psT") if i % 2 == 0 else (psO_bf, "psO")
            A("act", lambda e: e.activation(out=junk, in_=xt, func=AF.Square, accum_out=stat[:, 0, col:col + 1]),
              reads=[xkey], writes=["junk", ("ss", col)])
            A("act", lambda e: e.activation(out=stat[:, 1, col:col + 1], in_=stat[:, 0, col:col + 1], func=AF.Sqrt,
                                            scale=1.0 / 1024, bias=epsc[:]), reads=[("ss", col), "epsc"], writes=[("sd", col)])
            A("dve", lambda e: e.reciprocal(out=stat[:, 2, col:col + 1], in_=stat[:, 1, col:col + 1]),
              reads=[("sd", col)], writes=[("rs", col)])
            A("dve", lambda e: e.tensor_scalar(out=ub[b2], in0=xt, scalar1=stat[:, 2, col:col + 1], scalar2=None,
                                               op0=ALU.mult), reads=[xkey, ("rs", col)], writes=[("ub", b2)])

            def partB():
                for kc in range(8):
                    A("pe", lambda e, kc=kc: e.transpose(out=pT[:, kc * 128:(kc + 1) * 128], in_=ub[b2][:, kc * 128:(kc + 1) * 128],
                                                         identity=identb[:]), reads=[("ub", b2), "identb"], writes=[pTk])
                src = pT[:, 0:1024].rearrange("p (a b) -> p a b", a=8)
                if i % 2 == 0:
                    A("act", lambda e: e.activation(out=dstT, in_=src, func=AF.Copy), reads=[pTk], writes=[dkey])
                else:
                    A("dve", lambda e: e.tensor_copy(out=dstT, in_=src), reads=[pTk], writes=[dkey])
            return partB

        def proj_F(wt, wkey, coff, M, srcT, skey, t0, N, pst, pkey):
            for kc in range(8):
                A("pe", lambda e, kc=kc: e.matmul(pst, lhsT=wt[:, kc, coff:coff + M], rhs=srcT[:, kc, t0:t0 + N],
                                                  start=(kc == 0), stop=(kc == 7)), reads=[wkey, skey], writes=[pkey])

        def proj_T(wt, wkey, coff, NC, srcT, skey, t0, pst, pkey):
            for kc in range(8):
                A("pe", lambda e, kc=kc: e.matmul(pst, lhsT=srcT[:, kc, t0:t0 + 128], rhs=wt[:, kc, coff:coff + NC],
                                                  start=(kc == 0), stop=(kc == 7)), reads=[wkey, skey], writes=[pkey])

        scr_reset()
        _ = carve([128, 128], F32)
        xs = [carve([128, 2, 1024], F32) for _ in range(4)]
        ub = [carve([128, 1024], BF16) for _ in range(3)]
        junk = carve([128, 1024], BF16)
        mT = carve([128, 8, 256], BF16)
        norm_transpose(memb[0:256, :], [mT[:, :, 0:128], mT[:, :, 128:256]], "mT", xs, ub, junk)
        norm_flush()
        for half in range(2):
            for cc in range(2):
                wt, wk = load_w(w_mem, [(half * 512 + cc * 256, 256)], mnwc)
                if half == 0:
                    for c2 in range(2):
                        c = cc * 2 + c2
                        proj_F(wt, wk, c2 * 128, 128, mT, "mT", 0, 256, psA[:, 0:256], "psA")
                        A("dve", lambda e, c=c: e.tensor_copy(out=kmTx[:, c, :], in_=psA[:, 0:256]), reads=["psA"], writes=["kmTx"])
                else:
                    for mt in range(2):
                        proj_T(wt, wk, 0, 256, mT, "mT", mt * 128, psB[:, 0:256], "psB")
                        A("dve", lambda e, mt=mt, cc=cc: e.tensor_copy(out=vmx[:, mt, cc * 256:(cc + 1) * 256], in_=psB[:, 0:256]),
                          reads=["psB"], writes=["vmx"])
        for i in range(16):
            norm_transpose(xall[i * 256:(i + 1) * 256, :], [uT[:, :, i * 256:i * 256 + 128], uT[:, :, i * 256 + 128:(i + 1) * 256]],
                           "uT", xs, ub, junk)
        norm_flush()

        def phase2():
            scr_reset()
            qcT = carve([128, T_OWN], BF16)
            gcT = carve([128, T_OWN], BF16)
            ptx = [carve([128, 2, 512], BF16) for _ in range(2)]
            recx = carve([128, 512], F32)
            t1x = carve([128, 512], F32)
            XS = 1.0 / math.sqrt(128.0)
            for c in range(int(os.environ.get('DBG_NXH', '4'))):
                wt, wk = load_w(w_in, [(QC + c * 128, 128), (GC + c * 128, 128)], nwc)
                for g in range(int(os.environ.get('DBG_NXPG', '4'))):
                    if 'q' in os.environ.get('DBG_XPARTS', 'qg'):
                        proj_F(wt, wk, 0, 128, uT, "uT", T_OWN + g * 512, 512, psA[:], "psA")
                        A("dve", lambda e, g=g: e.tensor_copy(out=qcT[:, g * 512:(g + 1) * 512], in_=psA[:]), reads=["psA"], writes=[("qcT", g)])
                    if 'g' not in os.environ.get('DBG_XPARTS', 'qg'):
                        continue
                    proj_F(wt, wk, 128, 128, uT, "uT", T_OWN + g * 512, 512, psB[:], "psB")
                    A("act", lambda e, g=g: e.activation(out=gcT[:, g * 512:(g + 1) * 512], in_=psB[:], func=(AF.Copy if os.environ.get("DBG_NOSILU") else AF.Silu)),
                      reads=["psB"], writes=[("gcT", g)])
                for g in range(int(os.environ.get('DBG_NXG', '4'))):
                    pS = psS[g % 2]
                    pk = "psS%d" % (g % 2)
                    pt = ptx[g % 2]
                    ptk = ("ptx", g % 2)
                    for mt in range(2):
                        A("pe", lambda e, mt=mt, pS=pS, g=g, c=c: e.matmul(pS[:, mt * 512:(mt + 1) * 512], lhsT=kmTx[:, c, mt * 128:(mt + 1) * 128],
                                                                          rhs=qcT[:, g * 512:(g + 1) * 512], start=True, stop=True),
                          reads=["kmTx", ("qcT", g)], writes=[pk])
                    A("act", lambda e, pS=pS, pt=pt: e.activation(out=pt.rearrange("p a b -> p (a b)"), in_=pS[:], func=AF.Exp, scale=XS),
                      reads=[pk], writes=[ptk])
                    for mt in range(2):
                        A("pe", lambda e, mt=mt, pt=pt, c=c: e.matmul(psO[:], lhsT=vmx[:, mt, c * 128:(c + 1) * 128], rhs=pt[:, mt, :],
                                                                     start=(mt == 0), stop=(mt == 1)), reads=["vmx", ptk], writes=["psO"])
                    for mt in range(2):
                        A("pe", lambda e, mt=mt, pt=pt: e.matmul(psB[:], lhsT=onesb[:], rhs=pt[:, mt, :], start=(mt == 0), stop=(mt == 1)),
                          reads=["onesb", ptk], writes=["psB"])
                    A("dve", lambda e: e.reciprocal(out=recx, in_=psB[:]), reads=["psB"], writes=["recx"])
                    A("dve", lambda e: e.tensor_tensor(out=t1x, in0=psO[:], in1=recx, op=ALU.mult), reads=["psO", "recx"], writes=["t1x"])
                    A("pool", lambda e, g=g, c=c: e.tensor_tensor(out=mixT[:, 8 + c, g * 512:(g + 1) * 512], in0=t1x,
                                                                 in1=gcT[:, g * 512:(g + 1) * 512], op=ALU.mult),
                      reads=["t1x", ("gcT", g)], writes=[("mixT", 8 + c)])

        def phase3():
            scr_reset()
            tri8 = carve([128, 1024], BF16)
            rmask = carve([128, 512], F32)
            tri8f = carve([128, 1024], F32)
            dma(tri8f, tri8_d, writes=["tri8f"])
            A("dve", lambda e: e.tensor_copy(out=tri8, in_=tri8f), reads=["tri8f"], writes=["tri8"])
            dma(rmask, rmask_d, writes=["rmask"])
            scr_off[0] -= 4096
            S.barrier()
            bufs = []
            for pr in range(2):
                b = {}
                b["wg"] = carve([128, 8, 784], BF16)
                b["zt"] = carve([16, 512], F32)
                b["tmpA"] = carve([128, 512], F32)
                b["Bg"] = carve([128, 512], F32)
                b["tmpE"] = carve([128, 512], F32)
                b["sdg"] = carve([128, 512], F32)
                b["ktg"] = carve([128, 512], BF16)
                b["khg"] = carve([128, 512], BF16)
                b["qtg"] = carve([128, 512], BF16)
                b["dg"] = carve([128, 4], F32)
                b["khtok"] = carve([128, 4, 128], BF16)
                b["vbg"] = carve([128, 4, 256], BF16)
                b["Sbg"] = carve([128, 4, 128], BF16)
                b["Sst"] = carve([128, 128], F32)
                b["attm"] = carve([128, 2, 4, 128], BF16)
                b["gbg"] = carve([128, 512], BF16)
                bufs.append(b)
            for pr in range(2):
                wg = bufs[pr]["wg"]
                wk_ = ("wg", pr)
                load_w(w_in, [(QB + pr * 128, 128), (KB + pr * 128, 128)], nwc, dst=wg[:, :, 0:256], dkey=wk_)
                load_w(w_in, [(VB + pr * 256, 256)], nwc, dst=wg[:, :, 256:512], dkey=wk_)
                load_w(w_in, [(GB + pr * 256, 256)], nwc, dst=wg[:, :, 512:768], dkey=wk_)
                load_w(w_in, [(ZB, 16)], nwc, dst=wg[:, :, 768:784], dkey=wk_)
                A("dve", lambda e, pr=pr: e.memset(bufs[pr]["Sst"], 0.0), writes=[("Sst", pr)])

            def group_body(pr, g):
                b = bufs[pr]
                wg, zt, tmpA, Bg, tmpE, sdg = b["wg"], b["zt"], b["tmpA"], b["Bg"], b["tmpE"], b["sdg"]
                ktg, khg, qtg, dg, khtok, vbg, Sbg, Sst, attm, gbg = (b["ktg"], b["khg"], b["qtg"], b["dg"], b["khtok"], b["vbg"],
                                                                     b["Sbg"], b["Sst"], b["attm"], b["gbg"])

                def k(name):
                    return (name, pr)
                wk_ = k("wg")
                own = g >= 4
                t0 = g * 512
                zt = bufs[g % 2]["zt"]
                zkey = ("ztS", g % 2)
                if pr == 0:
                    proj_F(wg, wk_, 768, 16, uT, "uT", t0, 512, psA[0:16, :], "psA")
                    A("act", lambda e: e.activation(out=zt, in_=psA[0:16, :], func=AF.Copy), reads=["psA"], writes=[zkey])
                yield
                A("pe", lambda e: e.matmul(psB[:], lhsT=wa2[0:16, pr * 128:(pr + 1) * 128], rhs=zt, start=True, stop=True),
                  reads=["wa2", zkey], writes=["psB"])
                A("act", lambda e: e.activation(out=tmpA, in_=psB[:], func=AF.Exp, scale=-1.0, bias=nbal[:, pr:pr + 1]),
                  reads=["psB", "nbal"], writes=[k("tmpA")])
                yield
                A("act", lambda e: e.activation(out=tmpA, in_=tmpA, func=AF.Ln, bias=1.0), reads=[k("tmpA")], writes=[k("tmpA")])
                A("dve", lambda e: e.tensor_tensor_scan(out=Bg, data0=rmask, data1=tmpA, initial=0.0, op0=ALU.mult, op1=ALU.add),
                  reads=[k("tmpA"), "rmask"], writes=[k("Bg")])
                yield
                proj_F(wg, wk_, 128, 128, uT, "uT", t0, 512, psA[:], "psA")
                A("act", lambda e: e.activation(out=tmpE, in_=Bg, func=AF.Exp, scale=1.0 / 16), reads=[k("Bg")], writes=[k("tmpE")])
                A("dve", lambda e: e.tensor_tensor(out=ktg, in0=psA[:], in1=tmpE, op=ALU.mult), reads=["psA", k("tmpE")], writes=[k("ktg")])
                A("act", lambda e: e.activation(out=dg, in_=Bg.rearrange("p (a b) -> p a b", a=4)[:, :, 127], func=AF.Exp, scale=-1.0 / 16),
                  reads=[k("Bg")], writes=[k("dg")])
                yield
                for ch in range(4):
                    A("dve", lambda e, ch=ch: e.tensor_scalar(out=khg[:, ch * 128:(ch + 1) * 128], in0=ktg[:, ch * 128:(ch + 1) * 128],
                                                              scalar1=dg[:, ch:ch + 1], scalar2=None, op0=ALU.mult),
                      reads=[k("ktg"), k("dg")], writes=[k("khg")])
                for ch in range(4):
                    A("pe", lambda e, ch=ch: e.transpose(out=psT[:, ch * 128:(ch + 1) * 128], in_=khg[:, ch * 128:(ch + 1) * 128],
                                                         identity=identb[:]), reads=[k("khg"), "identb"], writes=["psT"])
                A("dve", lambda e: e.tensor_copy(out=khtok, in_=psT[:, 0:512].rearrange("p (a b) -> p a b", a=4)),
                  reads=["psT"], writes=[k("khtok")])
                yield
                for ch in range(4):
                    proj_T(wg, wk_, 256, 256, uT, "uT", t0 + ch * 128, psS[0][:, ch * 256:(ch + 1) * 256], "psS0")
                A("act", lambda e: e.activation(out=vbg.rearrange("p a b -> p (a b)"), in_=psS[0][:], func=AF.Copy),
                  reads=["psS0"], writes=[k("vbg")])
                yield
                for ch in range(4):
                    A("pe", lambda e, ch=ch: e.matmul(psS[1][:, ch * 256:(ch + 1) * 256], lhsT=khtok[:, ch, :], rhs=vbg[:, ch, :],
                                                      start=True, stop=True), reads=[k("khtok"), k("vbg")], writes=["psS1"])
                for ch in range(4):
                    if own:
                        A("pool", lambda e, ch=ch: e.tensor_copy(out=Sbg[:, ch, :], in_=Sst), reads=[k("Sst")], writes=[k("Sbg")])
                    for hh in range(2):
                        r0 = hh * 64
                        A("dve", lambda e, ch=ch, hh=hh, r0=r0: e.scalar_tensor_tensor(
                            out=Sst[r0:r0 + 64, :], in0=Sst[r0:r0 + 64, :], scalar=dg[r0:r0 + 64, ch:ch + 1],
                            in1=psS[1][r0:r0 + 64, ch * 256 + hh * 128:ch * 256 + (hh + 1) * 128], op0=ALU.mult, op1=ALU.add),
                          reads=[k("Sst"), k("dg"), "psS1"], writes=[k("Sst")])
                yield
                if not own:
                    return
                proj_F(wg, wk_, 0, 128, uT, "uT", t0, 512, psA[:], "psA")
                A("act", lambda e: e.activation(out=tmpE, in_=Bg, func=AF.Exp, scale=-1.0 / 16), reads=[k("Bg")], writes=[k("tmpE")])
                A("dve", lambda e: e.scalar_tensor_tensor(out=qtg, in0=psA[:], scalar=0.125, in1=tmpE, op0=ALU.mult, op1=ALU.mult),
                  reads=["psA", k("tmpE")], writes=[k("qtg")])
                yield
                for hh in range(2):
                    r0 = hh * 64
                    for ch in range(4):
                        A("pe", lambda e, hh=hh, ch=ch, r0=r0: e.matmul(
                            psS[0][:, (hh * 4 + ch) * 128:(hh * 4 + ch + 1) * 128], lhsT=ktg[r0:r0 + 64, ch * 128:(ch + 1) * 128],
                            rhs=qtg[r0:r0 + 64, ch * 128:(ch + 1) * 128], start=True, stop=True),
                          reads=[k("ktg"), k("qtg")], writes=["psS0"])
                A("dve", lambda e: e.tensor_tensor(out=attm.rearrange("p a b c -> p (a b c)"), in0=psS[0][:], in1=tri8, op=ALU.mult),
                  reads=["psS0", "tri8"], writes=[k("attm")])
                yield
                for hh in range(2):
                    r0 = hh * 64
                    hd = pr * 2 + hh
                    for ch in range(4):
                        A("pe", lambda e, hh=hh, ch=ch, r0=r0: e.matmul(
                            psO[:, ch * 128:(ch + 1) * 128], lhsT=Sbg[r0:r0 + 64, ch, :], rhs=qtg[r0:r0 + 64, ch * 128:(ch + 1) * 128],
                            start=True, stop=False), reads=[k("Sbg"), k("qtg")], writes=["psO"])
                        A("pe", lambda e, hh=hh, ch=ch: e.matmul(
                            psO[:, ch * 128:(ch + 1) * 128], lhsT=vbg[:, ch, hh * 128:(hh + 1) * 128], rhs=attm[:, hh, ch, :],
                            start=False, stop=True), reads=[k("vbg"), k("attm")], writes=["psO"])
                    A("act", lambda e: e.activation(out=khg, in_=psO[:], func=AF.Square), reads=["psO"], writes=[k("khg")])
                    A("pe", lambda e: e.matmul(psB[:], lhsT=onesb[:], rhs=khg, start=True, stop=True), reads=["onesb", k("khg")], writes=["psB"])
                    A("act", lambda e: e.activation(out=sdg, in_=psB[:], func=AF.Sqrt, scale=1.0 / 128, bias=epsc[:]),
                      reads=["psB", "epsc"], writes=[k("sdg")])
                    A("dve", lambda e: e.reciprocal(out=sdg, in_=sdg), reads=[k("sdg")], writes=[k("sdg")])
                    A("dve", lambda e: e.scalar_tensor_tensor(out=tmpE, in0=psO[:], scalar=gnwc[:, 0:1], in1=sdg, op0=ALU.mult, op1=ALU.mult),
                      reads=["psO", "gnwc", k("sdg")], writes=[k("tmpE")])
                    proj_F(wg, wk_, 512 + hh * 128, 128, uT, "uT", t0, 512, psA[:], "psA")
                    A("act", lambda e: e.activation(out=gbg, in_=psA[:], func=AF.Silu), reads=["psA"], writes=[k("gbg")])
                    A("pool", lambda e, hd=hd: e.tensor_tensor(out=mixT[:, 4 + hd, (g - 4) * 512:(g - 3) * 512], in0=tmpE, in1=gbg, op=ALU.mult),
                      reads=[k("tmpE"), k("gbg")], writes=[("mixT", 4 + hd)])
                    yield

            for g in range(8):
                gens = [group_body(0, g), group_body(1, g)]
                while gens:
                    for gen in list(gens):
                        try:
                            next(gen)
                        except StopIteration:
                            gens.remove(gen)

        def phase4():
            scr_reset()
            Kaug = [carve([80, T_ALL], BF16) for _ in range(2)]
            Qaug = [carve([80, T_OWN], BF16) for _ in range(2)]
            V3 = carve([128, 32, 192], BF16)
            gaT = carve([128, T_OWN], BF16)
            ptm = [carve([128, 2, 512], BF16) for _ in range(3)]
            sbias = carve([128, 2, 512], F32)
            bDP = [carve([128, 4, 256], F32) for _ in range(2)]
            bD = [t_[:, 0:2, :] for t_ in bDP]
            bP = [t_[:, 2:4, :] for t_ in bDP]
            diagM = carve([128, 2, 256], F32)
            gmask = carve([128, 16, 16], F32)
            dforce = carve([128, 16, 16], F32)
            gm = carve([128, 16, 16], F32)
            m8 = carve([128, 16, 8], F32)
            thrc = carve([128, 16], F32)
            Fsel = carve([128, 16, 16], F32)
            FTin = carve([128, 16, 80], BF16)
            recm = carve([128, 512], F32)
            t1m = carve([128, 512], F32)
            ksum = carve([128, 16], F32)
            ksumB = carve([128, 16], F32)
            kmT = [carve([64, 16], BF16) for _ in range(2)]
            ohst = sbias.rearrange("p a b -> p (a b)")[0:80]
            dma(diagM.rearrange("p a b -> p (a b)"), diagM_d, writes=["diagM"])
            dma(gmask.rearrange("p a b -> p (a b)"), gmask_d, writes=["gmask"])
            dma(dforce.rearrange("p a b -> p (a b)"), dforce_d, writes=["dforce"])
            for q4 in range(4):
                dma(ohst[64:80, :], onehot_d[:, q4 * 1024:(q4 + 1) * 1024], writes=[("sbias", 0), ("sbias", 256)])
                for hh in range(2):
                    A("dve", lambda e, hh=hh, q4=q4: e.tensor_copy(out=Kaug[hh][64:80, q4 * 1024:(q4 + 1) * 1024], in_=ohst[64:80, :]),
                      reads=[("sbias", 0), ("sbias", 256)], writes=[("Koh", hh)])
            MS = os.environ.get("DBG_MSENG", "pool")
            A(MS, lambda e: e.memset(V3[:, :, 64:128], 1.0), writes=["V3ones"])
            A(MS, lambda e: e.memset(FTin.rearrange("p a b -> p (a b)"), 0.0), writes=["FTin"])
            for p in range(int(os.environ.get('DBG_NPAIR', '4'))):
                PARTS = os.environ.get('DBG_P4PARTS', 'kqvg')
                wt, wk = load_w(w_in, [(KA + p * 128, 128), (QA + p * 128, 128)], nwc)
                for g in range(8 if 'k' in PARTS else 0):
                    pst, pk = (psA, "psA") if g % 2 == 0 else (psB, "psB")
                    proj_F(wt, wk, 0, 128, uT, "uT", g * 512, 512, pst[:], pk)
                    for blk in range(2):
                        cs = slice(blk * 256, (blk + 1) * 256)
                        ts = slice(g * 512 + blk * 256, g * 512 + (blk + 1) * 256)
                        col = g * 2 + blk
                        A("act", lambda e, pst=pst, cs=cs, ts=ts, col=col: e.activation(
                            out=Kaug[0][0:64, ts], in_=pst[0:64, cs], func=AF.Copy, accum_out=ksum[0:64, col:col + 1]),
                          reads=[pk], writes=[("Kaug", 0), ("ksA", col)])
                        A("dve", lambda e, pst=pst, cs=cs, ts=ts, col=col: e.tensor_scalar(
                            out=Kaug[1][0:64, ts], in0=pst[64:128, cs], scalar1=1.0, scalar2=0.0, op0=ALU.mult, op1=ALU.add,
                            accum_out=ksumB[0:64, col:col + 1]), reads=[pk], writes=[("Kaug", 1), ("ksB", col)])
                A("dve", lambda e: e.tensor_scalar(out=kmT[0][:], in0=ksum[0:64, :], scalar1=1.0 / 256, scalar2=None, op0=ALU.mult),
                  reads=[("ksA", c_) for c_ in range(16)], writes=[("kmT", 0)])
                A("dve", lambda e: e.tensor_scalar(out=kmT[1][:], in0=ksumB[0:64, :], scalar1=1.0 / 256, scalar2=None, op0=ALU.mult),
                  reads=[("ksB", c_) for c_ in range(16)], writes=[("kmT", 1)])
                for g in range(4 if 'q' in PARTS else 0):
                    pst, pk = (psA, "psA") if g % 2 == 0 else (psB, "psB")
                    proj_F(wt, wk, 128, 128, uT, "uT", T_OWN + g * 512, 512, pst[:], pk)
                    A("act", lambda e, g=g, pst=pst: e.activation(out=Qaug[0][0:64, g * 512:(g + 1) * 512], in_=pst[0:64, :], func=AF.Copy, scale=0.125),
                      reads=[pk], writes=[("Qaug", 0)])
                    A("dve", lambda e, g=g, pst=pst: e.tensor_scalar(out=Qaug[1][0:64, g * 512:(g + 1) * 512], in0=pst[64:128, :], scalar1=0.125,
                                                                     scalar2=None, op0=ALU.mult), reads=[pk], writes=[("Qaug", 1)])
                wt, wk = load_w(w_in, [(VA + p * 128, 128), (GA + p * 128, 128)], nwc)
                def gen_vg():
                    for g4 in range(8 if 'v' in PARTS else 0):
                        pst, pk = (psA, "psA") if g4 % 2 == 0 else (psB, "psB")
                        for ti in range(4):
                            proj_T(wt, wk, 0, 128, uT, "uT", (g4 * 4 + ti) * 128, pst[:, ti * 128:(ti + 1) * 128], pk)
                        pv = pst[:].rearrange("p (a b) -> p a b", a=4)
                        A("act", lambda e, g4=g4, pv=pv: e.activation(out=V3[:, g4 * 4:(g4 + 1) * 4, 0:64], in_=pv[:, :, 0:64], func=AF.Copy),
                          reads=[pk], writes=["V3"])
                        A("dve", lambda e, g4=g4, pv=pv: e.tensor_copy(out=V3[:, g4 * 4:(g4 + 1) * 4, 128:192], in_=pv[:, :, 64:128]),
                          reads=[pk], writes=["V3"])
                        yield
                    for g in range(4 if 'g' in PARTS else 0):
                        pst, pk = (psA, "psA") if g % 2 == 0 else (psB, "psB")
                        proj_F(wt, wk, 128, 128, uT, "uT", T_OWN + g * 512, 512, pst[:], pk)
                        A("act", lambda e, g=g, pst=pst: e.activation(out=gaT[:, g * 512:(g + 1) * 512], in_=pst[:], func=AF.Silu),
                          reads=[pk], writes=["gaT"])
                        yield
                        yield
                def gen_gate():
                    for hh in range(int(os.environ.get('DBG_NHH', '2'))):
                        h = p * 2 + hh
                        dma(bDP[hh].rearrange("p a b -> p (a b)"), biasDP_d[h], writes=[("bD", hh), ("bP", hh)])
                        A("dve", lambda e, hh=hh, h=h: e.scalar_tensor_tensor(
                            out=bD[hh], in0=bD[hh], scalar=c31[:, h:h + 1], in1=diagM, op0=ALU.subtract, op1=ALU.add),
                          reads=[("bD", hh), "c31", "diagM"], writes=[("bD", hh)])
                        A("dve", lambda e, hh=hh, h=h: e.tensor_scalar(
                            out=bP[hh], in0=bP[hh], scalar1=c31[:, h:h + 1], scalar2=None, op0=ALU.subtract), reads=[("bP", hh), "c31"], writes=[("bP", hh)])
                        yield
                        gps = psS[0][:, 0:256].rearrange("p (a b) -> p a b", a=16)
                        for ti in range(16):
                            A("pe", lambda e, ti=ti, hh=hh: e.matmul(psS[0][:, ti * 16:(ti + 1) * 16], lhsT=Qaug[hh][0:64, ti * 128:(ti + 1) * 128],
                                                                     rhs=kmT[hh][:], start=True, stop=True),
                              reads=[("Qaug", hh), ("kmT", hh)], writes=["psS0"])
                        A("dve", lambda e: e.tensor_tensor(out=gm, in0=gps, in1=gmask, op=ALU.add), reads=["psS0", "gmask"], writes=["gm"])
                        for ti in range(16):
                            A("dve", lambda e, ti=ti: e.max(out=m8[:, ti, :], in_=gm[:, ti, :]), reads=["gm"], writes=["m8"])
                        A("dve", lambda e: e.tensor_scalar(out=thrc, in0=m8[:, :, 2], scalar1=-1e29, scalar2=None, op0=ALU.max),
                          reads=["m8"], writes=["thrc"])
                        yield
                        for ti in range(16):
                            A("dve", lambda e, ti=ti: e.tensor_scalar(out=Fsel[:, ti, :], in0=gm[:, ti, :], scalar1=thrc[:, ti:ti + 1],
                                                                      scalar2=None, op0=ALU.is_ge), reads=["gm", "thrc"], writes=["Fsel"])
                        A("dve", lambda e: e.tensor_scalar(out=Fsel, in0=Fsel, scalar1=-1.0, scalar2=-NEGM, op0=ALU.add, op1=ALU.mult),
                          reads=["Fsel"], writes=["Fsel"])
                        A("dve", lambda e: e.tensor_tensor(out=Fsel, in0=Fsel, in1=dforce, op=ALU.add), reads=["Fsel", "dforce"], writes=["Fsel"])
                        A("dve", lambda e, h=h: e.tensor_scalar(out=FTin[:, :, 64:80], in0=Fsel, scalar1=0.0, scalar2=c31[:, h:h + 1],
                                                                op0=ALU.min, op1=ALU.add), reads=["Fsel", "c31"], writes=["FTin"])
                        for half in range(2):
                            for t8 in range(8):
                                ti = half * 8 + t8
                                A("pe", lambda e, ti=ti, t8=t8: e.transpose(out=psT[0:80, t8 * 128:(t8 + 1) * 128], in_=FTin[:, ti, :],
                                                                            identity=identb[:]), reads=["FTin", "identb"], writes=["psT"])
                            A("dve", lambda e, half=half, hh=hh: e.tensor_copy(out=Qaug[hh][64:80, half * 1024:(half + 1) * 1024], in_=psT[64:80, :]),
                              reads=["psT"], writes=[("Qaug", hh)])
                            yield
                            yield
                gens_ = [gen_vg(), gen_gate()]
                while gens_:
                    for gen_ in list(gens_):
                        try:
                            next(gen_)
                        except StopIteration:
                            gens_.remove(gen_)
                for hh in range(int(os.environ.get('DBG_NHH', '2'))):
                    h = p * 2 + hh
                    orow = hh * 64
                    drow = 64 - hh * 64
                    slots_all = []
                    for qt in range(int(os.environ.get('DBG_NQT', '4'))):
                        l0 = 2 * qt
                        slots = list(range(8)) + [8 + m for m in range(l0 + 2)]
                        for si, n in enumerate(slots):
                            last = (n == 8 + l0 + 1)
                            c0 = 256 if last else 0
                            kind_l = kind_r = "far"
                            if n == 8 + l0 + 1:
                                kind_l, kind_r = "skip", "diag"
                            elif n == 8 + l0:
                                kind_l, kind_r = "diag", "prev"
                            elif n == 8 + l0 - 1:
                                kind_l = "prev"
                            gi = len(slots_all)
                            slots_all.append(dict(qt=qt, si=si, n=n, nslot=len(slots), c0=c0, NQ=512 - c0, q0=qt * 512 + c0,
                                                  kl=kind_l, kr=kind_r, b=gi % 3, bt=gi % 3))

                    psS3 = [psS[0], psS[1], psAB]
                    psK3 = [["psS0"], ["psS1"], ["psA", "psB"]]

                    def emit_S(sl):
                        pS, pk = psS3[sl["b"]], psK3[sl["b"]]
                        n, c0, q0, NQ = sl["n"], sl["c0"], sl["q0"], sl["NQ"]
                        for kt in range(2):
                            A("pe", lambda e, kt=kt, pS=pS, n=n, q0=q0, NQ=NQ, c0=c0, hh=hh: e.matmul(
                                pS[:, kt * 512 + c0:kt * 512 + 512], lhsT=Kaug[hh][0:80, n * 256 + kt * 128:n * 256 + (kt + 1) * 128],
                                rhs=Qaug[hh][0:80, q0:q0 + NQ], start=True, stop=True),
                              reads=[("Kaug", hh), ("Koh", hh), ("Qaug", hh)], writes=pk)

                    def emit_exp(sl):
                        pS, pk = psS3[sl["b"]], psK3[sl["b"]]
                        pt, ptk = ptm[sl["bt"]], ("ptm", sl["bt"])
                        pS3 = pS[:].rearrange("p (a b) -> p a b", a=2)
                        if sl["kl"] == "far" and sl["kr"] == "far":
                            A("act", lambda e, pS=pS, pt=pt: e.activation(out=pt.rearrange("p a b -> p (a b)"), in_=pS[:], func=AF.Exp),
                              reads=pk, writes=[ptk])
                            return
                        for (kind, cs) in ((sl["kl"], 0), (sl["kr"], 256)):
                            if kind == "skip":
                                continue
                            if kind == "far":
                                A("act", lambda e, pS3=pS3, pt=pt, cs=cs: e.activation(out=pt[:, :, cs:cs + 256], in_=pS3[:, :, cs:cs + 256],
                                                                                       func=AF.Exp), reads=pk, writes=[ptk])
                                continue
                            bt, bk = (bD[hh], ("bD", hh)) if kind == "diag" else (bP[hh], ("bP", hh))
                            A("dve", lambda e, pS3=pS3, cs=cs, bt=bt: e.tensor_tensor(out=sbias[:, :, cs:cs + 256], in0=pS3[:, :, cs:cs + 256],
                                                                                      in1=bt, op=ALU.add), reads=pk + [bk], writes=[("sbias", cs)])
                            A("act", lambda e, pt=pt, cs=cs: e.activation(out=pt[:, :, cs:cs + 256], in_=sbias[:, :, cs:cs + 256], func=AF.Exp),
                              reads=[("sbias", cs)], writes=[ptk])

                    def emit_PV(sl):
                        pt, ptk = ptm[sl["bt"]], ("ptm", sl["bt"])
                        n, c0, si, nslot, qt = sl["n"], sl["c0"], sl["si"], sl["nslot"], sl["qt"]
                        for kt in range(2):
                            A("pe", lambda e, kt=kt, pt=pt, n=n, c0=c0, si=si, nslot=nslot, hh=hh: e.matmul(
                                psO[:, c0:512], lhsT=V3[:, n * 2 + kt, hh * 64:hh * 64 + 128], rhs=pt[:, kt, c0:512],
                                start=(si == 0 and kt == 0), stop=(si == nslot - 1 and kt == 1)),
                              reads=["V3", "V3ones", ptk], writes=["psO"])
                        if si != nslot - 1:
                            return
                        A("dve", lambda e, orow=orow, drow=drow: e.reciprocal(out=recm[orow:orow + 64, :], in_=psO[drow:drow + 64, :]),
                          reads=["psO"], writes=["recm"])
                        A("dve", lambda e, orow=orow: e.tensor_tensor(out=t1m[orow:orow + 64, :], in0=psO[orow:orow + 64, :],
                                                                      in1=recm[orow:orow + 64, :], op=ALU.mult), reads=["psO", "recm"], writes=["t1m"])
                        A("pool", lambda e, orow=orow, qt=qt, p=p: e.tensor_tensor(
                            out=mixT[orow:orow + 64, p, qt * 512:(qt + 1) * 512], in0=t1m[orow:orow + 64, :],
                            in1=gaT[orow:orow + 64, qt * 512:(qt + 1) * 512], op=ALU.mult), reads=["t1m", "gaT"], writes=[("mixT", p)])

                    for sl in slots_all[0:3]:
                        emit_S(sl)
                    for gi, sl in enumerate(slots_all):
                        emit_exp(sl)
                        if gi + 3 < len(slots_all):
                            emit_S(slots_all[gi + 3])
                        emit_PV(sl)

        def phase5():
            scr_reset()
            wo = carve([128, 12, 1024], BF16)
            fnw = carve([128, 1024], F32)
            xs5 = [carve([128, 2, 1024], F32) for _ in range(2)]
            hres = carve([128, 1024], F32)
            junk5 = carve([128, 1024], BF16)
            ot = [carve([128, 2, 1024], F32) for _ in range(2)]
            dma(fnw, fnw_d, writes=["fnw"])
            for cc in range(4):
                load_w(w_out, [(cc * 256, 256)], None, kc0=0, nkc=8, dst=wo[:, 0:8, cc * 256:(cc + 1) * 256], dkey="wo")
                load_w(w_out, [(cc * 256, 256)], None, kc0=8, nkc=4, dst=wo[:, 8:12, cc * 256:(cc + 1) * 256], dkey="wo")
            mix_keys = [("mixT", k) for k in range(12)]
            finals = []
            for ip in range(8):
                bx = ip % 2
                dma(xs5[bx], xall[T_OWN + ip * 256:T_OWN + (ip + 1) * 256, :].rearrange("(t p) d -> p t d", p=128), writes=[("xs5", bx)])
                for t in range(2):
                    i = ip * 2 + t
                    for half in range(2):
                        pst, pk = (psA, "psA") if half == 0 else (psB, "psB")
                        for kt in range(12):
                            A("pe", lambda e, kt=kt, half=half, pst=pst, i=i: e.matmul(
                                pst[:], lhsT=mixT[:, kt, i * 128:(i + 1) * 128], rhs=wo[:, kt, half * 512:(half + 1) * 512],
                                start=(kt == 0), stop=(kt == 11)), reads=mix_keys + ["wo"], writes=[pk])
                        A("dve", lambda e, half=half, pst=pst, bx=bx, t=t: e.tensor_tensor(
                            out=hres[:, half * 512:(half + 1) * 512], in0=pst[:], in1=xs5[bx][:, t, half * 512:(half + 1) * 512], op=ALU.add),
                          reads=[pk, ("xs5", bx)], writes=["hres"])
                    col = i
                    A("act", lambda e, col=col: e.activation(out=junk5, in_=hres, func=AF.Square, accum_out=stat[:, 0, col:col + 1]),
                      reads=["hres"], writes=["junk5", ("ss", col)])
                    A("act", lambda e, col=col: e.activation(out=stat[:, 1, col:col + 1], in_=stat[:, 0, col:col + 1], func=AF.Sqrt,
                                                             scale=1.0 / 1024, bias=epsc[:]), reads=[("ss", col), "epsc"], writes=[("sd", col)])
                    A("dve", lambda e, col=col: e.reciprocal(out=stat[:, 2, col:col + 1], in_=stat[:, 1, col:col + 1]),
                      reads=[("sd", col)], writes=[("rs", col)])
                    A("dve", lambda e, bx=bx, t=t, col=col: e.scalar_tensor_tensor(out=ot[bx][:, t, :], in0=hres, scalar=stat[:, 2, col:col + 1],
                                                                                  in1=fnw, op0=ALU.mult, op1=ALU.mult),
                      reads=["hres", ("rs", col), "fnw"], writes=[("ot", bx)])
                finals.append(dma(out_d[ip * 256:(ip + 1) * 256, :].rearrange("(t p) d -> p t d", p=128), ot[bx], reads=[("ot", bx)]))
            return finals

        def phase6():
            S.nolimit = True
            scr_reset()
            stg = [carve([128, 512], F32) for _ in range(2)]
            fin = []
            for kt in range(4, 12):
                for c in range(4):
                    k = (kt * 4 + c) % 2
                    A("dve", lambda e, kt=kt, c=c, k=k: e.tensor_copy(out=stg[k], in_=mixT[:, kt, c * 512:(c + 1) * 512]),
                      reads=[("mixT", kt)], writes=[("stg", k)])
                    fin.append(dma(mixo_d[:, (kt - 4) * T_OWN + c * 512:(kt - 4) * T_OWN + (c + 1) * 512], stg[k], reads=[("stg", k)]))
            return fin

        def phase7():
            scr_reset()
            stg = [carve([128, 512], F32) for _ in range(2)]
            for kt in range(4, 12):
                for c in range(4):
                    k = (kt * 4 + c) % 2
                    dma(stg[k], mixi_d[:, (kt - 4) * T_OWN + c * 512:(kt - 4) * T_OWN + (c + 1) * 512], writes=[("stg", k)])
                    A("dve", lambda e, kt=kt, c=c, k=k: e.tensor_copy(out=mixT[:, kt, c * 512:(c + 1) * 512], in_=stg[k]),
                      reads=[("stg", k)], writes=[("mixT", kt)])

        def phase8():
            for k in range(int(os.environ.get('DBG_PAD', '1000'))):
                if os.environ.get('DBG_PADENG', 'pool') == 'pe':
                    A("pe", lambda e: e.transpose(out=psT[:, 0:128], in_=identb[:], identity=identb[:]), writes=[("padjunk", k)])
                elif os.environ.get('DBG_PADENG', 'pool') == 'act':
                    A("act", lambda e: e.activation(out=stat[:, 2, 39:40], in_=stat[:, 2, 38:39], func=AF.Copy), writes=[("padjunk", k)])
                else:
                    A(os.environ.get('DBG_PADENG', 'pool'), lambda e: e.memset(stat[:, 2, 39:40], 0.0), writes=[("padjunk", k)])

        fns = {2: phase2, 3: phase3, 4: phase4, 5: phase5, 6: phase6, 7: phase7, 8: phase8}
        finals = None
        for ph in phases:
            if ph in fns:
                r_ = fns[ph]()
                if ph in (5, 6):
                    finals = r_
        S.emit(final_waits=finals)
    return nc


def _t5_bucket(rel):
    n = np.maximum(rel, 0).astype(np.int32)
    is_small = n < 16
    nf = np.maximum(n, 16).astype(np.float32)
    large = 16 + (np.log(nf / np.float32(16.0)) / np.float32(math.log(128 / 16)) * np.float32(16)).astype(np.int32)
    large = np.minimum(large, 31)
    return np.where(is_small, n, large)


_NC_CACHE = {}


def _prep(x, mem, norm_w, w_in, w_alpha2, b_alpha, gla_norm_w, mem_norm_w, w_mem_kv, w_out, rel_bias, final_norm_w):
    f32 = np.float32
    x = np.asarray(x, f32)
    mem = np.asarray(mem, f32)
    rel_bias = np.asarray(rel_bias, f32)

    p = np.arange(128)[:, None, None]
    kt = np.arange(2)[None, :, None]
    q = np.arange(256)[None, None, :]
    kloc = kt * 128 + p
    relD = q - kloc
    bktD = _t5_bucket(relD)
    bktP = _t5_bucket(256 + relD)
    biasD = np.ascontiguousarray(np.transpose(rel_bias[bktD], (3, 0, 1, 2))).reshape(8, 128, 512).astype(f32)
    biasP = np.ascontiguousarray(np.transpose(rel_bias[bktP], (3, 0, 1, 2))).reshape(8, 128, 512).astype(f32)
    diagM = np.where(relD >= 0, 0.0, NEGM).astype(f32).reshape(128, 512)
    c31 = np.ascontiguousarray(np.broadcast_to(rel_bias[31][None, :], (128, 8))).astype(f32)
    ident = np.eye(128, dtype=f32)
    onehot = (np.arange(T_ALL)[None, :] // 256 == np.arange(16)[:, None]).astype(f32)
    s_ = np.arange(128)[:, None]
    t_ = np.arange(128)[None, :]
    tri8 = np.tile((s_ <= t_).astype(f32), (1, 8))
    rmask = np.tile((np.arange(512) % 128 != 0).astype(f32)[None, :], (128, 1))
    common = {
        "w_in": np.ascontiguousarray(np.asarray(w_in, f32)[0]),
        "w_mem": np.ascontiguousarray(np.asarray(w_mem_kv, f32)[0]),
        "w_out": np.ascontiguousarray(np.asarray(w_out, f32)[0]),
        "nwc": np.ascontiguousarray(np.asarray(norm_w, f32)[0].reshape(8, 128).T),
        "mnwc": np.ascontiguousarray(np.asarray(mem_norm_w, f32)[0].reshape(8, 128).T),
        "fnw": np.ascontiguousarray(np.broadcast_to(np.asarray(final_norm_w, f32)[None, :], (128, 1024))),
        "balc": np.ascontiguousarray(np.asarray(b_alpha, f32)[0].reshape(2, 128).T),
        "wa2": np.ascontiguousarray(np.asarray(w_alpha2, f32)[0]),
        "gnwc": np.ascontiguousarray(np.asarray(gla_norm_w, f32)[0].reshape(128, 1)),
        "biasDP": np.ascontiguousarray(np.concatenate([biasD, biasP], axis=2)), "c31": c31, "ident": ident, "diagM": diagM,
        "onehot": onehot, "tri8": tri8, "rmask": rmask,
    }
    in_maps = []
    for core in range(8):
        b, j = core // 2, core % 2
        own = x[b, j * T_OWN:(j + 1) * T_OWN]
        prev = x[b, 0:T_OWN] if j == 1 else np.zeros((T_OWN, 1024), f32)
        gm = np.full((16, 16), -1e30, f32)
        df = np.zeros((16, 16), f32)
        for ti in range(16):
            l = ti // 2
            for n in range(16):
                valid = (n < 8 and j == 1) or (n >= 8 and (n - 8) < l)
                if valid:
                    gm[ti, n] = 0.0
            df[ti, 8 + l] = -NEGM
        m = dict(common)
        m["xall"] = np.ascontiguousarray(np.concatenate([prev, own], axis=0))
        m["memb"] = np.ascontiguousarray(mem[b])
        m["gmask"] = np.ascontiguousarray(np.broadcast_to(gm.reshape(1, 256), (128, 256)))
        m["dforce"] = np.ascontiguousarray(np.broadcast_to(df.reshape(1, 256), (128, 256)))
        in_maps.append(m)
    return in_maps


def kernel(x, mem, norm_w, w_in, w_alpha2, b_alpha, gla_norm_w, mem_norm_w, w_mem_kv, w_out, rel_bias, final_norm_w):
    f32 = np.float32
    in_maps = _prep(x, mem, norm_w, w_in, w_alpha2, b_alpha, gla_norm_w, mem_norm_w, w_mem_kv, w_out, rel_bias, final_norm_w)
    if "nc" not in _NC_CACHE:
        _NC_CACHE["nc"] = build_program((1, 2, 3, 4, 5))
    res = run_bass_kernel_spmd(_NC_CACHE["nc"], in_maps, core_ids=list(range(8)))
    out = np.empty((4, 4096, 1024), f32)
    for core in range(8):
        b, j = core // 2, core % 2
        out[b, j * T_OWN:(j + 1) * T_OWN] = res.results[core]["out"]
    return out
```

```python
import math
import os
import contextlib
import numpy as np
import concourse.bass as bass
import concourse.mybir as mybir
from concourse.bass_utils import run_bass_kernel_spmd

F32 = mybir.dt.float32
BF16 = mybir.dt.bfloat16
ALU = mybir.AluOpType
AF = mybir.ActivationFunctionType
AX = mybir.AxisListType

COMPUTE = ("pe", "act", "dve", "pool")
N_DMA_SEMS = int(os.environ.get("DBG_NDS", "12"))
NEGM = -30000.0
NOP_AFTER_WAIT = int(os.environ.get('DBG_NAW', '1'))
NAW_SKIP = tuple(os.environ.get('DBG_NAWSKIP', 'pe,act,pool,sp,dve').split(','))
EPS = 1e-6


class _Op:
    __slots__ = ("eng", "fn", "deps", "idx", "signal", "count", "dsem", "is_dma", "epoch", "where", "waits", "snap")


class Sched:
    def __init__(self, nc):
        self.nc = nc
        self.ops = {e: [] for e in COMPUTE + ("sp",)}
        self.last_write = {}
        self.readers = {}
        self.dma_rr = 0
        self.dma_last = [None] * N_DMA_SEMS
        self.dma_count = [0] * N_DMA_SEMS
        self.all_ops = []
        self.pending = {}

    def barrier(self):
        lasts = [self.ops[e][-1] for e in self.ops if self.ops[e]]
        lasts += [d for d in self.dma_last if d is not None]
        for e in self.ops:
            self.pending[e] = list(lasts)

    def add(self, eng, fn, reads=(), writes=(), dma=False):
        lim = int(os.environ.get("DBG_MAXOPS", "0"))
        if lim and not getattr(self, "nolimit", False) and len(self.all_ops) >= lim:
            return None
        op = _Op()
        op.eng, op.fn, op.is_dma = eng, fn, dma
        op.signal, op.count, op.dsem = False, None, None
        deps = []
        if self.pending.get(eng):
            deps.extend(self.pending[eng])
            self.pending[eng] = None
        for r in reads:
            w = self.last_write.get(r)
            if w is not None:
                deps.append(w)
        for w_ in writes:
            w = self.last_write.get(w_)
            if w is not None:
                deps.append(w)
            deps.extend(self.readers.get(w_, ()))
        if dma:
            k = self.dma_rr
            self.dma_rr = (self.dma_rr + 1) % N_DMA_SEMS
            op.dsem = k
            if self.dma_last[k] is not None:
                deps.append(self.dma_last[k])
            self.dma_last[k] = op
            self.dma_count[k] += 1
            op.count = 16 * self.dma_count[k]
            op.signal = True
        op.deps = [d for d in deps if d is not op]
        op.idx = len(self.ops[eng])
        if os.environ.get("DBG_DUMP"):
            import sys as _sys
            f = _sys._getframe(1)
            while f.f_code.co_name in ("A", "dma", "add"):
                f = f.f_back
            op.where = "%s:%d" % (f.f_code.co_name, f.f_lineno)
        self.ops[eng].append(op)
        self.all_ops.append(op)
        for r in reads:
            self.readers.setdefault(r, []).append(op)
        for w_ in writes:
            self.last_write[w_] = op
            self.readers[w_] = []
        return op

    @staticmethod
    def _skip(d, op):
        if d.is_dma or op.is_dma or d.eng != op.eng:
            return False
        if d.eng == "pe":
            return True
        return d.idx < op.idx - 3

    def emit(self, final_waits=()):
        nc = self.nc
        for op in self.all_ops:
            for d in op.deps:
                if not d.is_dma and not self._skip(d, op):
                    d.signal = True
        for op in final_waits:
            op.signal = True
        EPOCH = 2000
        nep = {}
        for e in self.ops:
            c = 0
            for op in self.ops[e]:
                if not op.is_dma and op.signal:
                    op.epoch = c // EPOCH
                    op.count = c % EPOCH + 1
                    c += 1
            nep[e] = c // EPOCH + 1
        know = {e: {} for e in self.ops}
        for op in self.all_ops:
            K = know[op.eng]
            cand = {}
            for d in op.deps:
                if d.is_dma:
                    key = ("d", d.dsem)
                else:
                    if self._skip(d, op):
                        continue
                    key = ("c", d.eng, d.epoch)
                if key not in cand or d.count > cand[key].count:
                    cand[key] = d
            order = sorted(cand.items(), key=lambda kv: -len(kv[1].snap))
            op.waits = []
            for key, d in order:
                if K.get(key, 0) >= d.count:
                    continue
                op.waits.append((key, d.count))
                K[key] = d.count
                for k2, v2 in d.snap.items():
                    if v2 > K.get(k2, 0):
                        K[k2] = v2
            op.snap = dict(K)
        with contextlib.ExitStack() as st:
            dsems = [st.enter_context(nc.semaphore("d_%d" % i)) for i in range(N_DMA_SEMS)]
            sems = {(e, k): st.enter_context(nc.semaphore("s_%s_%d" % (e, k))) for e in self.ops for k in range(nep[e])}
            block = st.enter_context(nc.Block())

            def run(e, handle):
                seen = {}
                for _ in range(int(os.environ.get("DBG_NOP_" + e.upper(), "0"))):
                    handle.engine_nop()
                for op in self.ops[e]:
                    wl = op.waits
                    for key, val in wl:
                        handle.wait_ge(dsems[key[1]] if key[0] == "d" else sems[(key[1], key[2])], val)
                    if wl and NOP_AFTER_WAIT and e not in NAW_SKIP:
                        handle.nop()
                    if os.environ.get("DBG_DUMP"):
                        with open(os.environ["DBG_DUMP"], "a") as fh:
                            fh.write("%s %d %s waits=%s sig=%s\n" % (e, op.idx, op.where, wl,
                                     (("d", op.dsem, op.count) if op.is_dma else ((e, op.epoch, op.count) if op.signal else None))))
                    inst = op.fn(handle)
                    if op.is_dma:
                        inst.then_inc(dsems[op.dsem], 16)
                    elif op.signal:
                        inst.then_inc(sems[(e, op.epoch)], 1)
                if e == "sp":
                    for op in final_waits:
                        handle.wait_ge(dsems[op.dsem] if op.is_dma else sems[(op.eng, op.epoch)], op.count)

            block.sync(lambda h: run("sp", h))
            block.tensor(lambda h: run("pe", h))
            block.scalar(lambda h: run("act", h))
            block.vector(lambda h: run("dve", h))
            block.gpsimd(lambda h: run("pool", h))


QA, KA, VA, GA, QB, KB, VB, GB, ZB, QC, GC = 0, 512, 1024, 1536, 2048, 2304, 2560, 3072, 3584, 3600, 4112
T_OWN = 2048
T_ALL = 4096
SCR_BYTES = 73 * 1024


def build_program(phases=(1, 4, 2, 3, 5), mode=None):
    nc = bass.Bass("TRN2", target_bir_lowering=False)

    def din(name, shape):
        return nc.dram_tensor(name, list(shape), F32, kind="ExternalInput").ap()

    xall = din("xall", [T_ALL, 1024])
    memb = din("memb", [256, 1024])
    w_in = din("w_in", [1024, 4624])
    w_mem = din("w_mem", [1024, 1024])
    w_out = din("w_out", [1536, 1024])
    nwc_d = din("nwc", [128, 8])
    mnwc_d = din("mnwc", [128, 8])
    fnw_d = din("fnw", [128, 1024])
    balc_d = din("balc", [128, 2])
    wa2_d = din("wa2", [16, 256])
    gnwc_d = din("gnwc", [128, 1])
    biasDP_d = din("biasDP", [8, 128, 1024])
    c31_d = din("c31", [128, 8])
    ident_d = din("ident", [128, 128])
    diagM_d = din("diagM", [128, 512])
    gmask_d = din("gmask", [128, 256])
    dforce_d = din("dforce", [128, 256])
    onehot_d = din("onehot", [16, T_ALL])
    tri8_d = din("tri8", [128, 1024])
    rmask_d = din("rmask", [128, 512])
    out_d = nc.dram_tensor("out", [T_OWN, 1024], F32, kind="ExternalOutput").ap() if mode != "A" else None
    mixo_d = nc.dram_tensor("mixo", [128, 8 * T_OWN], F32, kind="ExternalOutput").ap() if mode == "A" else None
    mixi_d = din("mixi", [128, 8 * T_OWN]) if mode == "B" else None

    with contextlib.ExitStack() as st:
        def sb(name, shape, dt):
            return st.enter_context(nc.sbuf_tensor(name, list(shape), dt))

        def ps(name, shape, dt):
            return st.enter_context(nc.psum_tensor(name, list(shape), dt))

        uT = sb("uT", [128, 8, T_ALL], BF16)
        mixT = sb("mixT", [128, 12, T_OWN], BF16)
        wst = sb("wst", [128, 8, 256], F32)
        wb = [sb("wb0", [128, 8, 256], BF16), sb("wb1", [128, 8, 256], BF16)]
        identb = sb("identb", [128, 128], BF16)
        onesb = sb("onesb", [128, 128], BF16)
        onesf = sb("onesf", [128, 128], F32)
        nwc = sb("nwc_s", [128, 8], F32)
        mnwc = sb("mnwc_s", [128, 8], F32)
        c31 = sb("c31_s", [128, 8], F32)
        balc = sb("balc_s", [128, 2], F32)
        nbal = sb("nbal_s", [128, 2], F32)
        wa2 = sb("wa2_s", [16, 256], F32)
        gnwc = sb("gnwc_s", [128, 1], F32)
        epsc = sb("epsc", [128, 1], F32)
        kmTx = sb("kmTx", [128, 4, 256], BF16)
        vmx = sb("vmx", [128, 2, 512], BF16)
        stat = sb("stat", [128, 3, 40], F32)
        scr = sb("scr", [128, SCR_BYTES // 4], F32)

        psAB = ps("psAB", [128, 1024], F32)
        psA = psAB[:, 0:512]
        psB = psAB[:, 512:1024]
        psS = [ps("psS0", [128, 1024], F32), ps("psS1", [128, 1024], F32)]
        psO = ps("psO", [128, 512], F32)
        psT = ps("psT", [128, 1024], BF16)
        psO_bf = psO[:].bitcast(BF16)

        S = Sched(nc)
        scr_off = [0]

        def scr_reset():
            S.barrier()
            scr_off[0] = 0

        def carve(shape, dt):
            n = 1
            for s_ in shape[1:]:
                n *= s_
            nbytes = n * (2 if dt == BF16 else 4)
            nbytes = (nbytes + 63) // 64 * 64
            o = scr_off[0]
            assert o + nbytes <= SCR_BYTES, ("scratch overflow", o, nbytes)
            scr_off[0] = o + nbytes
            v = scr[:, o // 4:(o + nbytes) // 4]
            if dt == BF16:
                v = v.bitcast(BF16)
            v = v[:, 0:n]
            if len(shape) == 3:
                v = v.rearrange("p (a b) -> p a b", a=shape[1])
            elif len(shape) == 4:
                v = v.rearrange("p (a b c) -> p a b c", a=shape[1], b=shape[2])
            if shape[0] < 128:
                v = v[0:shape[0]]
            return v

        def dma(out, in_, reads=(), writes=(), eng="sp"):
            return S.add(eng, lambda e, o=out, i=in_: e.dma_start(out=o, in_=i), reads=reads, writes=writes, dma=True)

        def A(eng, fn, reads=(), writes=()):
            return S.add(eng, fn, reads=reads, writes=writes)

        identf = scr[:, 0:128]
        dma(identf, ident_d, writes=["identf"])
        A("dve", lambda e: e.tensor_copy(out=identb[:], in_=identf), reads=["identf"], writes=["identb"])
        A("dve", lambda e: e.memset(onesb[:], 1.0), writes=["onesb"])
        A("dve", lambda e: e.memset(onesf[:], 1.0), writes=["onesf"])
        A("dve", lambda e: e.memset(epsc[:], EPS), writes=["epsc"])
        dma(nwc[:], nwc_d, writes=["nwc"])
        dma(mnwc[:], mnwc_d, writes=["mnwc"])
        dma(c31[:], c31_d, writes=["c31"])
        dma(balc[:], balc_d, writes=["balc"])
        dma(wa2[:], wa2_d, writes=["wa2"])
        dma(gnwc[:], gnwc_d, writes=["gnwc"])
        A("dve", lambda e: e.tensor_scalar(out=nbal[:], in0=balc[:], scalar1=-1.0, scalar2=None, op0=ALU.mult),
          reads=["balc"], writes=["nbal"])

        wcount = [0]

        def load_w(dram, pieces, scale, kc0=0, nkc=8, dst=None, dkey=None):
            if dst is None:
                k = wcount[0] % 2
                wcount[0] += 1
                dst, dkey = wb[k], "wb%d" % k
            off = 0
            for (c0, W) in pieces:
                src = dram[kc0 * 128:(kc0 + nkc) * 128, c0:c0 + W].rearrange("(kc p) c -> p kc c", p=128)
                dma(wst[:, 0:nkc, off:off + W], src, writes=["wst"])
                off += W
            for kc in range(nkc):
                if scale is not None:
                    A("dve", lambda e, kc=kc, off=off: e.tensor_scalar(
                        out=dst[:, kc, 0:off], in0=wst[:, kc, 0:off], scalar1=scale[:, kc0 + kc:kc0 + kc + 1],
                        scalar2=None, op0=ALU.mult), reads=["wst", "nwc", "mnwc"], writes=[dkey])
                else:
                    A("dve", lambda e, kc=kc, off=off: e.tensor_copy(out=dst[:, kc, 0:off], in_=wst[:, kc, 0:off]),
                      reads=["wst"], writes=[dkey])
            return dst, dkey

        ntile = [0]

        pend_B = []

        def norm_transpose(src2, dstTs, dkey, xs, ub, junk):
            ip = ntile[0] // 2
            bx = ip % len(xs)
            dma(xs[bx], src2.rearrange("(t p) d -> p t d", p=128), writes=[("xs", bx)])
            for t in range(2):
                partB = norm_partA(xs[bx][:, t, :], ("xs", bx), dstTs[t], dkey, ub, junk)
                if len(pend_B) >= 2:
                    pend_B.pop(0)()
                pend_B.append(partB)

        def norm_flush():
            while pend_B:
                pend_B.pop(0)()

        def norm_partA(xt, xkey, dstT, dkey, ub, junk):
            i = ntile[0]
            ntile[0] += 1
            b2 = i % len(ub)
            col = i % 40
            pT, pTk = (psT, "psT") if i % 2 == 0 else (psO_bf, "psO")
            A("act", lambda e: e.activation(out=junk, in_=xt, func=AF.Square, accum_out=stat[:, 0, col:col + 1]),
              reads=[xkey], writes=["junk", ("ss", col)])
            A("act", lambda e: e.activation(out=stat[:, 1, col:col + 1], in_=stat[:, 0, col:col + 1], func=AF.Sqrt,
                                            scale=1.0 / 1024, bias=epsc[:]), reads=[("ss", col), "epsc"], writes=[("sd", col)])
            A("dve", lambda e: e.reciprocal(out=stat[:, 2, col:col + 1], in_=stat[:, 1, col:col + 1]),
              reads=[("sd", col)], writes=[("rs", col)])
            A("dve", lambda e: e.tensor_scalar(out=ub[b2], in0=xt, scalar1=stat[:, 2, col:col + 1], scalar2=None,
                                               op0=ALU.mult), reads=[xkey, ("rs", col)], writes=[("ub", b2)])

            def partB():
                for kc in range(8):
                    A("pe", lambda e, kc=kc: e.transpose(out=pT[:, kc * 128:(kc + 1) * 128], in_=ub[b2][:, kc * 128:(kc + 1) * 128],
                                                         identity=identb[:]), reads=[("ub", b2), "identb"], writes=[pTk])
                src = pT[:, 0:1024].rearrange("p (a b) -> p a b", a=8)
                if i % 2 == 0:
                    A("act", lambda e: e.activation(out=dstT, in_=src, func=AF.Copy), reads=[pTk], writes=[dkey])
                else:
                    A("dve", lambda e: e.tensor_copy(out=dstT, in_=src), reads=[pTk], writes=[dkey])
            return partB

        def proj_F(wt, wkey, coff, M, srcT, skey, t0, N, pst, pkey):
            for kc in range(8):
                A("pe", lambda e, kc=kc: e.matmul(pst, lhsT=wt[:, kc, coff:coff + M], rhs=srcT[:, kc, t0:t0 + N],
                                                  start=(kc == 0), stop=(kc == 7)), reads=[wkey, skey], writes=[pkey])

        def proj_T(wt, wkey, coff, NC, srcT, skey, t0, pst, pkey):
            for kc in range(8):
                A("pe", lambda e, kc=kc: e.matmul(pst, lhsT=srcT[:, kc, t0:t0 + 128], rhs=wt[:, kc, coff:coff + NC],
                                                  start=(kc == 0), stop=(kc == 7)), reads=[wkey, skey], writes=[pkey])

        scr_reset()
        _ = carve([128, 128], F32)
        xs = [carve([128, 2, 1024], F32) for _ in range(4)]
        ub = [carve([128, 1024], BF16) for _ in range(3)]
        junk = carve([128, 1024], BF16)
        mT = carve([128, 8, 256], BF16)
        norm_transpose(memb[0:256, :], [mT[:, :, 0:128], mT[:, :, 128:256]], "mT", xs, ub, junk)
        norm_flush()
        for half in range(2):
            for cc in range(2):
                wt, wk = load_w(w_mem, [(half * 512 + cc * 256, 256)], mnwc)
                if half == 0:
                    for c2 in range(2):
                        c = cc * 2 + c2
                        proj_F(wt, wk, c2 * 128, 128, mT, "mT", 0, 256, psA[:, 0:256], "psA")
                        A("dve", lambda e, c=c: e.tensor_copy(out=kmTx[:, c, :], in_=psA[:, 0:256]), reads=["psA"], writes=["kmTx"])
                else:
                    for mt in range(2):
                        proj_T(wt, wk, 0, 256, mT, "mT", mt * 128, psB[:, 0:256], "psB")
                        A("dve", lambda e, mt=mt, cc=cc: e.tensor_copy(out=vmx[:, mt, cc * 256:(cc + 1) * 256], in_=psB[:, 0:256]),
                          reads=["psB"], writes=["vmx"])
        for i in range(16):
            norm_transpose(xall[i * 256:(i + 1) * 256, :], [uT[:, :, i * 256:i * 256 + 128], uT[:, :, i * 256 + 128:(i + 1) * 256]],
                           "uT", xs, ub, junk)
        norm_flush()

        def phase2():
            scr_reset()
            qcT = carve([128, T_OWN], BF16)
            gcT = carve([128, T_OWN], BF16)
            ptx = [carve([128, 2, 512], BF16) for _ in range(2)]
            recx = carve([128, 512], F32)
            t1x = carve([128, 512], F32)
            XS = 1.0 / math.sqrt(128.0)
            for c in range(int(os.environ.get('DBG_NXH', '4'))):
                wt, wk = load_w(w_in, [(QC + c * 128, 128), (GC + c * 128, 128)], nwc)
                for g in range(int(os.environ.get('DBG_NXPG', '4'))):
                    if 'q' in os.environ.get('DBG_XPARTS', 'qg'):
                        proj_F(wt, wk, 0, 128, uT, "uT", T_OWN + g * 512, 512, psA[:], "psA")
                        A("dve", lambda e, g=g: e.tensor_copy(out=qcT[:, g * 512:(g + 1) * 512], in_=psA[:]), reads=["psA"], writes=[("qcT", g)])
                    if 'g' not in os.environ.get('DBG_XPARTS', 'qg'):
                        continue
                    proj_F(wt, wk, 128, 128, uT, "uT", T_OWN + g * 512, 512, psB[:], "psB")
                    A("act", lambda e, g=g: e.activation(out=gcT[:, g * 512:(g + 1) * 512], in_=psB[:], func=(AF.Copy if os.environ.get("DBG_NOSILU") else AF.Silu)),
                      reads=["psB"], writes=[("gcT", g)])
                for g in range(int(os.environ.get('DBG_NXG', '4'))):
                    pS = psS[g % 2]
                    pk = "psS%d" % (g % 2)
                    pt = ptx[g % 2]
                    ptk = ("ptx", g % 2)
                    for mt in range(2):
                        A("pe", lambda e, mt=mt, pS=pS, g=g, c=c: e.matmul(pS[:, mt * 512:(mt + 1) * 512], lhsT=kmTx[:, c, mt * 128:(mt + 1) * 128],
                                                                          rhs=qcT[:, g * 512:(g + 1) * 512], start=True, stop=True),
                          reads=["kmTx", ("qcT", g)], writes=[pk])
                    A("act", lambda e, pS=pS, pt=pt: e.activation(out=pt.rearrange("p a b -> p (a b)"), in_=pS[:], func=AF.Exp, scale=XS),
                      reads=[pk], writes=[ptk])
                    for mt in range(2):
                        A("pe", lambda e, mt=mt, pt=pt, c=c: e.matmul(psO[:], lhsT=vmx[:, mt, c * 128:(c + 1) * 128], rhs=pt[:, mt, :],
                                                                     start=(mt == 0), stop=(mt == 1)), reads=["vmx", ptk], writes=["psO"])
                    for mt in range(2):
                        A("pe", lambda e, mt=mt, pt=pt: e.matmul(psB[:], lhsT=onesb[:], rhs=pt[:, mt, :], start=(mt == 0), stop=(mt == 1)),
                          reads=["onesb", ptk], writes=["psB"])
                    A("dve", lambda e: e.reciprocal(out=recx, in_=psB[:]), reads=["psB"], writes=["recx"])
                    A("dve", lambda e: e.tensor_tensor(out=t1x, in0=psO[:], in1=recx, op=ALU.mult), reads=["psO", "recx"], writes=["t1x"])
                    A("pool", lambda e, g=g, c=c: e.tensor_tensor(out=mixT[:, 8 + c, g * 512:(g + 1) * 512], in0=t1x,
                                                                 in1=gcT[:, g * 512:(g + 1) * 512], op=ALU.mult),
                      reads=["t1x", ("gcT", g)], writes=[("mixT", 8 + c)])

        def phase3():
            scr_reset()
            tri8 = carve([128, 1024], BF16)
            rmask = carve([128, 512], F32)
            tri8f = carve([128, 1024], F32)
            dma(tri8f, tri8_d, writes=["tri8f"])
            A("dve", lambda e: e.tensor_copy(out=tri8, in_=tri8f), reads=["tri8f"], writes=["tri8"])
            dma(rmask, rmask_d, writes=["rmask"])
            scr_off[0] -= 4096
            S.barrier()
            bufs = []
            for pr in range(2):
                b = {}
                b["wg"] = carve([128, 8, 784], BF16)
                b["zt"] = carve([16, 512], F32)
                b["tmpA"] = carve([128, 512], F32)
                b["Bg"] = carve([128, 512], F32)
                b["tmpE"] = carve([128, 512], F32)
                b["sdg"] = carve([128, 512], F32)
                b["ktg"] = carve([128, 512], BF16)
                b["khg"] = carve([128, 512], BF16)
                b["qtg"] = carve([128, 512], BF16)
                b["dg"] = carve([128, 4], F32)
                b["khtok"] = carve([128, 4, 128], BF16)
                b["vbg"] = carve([128, 4, 256], BF16)
                b["Sbg"] = carve([128, 4, 128], BF16)
                b["Sst"] = carve([128, 128], F32)
                b["attm"] = carve([128, 2, 4, 128], BF16)
                b["gbg"] = carve([128, 512], BF16)
                bufs.append(b)
            wa2b = carve([16, 256], BF16)
            A("dve", lambda e: e.tensor_copy(out=wa2b, in_=wa2[0:16, :]), reads=["wa2"], writes=["wa2b"])
            for pr in range(2):
                wg = bufs[pr]["wg"]
                wk_ = ("wg", pr)
                load_w(w_in, [(QB + pr * 128, 128), (KB + pr * 128, 128)], nwc, dst=wg[:, :, 0:256], dkey=wk_)
                load_w(w_in, [(VB + pr * 256, 256)], nwc, dst=wg[:, :, 256:512], dkey=wk_)
                load_w(w_in, [(GB + pr * 256, 256)], nwc, dst=wg[:, :, 512:768], dkey=wk_)
                load_w(w_in, [(ZB, 16)], nwc, dst=wg[:, :, 768:784], dkey=wk_)
                A("dve", lambda e, pr=pr: e.memset(bufs[pr]["Sst"], 0.0), writes=[("Sst", pr)])

            def group_body(pr, g):
                b = bufs[pr]
                wg, zt, tmpA, Bg, tmpE, sdg = b["wg"], b["zt"], b["tmpA"], b["Bg"], b["tmpE"], b["sdg"]
                ktg, khg, qtg, dg, khtok, vbg, Sbg, Sst, attm, gbg = (b["ktg"], b["khg"], b["qtg"], b["dg"], b["khtok"], b["vbg"],
                                                                     b["Sbg"], b["Sst"], b["attm"], b["gbg"])

                def k(name):
                    return (name, pr)
                wk_ = k("wg")
                own = g >= 4
                t0 = g * 512
                zt = bufs[g % 2]["zt"].bitcast(BF16)[:, 0:512]
                zkey = ("ztS", g % 2)
                if pr == 0:
                    proj_F(wg, wk_, 768, 16, uT, "uT", t0, 512, psA[0:16, :], "psA")
                    A("act", lambda e: e.activation(out=zt, in_=psA[0:16, :], func=AF.Copy), reads=["psA"], writes=[zkey])
                yield
                A("pe", lambda e: e.matmul(psB[:], lhsT=wa2b[0:16, pr * 128:(pr + 1) * 128], rhs=zt, start=True, stop=True),
                  reads=["wa2b", zkey], writes=["psB"])
                A("act", lambda e: e.activation(out=tmpA, in_=psB[:], func=AF.Exp, scale=-1.0, bias=nbal[:, pr:pr + 1]),
                  reads=["psB", "nbal"], writes=[k("tmpA")])
                yield
                A("act", lambda e: e.activation(out=tmpA, in_=tmpA, func=AF.Ln, bias=1.0), reads=[k("tmpA")], writes=[k("tmpA")])
                A("dve", lambda e: e.tensor_tensor_scan(out=Bg, data0=rmask, data1=tmpA, initial=0.0, op0=ALU.mult, op1=ALU.add),
                  reads=[k("tmpA"), "rmask"], writes=[k("Bg")])
                yield
                proj_F(wg, wk_, 128, 128, uT, "uT", t0, 512, psA[:], "psA")
                A("act", lambda e: e.activation(out=tmpE, in_=Bg, func=AF.Exp, scale=1.0 / 16), reads=[k("Bg")], writes=[k("tmpE")])
                A("dve", lambda e: e.tensor_tensor(out=ktg, in0=psA[:], in1=tmpE, op=ALU.mult), reads=["psA", k("tmpE")], writes=[k("ktg")])
                A("act", lambda e: e.activation(out=dg, in_=Bg.rearrange("p (a b) -> p a b", a=4)[:, :, 127], func=AF.Exp, scale=-1.0 / 16),
                  reads=[k("Bg")], writes=[k("dg")])
                yield
                for ch in range(4):
                    A("dve", lambda e, ch=ch: e.tensor_scalar(out=khg[:, ch * 128:(ch + 1) * 128], in0=ktg[:, ch * 128:(ch + 1) * 128],
                                                              scalar1=dg[:, ch:ch + 1], scalar2=None, op0=ALU.mult),
                      reads=[k("ktg"), k("dg")], writes=[k("khg")])
                for ch in range(4):
                    A("pe", lambda e, ch=ch: e.transpose(out=psT[:, ch * 128:(ch + 1) * 128], in_=khg[:, ch * 128:(ch + 1) * 128],
                                                         identity=identb[:]), reads=[k("khg"), "identb"], writes=["psT"])
                A("dve", lambda e: e.tensor_copy(out=khtok, in_=psT[:, 0:512].rearrange("p (a b) -> p a b", a=4)),
                  reads=["psT"], writes=[k("khtok")])
                yield
                for ch in range(4):
                    proj_T(wg, wk_, 256, 256, uT, "uT", t0 + ch * 128, psS[0][:, ch * 256:(ch + 1) * 256], "psS0")
                A("act", lambda e: e.activation(out=vbg.rearrange("p a b -> p (a b)"), in_=psS[0][:], func=AF.Copy),
                  reads=["psS0"], writes=[k("vbg")])
                yield
                for ch in range(4):
                    A("pe", lambda e, ch=ch: e.matmul(psS[1][:, ch * 256:(ch + 1) * 256], lhsT=khtok[:, ch, :], rhs=vbg[:, ch, :],
                                                      start=True, stop=True), reads=[k("khtok"), k("vbg")], writes=["psS1"])
                for ch in range(4):
                    if own:
                        A("pool", lambda e, ch=ch: e.tensor_copy(out=Sbg[:, ch, :], in_=Sst), reads=[k("Sst")], writes=[k("Sbg")])
                    for hh in range(2):
                        r0 = hh * 64
                        A("dve", lambda e, ch=ch, hh=hh, r0=r0: e.scalar_tensor_tensor(
                            out=Sst[r0:r0 + 64, :], in0=Sst[r0:r0 + 64, :], scalar=dg[r0:r0 + 64, ch:ch + 1],
                            in1=psS[1][r0:r0 + 64, ch * 256 + hh * 128:ch * 256 + (hh + 1) * 128], op0=ALU.mult, op1=ALU.add),
                          reads=[k("Sst"), k("dg"), "psS1"], writes=[k("Sst")])
                yield
                if not own:
                    return
                proj_F(wg, wk_, 0, 128, uT, "uT", t0, 512, psA[:], "psA")
                A("act", lambda e: e.activation(out=tmpE, in_=Bg, func=AF.Exp, scale=-1.0 / 16), reads=[k("Bg")], writes=[k("tmpE")])
                A("dve", lambda e: e.scalar_tensor_tensor(out=qtg, in0=psA[:], scalar=0.125, in1=tmpE, op0=ALU.mult, op1=ALU.mult),
                  reads=["psA", k("tmpE")], writes=[k("qtg")])
                yield
                for hh in range(2):
                    r0 = hh * 64
                    for ch in range(4):
                        A("pe", lambda e, hh=hh, ch=ch, r0=r0: e.matmul(
                            psS[0][:, (hh * 4 + ch) * 128:(hh * 4 + ch + 1) * 128], lhsT=ktg[r0:r0 + 64, ch * 128:(ch + 1) * 128],
                            rhs=qtg[r0:r0 + 64, ch * 128:(ch + 1) * 128], start=True, stop=True),
                          reads=[k("ktg"), k("qtg")], writes=["psS0"])
                A("dve", lambda e: e.tensor_tensor(out=attm.rearrange("p a b c -> p (a b c)"), in0=psS[0][:], in1=tri8, op=ALU.mult),
                  reads=["psS0", "tri8"], writes=[k("attm")])
                yield
                for hh in range(2):
                    r0 = hh * 64
                    hd = pr * 2 + hh
                    for ch in range(4):
                        A("pe", lambda e, hh=hh, ch=ch, r0=r0: e.matmul(
                            psO[:, ch * 128:(ch + 1) * 128], lhsT=Sbg[r0:r0 + 64, ch, :], rhs=qtg[r0:r0 + 64, ch * 128:(ch + 1) * 128],
                            start=True, stop=False), reads=[k("Sbg"), k("qtg")], writes=["psO"])
                        A("pe", lambda e, hh=hh, ch=ch: e.matmul(
                            psO[:, ch * 128:(ch + 1) * 128], lhsT=vbg[:, ch, hh * 128:(hh + 1) * 128], rhs=attm[:, hh, ch, :],
                            start=False, stop=True), reads=[k("vbg"), k("attm")], writes=["psO"])
                    A("act", lambda e: e.activation(out=khg, in_=psO[:], func=AF.Square), reads=["psO"], writes=[k("khg")])
                    A("pe", lambda e: e.matmul(psB[:], lhsT=onesb[:], rhs=khg, start=True, stop=True), reads=["onesb", k("khg")], writes=["psB"])
                    A("act", lambda e: e.activation(out=sdg, in_=psB[:], func=AF.Sqrt, scale=1.0 / 128, bias=epsc[:]),
                      reads=["psB", "epsc"], writes=[k("sdg")])
                    A("dve", lambda e: e.reciprocal(out=sdg, in_=sdg), reads=[k("sdg")], writes=[k("sdg")])
                    A("dve", lambda e: e.scalar_tensor_tensor(out=tmpE, in0=psO[:], scalar=gnwc[:, 0:1], in1=sdg, op0=ALU.mult, op1=ALU.mult),
                      reads=["psO", "gnwc", k("sdg")], writes=[k("tmpE")])
                    proj_F(wg, wk_, 512 + hh * 128, 128, uT, "uT", t0, 512, psA[:], "psA")
                    A("act", lambda e: e.activation(out=gbg, in_=psA[:], func=AF.Silu), reads=["psA"], writes=[k("gbg")])
                    A("pool", lambda e, hd=hd: e.tensor_tensor(out=mixT[:, 4 + hd, (g - 4) * 512:(g - 3) * 512], in0=tmpE, in1=gbg, op=ALU.mult),
                      reads=[k("tmpE"), k("gbg")], writes=[("mixT", 4 + hd)])
                    yield

            for g in range(8):
                gens = [group_body(0, g), group_body(1, g)]
                while gens:
                    for gen in list(gens):
                        try:
                            next(gen)
                        except StopIteration:
                            gens.remove(gen)

        def phase4():
            scr_reset()
            Kaug = [carve([80, T_ALL], BF16) for _ in range(2)]
            Qaug = [carve([80, T_OWN], BF16) for _ in range(2)]
            V3 = carve([128, 32, 192], BF16)
            gaT = carve([128, T_OWN], BF16)
            ptm = [carve([128, 2, 512], BF16) for _ in range(3)]
            sbias = carve([128, 2, 512], F32)
            bDP = [carve([128, 4, 256], F32) for _ in range(2)]
            bD = [t_[:, 0:2, :] for t_ in bDP]
            bP = [t_[:, 2:4, :] for t_ in bDP]
            diagM = carve([128, 2, 256], F32)
            gmask = carve([128, 16, 16], F32)
            dforce = carve([128, 16, 16], F32)
            gm = carve([128, 16, 16], F32)
            m8 = carve([128, 16, 8], F32)
            thrc = carve([128, 16], F32)
            Fsel = carve([128, 16, 16], F32)
            FTin = carve([128, 16, 80], BF16)
            recm = carve([128, 512], F32)
            t1m = carve([128, 512], F32)
            ksum = carve([128, 16], F32)
            ksumB = carve([128, 16], F32)
            kmT = [carve([64, 16], BF16) for _ in range(2)]
            ohst = sbias.rearrange("p a b -> p (a b)")[0:80]
            dma(diagM.rearrange("p a b -> p (a b)"), diagM_d, writes=["diagM"])
            dma(gmask.rearrange("p a b -> p (a b)"), gmask_d, writes=["gmask"])
            dma(dforce.rearrange("p a b -> p (a b)"), dforce_d, writes=["dforce"])
            for q4 in range(4):
                dma(ohst[64:80, :], onehot_d[:, q4 * 1024:(q4 + 1) * 1024], writes=[("sbias", 0), ("sbias", 256)])
                for hh in range(2):
                    A("dve", lambda e, hh=hh, q4=q4: e.tensor_copy(out=Kaug[hh][64:80, q4 * 1024:(q4 + 1) * 1024], in_=ohst[64:80, :]),
                      reads=[("sbias", 0), ("sbias", 256)], writes=[("Koh", hh)])
            MS = os.environ.get("DBG_MSENG", "pool")
            A(MS, lambda e: e.memset(V3[:, :, 64:128], 1.0), writes=["V3ones"])
            A(MS, lambda e: e.memset(FTin.rearrange("p a b -> p (a b)"), 0.0), writes=["FTin"])
            for p in range(int(os.environ.get('DBG_NPAIR', '4'))):
                PARTS = os.environ.get('DBG_P4PARTS', 'kqvg')
                wt, wk = load_w(w_in, [(KA + p * 128, 128), (QA + p * 128, 128)], nwc)
                for g in range(8 if 'k' in PARTS else 0):
                    pst, pk = (psA, "psA") if g % 2 == 0 else (psB, "psB")
                    proj_F(wt, wk, 0, 128, uT, "uT", g * 512, 512, pst[:], pk)
                    for blk in range(2):
                        cs = slice(blk * 256, (blk + 1) * 256)
                        ts = slice(g * 512 + blk * 256, g * 512 + (blk + 1) * 256)
                        col = g * 2 + blk
                        A("act", lambda e, pst=pst, cs=cs, ts=ts, col=col: e.activation(
                            out=Kaug[0][0:64, ts], in_=pst[0:64, cs], func=AF.Copy, accum_out=ksum[0:64, col:col + 1]),
                          reads=[pk], writes=[("Kaug", 0), ("ksA", col)])
                        A("dve", lambda e, pst=pst, cs=cs, ts=ts, col=col: e.tensor_scalar(
                            out=Kaug[1][0:64, ts], in0=pst[64:128, cs], scalar1=1.0, scalar2=0.0, op0=ALU.mult, op1=ALU.add,
                            accum_out=ksumB[0:64, col:col + 1]), reads=[pk], writes=[("Kaug", 1), ("ksB", col)])
                A("dve", lambda e: e.tensor_scalar(out=kmT[0][:], in0=ksum[0:64, :], scalar1=1.0 / 256, scalar2=None, op0=ALU.mult),
                  reads=[("ksA", c_) for c_ in range(16)], writes=[("kmT", 0)])
                A("dve", lambda e: e.tensor_scalar(out=kmT[1][:], in0=ksumB[0:64, :], scalar1=1.0 / 256, scalar2=None, op0=ALU.mult),
                  reads=[("ksB", c_) for c_ in range(16)], writes=[("kmT", 1)])
                for g in range(4 if 'q' in PARTS else 0):
                    pst, pk = (psA, "psA") if g % 2 == 0 else (psB, "psB")
                    proj_F(wt, wk, 128, 128, uT, "uT", T_OWN + g * 512, 512, pst[:], pk)
                    A("act", lambda e, g=g, pst=pst: e.activation(out=Qaug[0][0:64, g * 512:(g + 1) * 512], in_=pst[0:64, :], func=AF.Copy, scale=0.125),
                      reads=[pk], writes=[("Qaug", 0)])
                    A("dve", lambda e, g=g, pst=pst: e.tensor_scalar(out=Qaug[1][0:64, g * 512:(g + 1) * 512], in0=pst[64:128, :], scalar1=0.125,
                                                                     scalar2=None, op0=ALU.mult), reads=[pk], writes=[("Qaug", 1)])
                wt, wk = load_w(w_in, [(VA + p * 128, 128), (GA + p * 128, 128)], nwc)
                def gen_vg():
                    for g4 in range(8 if 'v' in PARTS else 0):
                        pst, pk = (psA, "psA") if g4 % 2 == 0 else (psB, "psB")
                        for ti in range(4):
                            proj_T(wt, wk, 0, 128, uT, "uT", (g4 * 4 + ti) * 128, pst[:, ti * 128:(ti + 1) * 128], pk)
                        pv = pst[:].rearrange("p (a b) -> p a b", a=4)
                        A("act", lambda e, g4=g4, pv=pv: e.activation(out=V3[:, g4 * 4:(g4 + 1) * 4, 0:64], in_=pv[:, :, 0:64], func=AF.Copy),
                          reads=[pk], writes=["V3"])
                        A("dve", lambda e, g4=g4, pv=pv: e.tensor_copy(out=V3[:, g4 * 4:(g4 + 1) * 4, 128:192], in_=pv[:, :, 64:128]),
                          reads=[pk], writes=["V3"])
                        yield
                    for g in range(4 if 'g' in PARTS else 0):
                        pst, pk = (psA, "psA") if g % 2 == 0 else (psB, "psB")
                        proj_F(wt, wk, 128, 128, uT, "uT", T_OWN + g * 512, 512, pst[:], pk)
                        A("act", lambda e, g=g, pst=pst: e.activation(out=gaT[:, g * 512:(g + 1) * 512], in_=pst[:], func=AF.Silu),
                          reads=[pk], writes=["gaT"])
                        yield
                        yield
                def gen_gate():
                    for hh in range(int(os.environ.get('DBG_NHH', '2'))):
                        h = p * 2 + hh
                        dma(bDP[hh].rearrange("p a b -> p (a b)"), biasDP_d[h], writes=[("bD", hh), ("bP", hh)])
                        A("dve", lambda e, hh=hh, h=h: e.scalar_tensor_tensor(
                            out=bD[hh], in0=bD[hh], scalar=c31[:, h:h + 1], in1=diagM, op0=ALU.subtract, op1=ALU.add),
                          reads=[("bD", hh), "c31", "diagM"], writes=[("bD", hh)])
                        A("dve", lambda e, hh=hh, h=h: e.tensor_scalar(
                            out=bP[hh], in0=bP[hh], scalar1=c31[:, h:h + 1], scalar2=None, op0=ALU.subtract), reads=[("bP", hh), "c31"], writes=[("bP", hh)])
                        yield
                        gps = psS[0][:, 0:256].rearrange("p (a b) -> p a b", a=16)
                        for ti in range(16):
                            A("pe", lambda e, ti=ti, hh=hh: e.matmul(psS[0][:, ti * 16:(ti + 1) * 16], lhsT=Qaug[hh][0:64, ti * 128:(ti + 1) * 128],
                                                                     rhs=kmT[hh][:], start=True, stop=True),
                              reads=[("Qaug", hh), ("kmT", hh)], writes=["psS0"])
                        A("dve", lambda e: e.tensor_tensor(out=gm, in0=gps, in1=gmask, op=ALU.add), reads=["psS0", "gmask"], writes=["gm"])
                        for ti in range(16):
                            A("dve", lambda e, ti=ti: e.max(out=m8[:, ti, :], in_=gm[:, ti, :]), reads=["gm"], writes=["m8"])
                        A("dve", lambda e: e.tensor_scalar(out=thrc, in0=m8[:, :, 2], scalar1=-1e29, scalar2=None, op0=ALU.max),
                          reads=["m8"], writes=["thrc"])
                        yield
                        for ti in range(16):
                            A("dve", lambda e, ti=ti: e.tensor_scalar(out=Fsel[:, ti, :], in0=gm[:, ti, :], scalar1=thrc[:, ti:ti + 1],
                                                                      scalar2=None, op0=ALU.is_ge), reads=["gm", "thrc"], writes=["Fsel"])
                        A("dve", lambda e: e.tensor_scalar(out=Fsel, in0=Fsel, scalar1=-1.0, scalar2=-NEGM, op0=ALU.add, op1=ALU.mult),
                          reads=["Fsel"], writes=["Fsel"])
                        A("dve", lambda e: e.tensor_tensor(out=Fsel, in0=Fsel, in1=dforce, op=ALU.add), reads=["Fsel", "dforce"], writes=["Fsel"])
                        A("dve", lambda e, h=h: e.tensor_scalar(out=FTin[:, :, 64:80], in0=Fsel, scalar1=0.0, scalar2=c31[:, h:h + 1],
                                                                op0=ALU.min, op1=ALU.add), reads=["Fsel", "c31"], writes=["FTin"])
                        for half in range(2):
                            for t8 in range(8):
                                ti = half * 8 + t8
                                A("pe", lambda e, ti=ti, t8=t8: e.transpose(out=psT[0:80, t8 * 128:(t8 + 1) * 128], in_=FTin[:, ti, :],
                                                                            identity=identb[:]), reads=["FTin", "identb"], writes=["psT"])
                            A("dve", lambda e, half=half, hh=hh: e.tensor_copy(out=Qaug[hh][64:80, half * 1024:(half + 1) * 1024], in_=psT[64:80, :]),
                              reads=["psT"], writes=[("Qaug", hh)])
                            yield
                            yield
                gens_ = [gen_vg(), gen_gate()]
                while gens_:
                    for gen_ in list(gens_):
                        try:
                            next(gen_)
                        except StopIteration:
                            gens_.remove(gen_)
                for hh in range(int(os.environ.get('DBG_NHH', '2'))):
                    h = p * 2 + hh
                    orow = hh * 64
                    drow = 64 - hh * 64
                    slots_all = []
                    for qt in range(int(os.environ.get('DBG_NQT', '4'))):
                        l0 = 2 * qt
                        slots = list(range(8)) + [8 + m for m in range(l0 + 2)]
                        for si, n in enumerate(slots):
                            last = (n == 8 + l0 + 1)
                            c0 = 256 if last else 0
                            kind_l = kind_r = "far"
                            if n == 8 + l0 + 1:
                                kind_l, kind_r = "skip", "diag"
                            elif n == 8 + l0:
                                kind_l, kind_r = "diag", "prev"
                            elif n == 8 + l0 - 1:
                                kind_l = "prev"
                            gi = len(slots_all)
                            slots_all.append(dict(qt=qt, si=si, n=n, nslot=len(slots), c0=c0, NQ=512 - c0, q0=qt * 512 + c0,
                                                  kl=kind_l, kr=kind_r, b=gi % 3, bt=gi % 3))

                    psS3 = [psS[0], psS[1], psAB]
                    psK3 = [["psS0"], ["psS1"], ["psA", "psB"]]

                    def emit_S(sl):
                        pS, pk = psS3[sl["b"]], psK3[sl["b"]]
                        n, c0, q0, NQ = sl["n"], sl["c0"], sl["q0"], sl["NQ"]
                        for kt in range(2):
                            A("pe", lambda e, kt=kt, pS=pS, n=n, q0=q0, NQ=NQ, c0=c0, hh=hh: e.matmul(
                                pS[:, kt * 512 + c0:kt * 512 + 512], lhsT=Kaug[hh][0:80, n * 256 + kt * 128:n * 256 + (kt + 1) * 128],
                                rhs=Qaug[hh][0:80, q0:q0 + NQ], start=True, stop=True),
                              reads=[("Kaug", hh), ("Koh", hh), ("Qaug", hh)], writes=pk)

                    def emit_exp(sl):
                        pS, pk = psS3[sl["b"]], psK3[sl["b"]]
                        pt, ptk = ptm[sl["bt"]], ("ptm", sl["bt"])
                        pS3 = pS[:].rearrange("p (a b) -> p a b", a=2)
                        if sl["kl"] == "far" and sl["kr"] == "far":
                            A("act", lambda e, pS=pS, pt=pt: e.activation(out=pt.rearrange("p a b -> p (a b)"), in_=pS[:], func=AF.Exp),
                              reads=pk, writes=[ptk])
                            return
                        for (kind, cs) in ((sl["kl"], 0), (sl["kr"], 256)):
                            if kind == "skip":
                                continue
                            if kind == "far":
                                A("act", lambda e, pS3=pS3, pt=pt, cs=cs: e.activation(out=pt[:, :, cs:cs + 256], in_=pS3[:, :, cs:cs + 256],
                                                                                       func=AF.Exp), reads=pk, writes=[ptk])
                                continue
                            bt, bk = (bD[hh], ("bD", hh)) if kind == "diag" else (bP[hh], ("bP", hh))
                            A("dve", lambda e, pS3=pS3, cs=cs, bt=bt: e.tensor_tensor(out=sbias[:, :, cs:cs + 256], in0=pS3[:, :, cs:cs + 256],
                                                                                      in1=bt, op=ALU.add), reads=pk + [bk], writes=[("sbias", cs)])
                            A("act", lambda e, pt=pt, cs=cs: e.activation(out=pt[:, :, cs:cs + 256], in_=sbias[:, :, cs:cs + 256], func=AF.Exp),
                              reads=[("sbias", cs)], writes=[ptk])

                    def emit_PV(sl):
                        pt, ptk = ptm[sl["bt"]], ("ptm", sl["bt"])
                        n, c0, si, nslot, qt = sl["n"], sl["c0"], sl["si"], sl["nslot"], sl["qt"]
                        for kt in range(2):
                            A("pe", lambda e, kt=kt, pt=pt, n=n, c0=c0, si=si, nslot=nslot, hh=hh: e.matmul(
                                psO[:, c0:512], lhsT=V3[:, n * 2 + kt, hh * 64:hh * 64 + 128], rhs=pt[:, kt, c0:512],
                                start=(si == 0 and kt == 0), stop=(si == nslot - 1 and kt == 1)),
                              reads=["V3", "V3ones", ptk], writes=["psO"])
                        if si != nslot - 1:
                            return
                        A("dve", lambda e, orow=orow, drow=drow: e.reciprocal(out=recm[orow:orow + 64, :], in_=psO[drow:drow + 64, :]),
                          reads=["psO"], writes=["recm"])
                        A("dve", lambda e, orow=orow: e.tensor_tensor(out=t1m[orow:orow + 64, :], in0=psO[orow:orow + 64, :],
                                                                      in1=recm[orow:orow + 64, :], op=ALU.mult), reads=["psO", "recm"], writes=["t1m"])
                        A("pool", lambda e, orow=orow, qt=qt, p=p: e.tensor_tensor(
                            out=mixT[orow:orow + 64, p, qt * 512:(qt + 1) * 512], in0=t1m[orow:orow + 64, :],
                            in1=gaT[orow:orow + 64, qt * 512:(qt + 1) * 512], op=ALU.mult), reads=["t1m", "gaT"], writes=[("mixT", p)])

                    for sl in slots_all[0:3]:
                        emit_S(sl)
                    for gi, sl in enumerate(slots_all):
                        emit_exp(sl)
                        if gi + 3 < len(slots_all):
                            emit_S(slots_all[gi + 3])
                        emit_PV(sl)

        def phase5():
            scr_reset()
            wo = carve([128, 12, 1024], BF16)
            fnw = carve([128, 1024], F32)
            xs5 = [carve([128, 2, 1024], F32) for _ in range(2)]
            hres = carve([128, 1024], F32)
            junk5 = carve([128, 1024], BF16)
            ot = [carve([128, 2, 1024], F32) for _ in range(2)]
            dma(fnw, fnw_d, writes=["fnw"])
            for cc in range(4):
                load_w(w_out, [(cc * 256, 256)], None, kc0=0, nkc=8, dst=wo[:, 0:8, cc * 256:(cc + 1) * 256], dkey="wo")
                load_w(w_out, [(cc * 256, 256)], None, kc0=8, nkc=4, dst=wo[:, 8:12, cc * 256:(cc + 1) * 256], dkey="wo")
            mix_keys = [("mixT", k) for k in range(12)]
            finals = []
            for ip in range(8):
                bx = ip % 2
                dma(xs5[bx], xall[T_OWN + ip * 256:T_OWN + (ip + 1) * 256, :].rearrange("(t p) d -> p t d", p=128), writes=[("xs5", bx)])
                for t in range(2):
                    i = ip * 2 + t
                    for half in range(2):
                        pst, pk = (psA, "psA") if half == 0 else (psB, "psB")
                        for kt in range(12):
                            A("pe", lambda e, kt=kt, half=half, pst=pst, i=i: e.matmul(
                                pst[:], lhsT=mixT[:, kt, i * 128:(i + 1) * 128], rhs=wo[:, kt, half * 512:(half + 1) * 512],
                                start=(kt == 0), stop=(kt == 11)), reads=mix_keys + ["wo"], writes=[pk])
                        A("dve", lambda e, half=half, pst=pst, bx=bx, t=t: e.tensor_tensor(
                            out=hres[:, half * 512:(half + 1) * 512], in0=pst[:], in1=xs5[bx][:, t, half * 512:(half + 1) * 512], op=ALU.add),
                          reads=[pk, ("xs5", bx)], writes=["hres"])
                    col = i
                    A("act", lambda e, col=col: e.activation(out=junk5, in_=hres, func=AF.Square, accum_out=stat[:, 0, col:col + 1]),
                      reads=["hres"], writes=["junk5", ("ss", col)])
                    A("act", lambda e, col=col: e.activation(out=stat[:, 1, col:col + 1], in_=stat[:, 0, col:col + 1], func=AF.Sqrt,
                                                             scale=1.0 / 1024, bias=epsc[:]), reads=[("ss", col), "epsc"], writes=[("sd", col)])
                    A("dve", lambda e, col=col: e.reciprocal(out=stat[:, 2, col:col + 1], in_=stat[:, 1, col:col + 1]),
                      reads=[("sd", col)], writes=[("rs", col)])
                    A("dve", lambda e, bx=bx, t=t, col=col: e.scalar_tensor_tensor(out=ot[bx][:, t, :], in0=hres, scalar=stat[:, 2, col:col + 1],
                                                                                  in1=fnw, op0=ALU.mult, op1=ALU.mult),
                      reads=["hres", ("rs", col), "fnw"], writes=[("ot", bx)])
                finals.append(dma(out_d[ip * 256:(ip + 1) * 256, :].rearrange("(t p) d -> p t d", p=128), ot[bx], reads=[("ot", bx)]))
            return finals

        def phase6():
            S.nolimit = True
            scr_reset()
            stg = [carve([128, 512], F32) for _ in range(2)]
            fin = []
            for kt in range(4, 12):
                for c in range(4):
                    k = (kt * 4 + c) % 2
                    A("dve", lambda e, kt=kt, c=c, k=k: e.tensor_copy(out=stg[k], in_=mixT[:, kt, c * 512:(c + 1) * 512]),
                      reads=[("mixT", kt)], writes=[("stg", k)])
                    fin.append(dma(mixo_d[:, (kt - 4) * T_OWN + c * 512:(kt - 4) * T_OWN + (c + 1) * 512], stg[k], reads=[("stg", k)]))
            return fin

        def phase7():
            scr_reset()
            stg = [carve([128, 512], F32) for _ in range(2)]
            for kt in range(4, 12):
                for c in range(4):
                    k = (kt * 4 + c) % 2
                    dma(stg[k], mixi_d[:, (kt - 4) * T_OWN + c * 512:(kt - 4) * T_OWN + (c + 1) * 512], writes=[("stg", k)])
                    A("dve", lambda e, kt=kt, c=c, k=k: e.tensor_copy(out=mixT[:, kt, c * 512:(c + 1) * 512], in_=stg[k]),
                      reads=[("stg", k)], writes=[("mixT", kt)])

        def phase8():
            for k in range(int(os.environ.get('DBG_PAD', '1000'))):
                if os.environ.get('DBG_PADENG', 'pool') == 'pe':
                    A("pe", lambda e: e.transpose(out=psT[:, 0:128], in_=identb[:], identity=identb[:]), writes=[("padjunk", k)])
                elif os.environ.get('DBG_PADENG', 'pool') == 'act':
                    A("act", lambda e: e.activation(out=stat[:, 2, 39:40], in_=stat[:, 2, 38:39], func=AF.Copy), writes=[("padjunk", k)])
                else:
                    A(os.environ.get('DBG_PADENG', 'pool'), lambda e: e.memset(stat[:, 2, 39:40], 0.0), writes=[("padjunk", k)])

        fns = {2: phase2, 3: phase3, 4: phase4, 5: phase5, 6: phase6, 7: phase7, 8: phase8}
        finals = None
        for ph in phases:
            if ph in fns:
                r_ = fns[ph]()
                if ph in (5, 6):
                    finals = r_
        S.emit(final_waits=finals)
    return nc


def _t5_bucket(rel):
    n = np.maximum(rel, 0).astype(np.int32)
    is_small = n < 16
    nf = np.maximum(n, 16).astype(np.float32)
    large = 16 + (np.log(nf / np.float32(16.0)) / np.float32(math.log(128 / 16)) * np.float32(16)).astype(np.int32)
    large = np.minimum(large, 31)
    return np.where(is_small, n, large)


_NC_CACHE = {}


def _prep(x, mem, norm_w, w_in, w_alpha2, b_alpha, gla_norm_w, mem_norm_w, w_mem_kv, w_out, rel_bias, final_norm_w):
    f32 = np.float32
    x = np.asarray(x, f32)
    mem = np.asarray(mem, f32)
    rel_bias = np.asarray(rel_bias, f32)

    p = np.arange(128)[:, None, None]
    kt = np.arange(2)[None, :, None]
    q = np.arange(256)[None, None, :]
    kloc = kt * 128 + p
    relD = q - kloc
    bktD = _t5_bucket(relD)
    bktP = _t5_bucket(256 + relD)
    biasD = np.ascontiguousarray(np.transpose(rel_bias[bktD], (3, 0, 1, 2))).reshape(8, 128, 512).astype(f32)
    biasP = np.ascontiguousarray(np.transpose(rel_bias[bktP], (3, 0, 1, 2))).reshape(8, 128, 512).astype(f32)
    diagM = np.where(relD >= 0, 0.0, NEGM).astype(f32).reshape(128, 512)
    c31 = np.ascontiguousarray(np.broadcast_to(rel_bias[31][None, :], (128, 8))).astype(f32)
    ident = np.eye(128, dtype=f32)
    onehot = (np.arange(T_ALL)[None, :] // 256 == np.arange(16)[:, None]).astype(f32)
    s_ = np.arange(128)[:, None]
    t_ = np.arange(128)[None, :]
    tri8 = np.tile((s_ <= t_).astype(f32), (1, 8))
    rmask = np.tile((np.arange(512) % 128 != 0).astype(f32)[None, :], (128, 1))
    common = {
        "w_in": np.ascontiguousarray(np.asarray(w_in, f32)[0]),
        "w_mem": np.ascontiguousarray(np.asarray(w_mem_kv, f32)[0]),
        "w_out": np.ascontiguousarray(np.asarray(w_out, f32)[0]),
        "nwc": np.ascontiguousarray(np.asarray(norm_w, f32)[0].reshape(8, 128).T),
        "mnwc": np.ascontiguousarray(np.asarray(mem_norm_w, f32)[0].reshape(8, 128).T),
        "fnw": np.ascontiguousarray(np.broadcast_to(np.asarray(final_norm_w, f32)[None, :], (128, 1024))),
        "balc": np.ascontiguousarray(np.asarray(b_alpha, f32)[0].reshape(2, 128).T),
        "wa2": np.ascontiguousarray(np.asarray(w_alpha2, f32)[0]),
        "gnwc": np.ascontiguousarray(np.asarray(gla_norm_w, f32)[0].reshape(128, 1)),
        "biasDP": np.ascontiguousarray(np.concatenate([biasD, biasP], axis=2)), "c31": c31, "ident": ident, "diagM": diagM,
        "onehot": onehot, "tri8": tri8, "rmask": rmask,
    }
    in_maps = []
    for core in range(8):
        b, j = core // 2, core % 2
        own = x[b, j * T_OWN:(j + 1) * T_OWN]
        prev = x[b, 0:T_OWN] if j == 1 else np.zeros((T_OWN, 1024), f32)
        gm = np.full((16, 16), -1e30, f32)
        df = np.zeros((16, 16), f32)
        for ti in range(16):
            l = ti // 2
            for n in range(16):
                valid = (n < 8 and j == 1) or (n >= 8 and (n - 8) < l)
                if valid:
                    gm[ti, n] = 0.0
            df[ti, 8 + l] = -NEGM
        m = dict(common)
        m["xall"] = np.ascontiguousarray(np.concatenate([prev, own], axis=0))
        m["memb"] = np.ascontiguousarray(mem[b])
        m["gmask"] = np.ascontiguousarray(np.broadcast_to(gm.reshape(1, 256), (128, 256)))
        m["dforce"] = np.ascontiguousarray(np.broadcast_to(df.reshape(1, 256), (128, 256)))
        in_maps.append(m)
    return in_maps


def kernel(x, mem, norm_w, w_in, w_alpha2, b_alpha, gla_norm_w, mem_norm_w, w_mem_kv, w_out, rel_bias, final_norm_w):
    f32 = np.float32
    in_maps = _prep(x, mem, norm_w, w_in, w_alpha2, b_alpha, gla_norm_w, mem_norm_w, w_mem_kv, w_out, rel_bias, final_norm_w)
    if "nc" not in _NC_CACHE:
        _NC_CACHE["nc"] = build_program((1, 2, 3, 4, 5))
    res = run_bass_kernel_spmd(_NC_CACHE["nc"], in_maps, core_ids=list(range(8)))
    out = np.empty((4, 4096, 1024), f32)
    for core in range(8):
        b, j = core // 2, core % 2
        out[b, j * T_OWN:(j + 1) * T_OWN] = res.results[core]["out"]
    return out
```

```python
import math
import os
import contextlib
import numpy as np
import concourse.bass as bass
import concourse.mybir as mybir
from concourse.bass_utils import run_bass_kernel_spmd

F32 = mybir.dt.float32
BF16 = mybir.dt.bfloat16
ALU = mybir.AluOpType
AF = mybir.ActivationFunctionType
AX = mybir.AxisListType

COMPUTE = ("pe", "act", "dve", "pool")
N_DMA_SEMS = int(os.environ.get("DBG_NDS", "12"))
NEGM = -30000.0
NOP_AFTER_WAIT = int(os.environ.get('DBG_NAW', '1'))
NAW_SKIP = tuple(os.environ.get('DBG_NAWSKIP', 'pe,act,pool,sp,dve').split(','))
EPS = 1e-6


class _Op:
    __slots__ = ("eng", "fn", "deps", "idx", "signal", "count", "dsem", "is_dma", "epoch", "where", "waits", "snap")


class Sched:
    def __init__(self, nc):
        self.nc = nc
        self.ops = {e: [] for e in COMPUTE + ("sp",)}
        self.last_write = {}
        self.readers = {}
        self.dma_rr = 0
        self.dma_last = [None] * N_DMA_SEMS
        self.dma_count = [0] * N_DMA_SEMS
        self.all_ops = []
        self.pending = {}

    def barrier(self):
        lasts = [self.ops[e][-1] for e in self.ops if self.ops[e]]
        lasts += [d for d in self.dma_last if d is not None]
        for e in self.ops:
            self.pending[e] = list(lasts)

    def add(self, eng, fn, reads=(), writes=(), dma=False):
        lim = int(os.environ.get("DBG_MAXOPS", "0"))
        if lim and not getattr(self, "nolimit", False) and len(self.all_ops) >= lim:
            return None
        op = _Op()
        op.eng, op.fn, op.is_dma = eng, fn, dma
        op.signal, op.count, op.dsem = False, None, None
        deps = []
        if self.pending.get(eng):
            deps.extend(self.pending[eng])
            self.pending[eng] = None
        for r in reads:
            w = self.last_write.get(r)
            if w is not None:
                deps.append(w)
        for w_ in writes:
            w = self.last_write.get(w_)
            if w is not None:
                deps.append(w)
            deps.extend(self.readers.get(w_, ()))
        if dma:
            k = self.dma_rr
            self.dma_rr = (self.dma_rr + 1) % N_DMA_SEMS
            op.dsem = k
            if self.dma_last[k] is not None:
                deps.append(self.dma_last[k])
            self.dma_last[k] = op
            self.dma_count[k] += 1
            op.count = 16 * self.dma_count[k]
            op.signal = True
        op.deps = [d for d in deps if d is not op]
        op.idx = len(self.ops[eng])
        if os.environ.get("DBG_DUMP"):
            import sys as _sys
            f = _sys._getframe(1)
            while f.f_code.co_name in ("A", "dma", "add"):
                f = f.f_back
            op.where = "%s:%d" % (f.f_code.co_name, f.f_lineno)
        self.ops[eng].append(op)
        self.all_ops.append(op)
        for r in reads:
            self.readers.setdefault(r, []).append(op)
        for w_ in writes:
            self.last_write[w_] = op
            self.readers[w_] = []
        return op

    @staticmethod
    def _skip(d, op):
        if d.is_dma or op.is_dma or d.eng != op.eng:
            return False
        if d.eng == "pe":
            return True
        return d.idx < op.idx - 3

    def emit(self, final_waits=()):
        nc = self.nc
        for op in self.all_ops:
            for d in op.deps:
                if not d.is_dma and not self._skip(d, op):
                    d.signal = True
        for op in final_waits:
            op.signal = True
        EPOCH = 2000
        nep = {}
        for e in self.ops:
            c = 0
            for op in self.ops[e]:
                if not op.is_dma and op.signal:
                    op.epoch = c // EPOCH
                    op.count = c % EPOCH + 1
                    c += 1
            nep[e] = c // EPOCH + 1
        know = {e: {} for e in self.ops}
        for op in self.all_ops:
            K = know[op.eng]
            cand = {}
            for d in op.deps:
                if d.is_dma:
                    key = ("d", d.dsem)
                else:
                    if self._skip(d, op):
                        continue
                    key = ("c", d.eng, d.epoch)
                if key not in cand or d.count > cand[key].count:
                    cand[key] = d
            order = sorted(cand.items(), key=lambda kv: -len(kv[1].snap))
            op.waits = []
            for key, d in order:
                if K.get(key, 0) >= d.count:
                    continue
                op.waits.append((key, d.count))
                K[key] = d.count
                for k2, v2 in d.snap.items():
                    if v2 > K.get(k2, 0):
                        K[k2] = v2
            op.snap = dict(K)
        with contextlib.ExitStack() as st:
            dsems = [st.enter_context(nc.semaphore("d_%d" % i)) for i in range(N_DMA_SEMS)]
            sems = {(e, k): st.enter_context(nc.semaphore("s_%s_%d" % (e, k))) for e in self.ops for k in range(nep[e])}
            block = st.enter_context(nc.Block())

            def run(e, handle):
                seen = {}
                for _ in range(int(os.environ.get("DBG_NOP_" + e.upper(), "0"))):
                    handle.engine_nop()
                for op in self.ops[e]:
                    wl = op.waits
                    for key, val in wl:
                        handle.wait_ge(dsems[key[1]] if key[0] == "d" else sems[(key[1], key[2])], val)
                    if wl and NOP_AFTER_WAIT and e not in NAW_SKIP:
                        handle.nop()
                    if os.environ.get("DBG_DUMP"):
                        with open(os.environ["DBG_DUMP"], "a") as fh:
                            fh.write("%s %d %s waits=%s sig=%s\n" % (e, op.idx, op.where, wl,
                                     (("d", op.dsem, op.count) if op.is_dma else ((e, op.epoch, op.count) if op.signal else None))))
                    inst = op.fn(handle)
                    if op.is_dma:
                        inst.then_inc(dsems[op.dsem], 16)
                    elif op.signal:
                        inst.then_inc(sems[(e, op.epoch)], 1)
                if e == "sp":
                    for op in final_waits:
                        handle.wait_ge(dsems[op.dsem] if op.is_dma else sems[(op.eng, op.epoch)], op.count)

            block.sync(lambda h: run("sp", h))
            block.tensor(lambda h: run("pe", h))
            block.scalar(lambda h: run("act", h))
            block.vector(lambda h: run("dve", h))
            block.gpsimd(lambda h: run("pool", h))


QA, KA, VA, GA, QB, KB, VB, GB, ZB, QC, GC = 0, 512, 1024, 1536, 2048, 2304, 2560, 3072, 3584, 3600, 4112
T_OWN = 2048
T_ALL = 4096
SCR_BYTES = 73 * 1024


def build_program(phases=(1, 4, 2, 3, 5), mode=None):
    nc = bass.Bass("TRN2", target_bir_lowering=False)

    def din(name, shape):
        return nc.dram_tensor(name, list(shape), F32, kind="ExternalInput").ap()

    xall = din("xall", [T_ALL, 1024])
    memb = din("memb", [256, 1024])
    w_in = din("w_in", [1024, 4624])
    w_mem = din("w_mem", [1024, 1024])
    w_out = din("w_out", [1536, 1024])
    nwc_d = din("nwc", [128, 8])
    mnwc_d = din("mnwc", [128, 8])
    fnw_d = din("fnw", [128, 1024])
    balc_d = din("balc", [128, 2])
    wa2_d = din("wa2", [16, 256])
    gnwc_d = din("gnwc", [128, 1])
    biasDP_d = din("biasDP", [8, 128, 1024])
    c31_d = din("c31", [128, 8])
    ident_d = din("ident", [128, 128])
    diagM_d = din("diagM", [128, 512])
    gmask_d = din("gmask", [128, 256])
    dforce_d = din("dforce", [128, 256])
    onehot_d = din("onehot", [16, T_ALL])
    tri8_d = din("tri8", [128, 1024])
    rmask_d = din("rmask", [128, 512])
    out_d = nc.dram_tensor("out", [T_OWN, 1024], F32, kind="ExternalOutput").ap() if mode != "A" else None
    mixo_d = nc.dram_tensor("mixo", [128, 8 * T_OWN], F32, kind="ExternalOutput").ap() if mode == "A" else None
    mixi_d = din("mixi", [128, 8 * T_OWN]) if mode == "B" else None

    with contextlib.ExitStack() as st:
        def sb(name, shape, dt):
            return st.enter_context(nc.sbuf_tensor(name, list(shape), dt))

        def ps(name, shape, dt):
            return st.enter_context(nc.psum_tensor(name, list(shape), dt))

        uT = sb("uT", [128, 8, T_ALL], BF16)
        mixT = sb("mixT", [128, 12, T_OWN], BF16)
        wst = sb("wst", [128, 8, 256], F32)
        wb = [sb("wb0", [128, 8, 256], BF16), sb("wb1", [128, 8, 256], BF16)]
        identb = sb("identb", [128, 128], BF16)
        onesb = sb("onesb", [128, 128], BF16)
        onesf = sb("onesf", [128, 128], F32)
        nwc = sb("nwc_s", [128, 8], F32)
        mnwc = sb("mnwc_s", [128, 8], F32)
        c31 = sb("c31_s", [128, 8], F32)
        balc = sb("balc_s", [128, 2], F32)
        nbal = sb("nbal_s", [128, 2], F32)
        wa2 = sb("wa2_s", [16, 256], F32)
        gnwc = sb("gnwc_s", [128, 1], F32)
        epsc = sb("epsc", [128, 1], F32)
        kmTx = sb("kmTx", [128, 4, 256], BF16)
        vmx = sb("vmx", [128, 2, 512], BF16)
        stat = sb("stat", [128, 3, 40], F32)
        scr = sb("scr", [128, SCR_BYTES // 4], F32)

        psAB = ps("psAB", [128, 1024], F32)
        psA = psAB[:, 0:512]
        psB = psAB[:, 512:1024]
        psS = [ps("psS0", [128, 1024], F32), ps("psS1", [128, 1024], F32)]
        psO = ps("psO", [128, 512], F32)
        psT = ps("psT", [128, 1024], BF16)
        psO_bf = psO[:].bitcast(BF16)
        psT_f = psT[:].bitcast(F32)

        S = Sched(nc)
        scr_off = [0]

        def scr_reset():
            S.barrier()
            scr_off[0] = 0

        def carve(shape, dt):
            n = 1
            for s_ in shape[1:]:
                n *= s_
            nbytes = n * (2 if dt == BF16 else 4)
            nbytes = (nbytes + 63) // 64 * 64
            o = scr_off[0]
            assert o + nbytes <= SCR_BYTES, ("scratch overflow", o, nbytes)
            scr_off[0] = o + nbytes
            v = scr[:, o // 4:(o + nbytes) // 4]
            if dt == BF16:
                v = v.bitcast(BF16)
            v = v[:, 0:n]
            if len(shape) == 3:
                v = v.rearrange("p (a b) -> p a b", a=shape[1])
            elif len(shape) == 4:
                v = v.rearrange("p (a b c) -> p a b c", a=shape[1], b=shape[2])
            if shape[0] < 128:
                v = v[0:shape[0]]
            return v

        def dma(out, in_, reads=(), writes=(), eng="sp"):
            return S.add(eng, lambda e, o=out, i=in_: e.dma_start(out=o, in_=i), reads=reads, writes=writes, dma=True)

        def A(eng, fn, reads=(), writes=()):
            return S.add(eng, fn, reads=reads, writes=writes)

        identf = scr[:, 0:128]
        dma(identf, ident_d, writes=["identf"])
        A("dve", lambda e: e.tensor_copy(out=identb[:], in_=identf), reads=["identf"], writes=["identb"])
        A("dve", lambda e: e.memset(onesb[:], 1.0), writes=["onesb"])
        A("dve", lambda e: e.memset(onesf[:], 1.0), writes=["onesf"])
        A("dve", lambda e: e.memset(epsc[:], EPS), writes=["epsc"])
        dma(nwc[:], nwc_d, writes=["nwc"])
        dma(mnwc[:], mnwc_d, writes=["mnwc"])
        dma(c31[:], c31_d, writes=["c31"])
        dma(balc[:], balc_d, writes=["balc"])
        dma(wa2[:], wa2_d, writes=["wa2"])
        dma(gnwc[:], gnwc_d, writes=["gnwc"])
        A("dve", lambda e: e.tensor_scalar(out=nbal[:], in0=balc[:], scalar1=-1.0, scalar2=None, op0=ALU.mult),
          reads=["balc"], writes=["nbal"])

        wcount = [0]

        def load_w(dram, pieces, scale, kc0=0, nkc=8, dst=None, dkey=None):
            if dst is None:
                k = wcount[0] % 2
                wcount[0] += 1
                dst, dkey = wb[k], "wb%d" % k
            off = 0
            for (c0, W) in pieces:
                src = dram[kc0 * 128:(kc0 + nkc) * 128, c0:c0 + W].rearrange("(kc p) c -> p kc c", p=128)
                dma(wst[:, 0:nkc, off:off + W], src, writes=["wst"])
                off += W
            for kc in range(nkc):
                if scale is not None:
                    A("dve", lambda e, kc=kc, off=off: e.tensor_scalar(
                        out=dst[:, kc, 0:off], in0=wst[:, kc, 0:off], scalar1=scale[:, kc0 + kc:kc0 + kc + 1],
                        scalar2=None, op0=ALU.mult), reads=["wst", "nwc", "mnwc"], writes=[dkey])
                else:
                    A("dve", lambda e, kc=kc, off=off: e.tensor_copy(out=dst[:, kc, 0:off], in_=wst[:, kc, 0:off]),
                      reads=["wst"], writes=[dkey])
            return dst, dkey

        ntile = [0]

        pend_B = []

        def norm_transpose(src2, dstTs, dkey, xs, ub, junk):
            ip = ntile[0] // 2
            bx = ip % len(xs)
            dma(xs[bx], src2.rearrange("(t p) d -> p t d", p=128), writes=[("xs", bx)])
            for t in range(2):
                partB = norm_partA(xs[bx][:, t, :], ("xs", bx), dstTs[t], dkey, ub, junk)
                if len(pend_B) >= 2:
                    pend_B.pop(0)()
                pend_B.append(partB)

        def norm_flush():
            while pend_B:
                pend_B.pop(0)()

        def norm_partA(xt, xkey, dstT, dkey, ub, junk):
            i = ntile[0]
            ntile[0] += 1
            b2 = i % len(ub)
            col = i % 40
            pT, pTk = (psT, "psT") if i % 2 == 0 else (psO_bf, "psO")
            A("act", lambda e: e.activation(out=junk, in_=xt, func=AF.Square, accum_out=stat[:, 0, col:col + 1]),
              reads=[xkey], writes=["junk", ("ss", col)])
            A("act", lambda e: e.activation(out=stat[:, 1, col:col + 1], in_=stat[:, 0, col:col + 1], func=AF.Sqrt,
                                            scale=1.0 / 1024, bias=epsc[:]), reads=[("ss", col), "epsc"], writes=[("sd", col)])
            A("dve", lambda e: e.reciprocal(out=stat[:, 2, col:col + 1], in_=stat[:, 1, col:col + 1]),
              reads=[("sd", col)], writes=[("rs", col)])
            A("dve", lambda e: e.tensor_scalar(out=ub[b2], in0=xt, scalar1=stat[:, 2, col:col + 1], scalar2=None,
                                               op0=ALU.mult), reads=[xkey, ("rs", col)], writes=[("ub", b2)])

            def partB():
                for kc in range(8):
                    A("pe", lambda e, kc=kc: e.transpose(out=pT[:, kc * 128:(kc + 1) * 128], in_=ub[b2][:, kc * 128:(kc + 1) * 128],
                                                         identity=identb[:]), reads=[("ub", b2), "identb"], writes=[pTk])
                src = pT[:, 0:1024].rearrange("p (a b) -> p a b", a=8)
                if i % 2 == 0:
                    A("act", lambda e: e.activation(out=dstT, in_=src, func=AF.Copy), reads=[pTk], writes=[dkey])
                else:
                    A("dve", lambda e: e.tensor_copy(out=dstT, in_=src), reads=[pTk], writes=[dkey])
            return partB

        def proj_F(wt, wkey, coff, M, srcT, skey, t0, N, pst, pkey):
            for kc in range(8):
                A("pe", lambda e, kc=kc: e.matmul(pst, lhsT=wt[:, kc, coff:coff + M], rhs=srcT[:, kc, t0:t0 + N],
                                                  start=(kc == 0), stop=(kc == 7)), reads=[wkey, skey], writes=[pkey])

        def proj_T(wt, wkey, coff, NC, srcT, skey, t0, pst, pkey):
            for kc in range(8):
                A("pe", lambda e, kc=kc: e.matmul(pst, lhsT=srcT[:, kc, t0:t0 + 128], rhs=wt[:, kc, coff:coff + NC],
                                                  start=(kc == 0), stop=(kc == 7)), reads=[wkey, skey], writes=[pkey])

        scr_reset()
        _ = carve([128, 128], F32)
        xs = [carve([128, 2, 1024], F32) for _ in range(4)]
        ub = [carve([128, 1024], BF16) for _ in range(3)]
        junk = carve([128, 1024], BF16)
        mT = carve([128, 8, 256], BF16)
        norm_transpose(memb[0:256, :], [mT[:, :, 0:128], mT[:, :, 128:256]], "mT", xs, ub, junk)
        norm_flush()
        for half in range(2):
            for cc in range(2):
                wt, wk = load_w(w_mem, [(half * 512 + cc * 256, 256)], mnwc)
                if half == 0:
                    for c2 in range(2):
                        c = cc * 2 + c2
                        proj_F(wt, wk, c2 * 128, 128, mT, "mT", 0, 256, psA[:, 0:256], "psA")
                        A("dve", lambda e, c=c: e.tensor_copy(out=kmTx[:, c, :], in_=psA[:, 0:256]), reads=["psA"], writes=["kmTx"])
                else:
                    for mt in range(2):
                        proj_T(wt, wk, 0, 256, mT, "mT", mt * 128, psB[:, 0:256], "psB")
                        A("dve", lambda e, mt=mt, cc=cc: e.tensor_copy(out=vmx[:, mt, cc * 256:(cc + 1) * 256], in_=psB[:, 0:256]),
                          reads=["psB"], writes=["vmx"])
        for i in range(16):
            norm_transpose(xall[i * 256:(i + 1) * 256, :], [uT[:, :, i * 256:i * 256 + 128], uT[:, :, i * 256 + 128:(i + 1) * 256]],
                           "uT", xs, ub, junk)
        norm_flush()

        def phase2():
            scr_reset()
            qcT = carve([128, T_OWN], BF16)
            gcT = carve([128, T_OWN], BF16)
            ptx = [carve([128, 2, 512], BF16) for _ in range(2)]
            recx = carve([128, 512], F32)
            t1x = carve([128, 512], F32)
            XS = 1.0 / math.sqrt(128.0)
            for c in range(int(os.environ.get('DBG_NXH', '4'))):
                wt, wk = load_w(w_in, [(QC + c * 128, 128), (GC + c * 128, 128)], nwc)
                for g in range(int(os.environ.get('DBG_NXPG', '4'))):
                    if 'q' in os.environ.get('DBG_XPARTS', 'qg'):
                        proj_F(wt, wk, 0, 128, uT, "uT", T_OWN + g * 512, 512, psA[:], "psA")
                        A("dve", lambda e, g=g: e.tensor_copy(out=qcT[:, g * 512:(g + 1) * 512], in_=psA[:]), reads=["psA"], writes=[("qcT", g)])
                    if 'g' not in os.environ.get('DBG_XPARTS', 'qg'):
                        continue
                    proj_F(wt, wk, 128, 128, uT, "uT", T_OWN + g * 512, 512, psB[:], "psB")
                    A("act", lambda e, g=g: e.activation(out=gcT[:, g * 512:(g + 1) * 512], in_=psB[:], func=(AF.Copy if os.environ.get("DBG_NOSILU") else AF.Silu)),
                      reads=["psB"], writes=[("gcT", g)])
                for g in range(int(os.environ.get('DBG_NXG', '4'))):
                    pS = psS[g % 2]
                    pk = "psS%d" % (g % 2)
                    pt = ptx[g % 2]
                    ptk = ("ptx", g % 2)
                    for mt in range(2):
                        A("pe", lambda e, mt=mt, pS=pS, g=g, c=c: e.matmul(pS[:, mt * 512:(mt + 1) * 512], lhsT=kmTx[:, c, mt * 128:(mt + 1) * 128],
                                                                          rhs=qcT[:, g * 512:(g + 1) * 512], start=True, stop=True),
                          reads=["kmTx", ("qcT", g)], writes=[pk])
                    A("act", lambda e, pS=pS, pt=pt: e.activation(out=pt.rearrange("p a b -> p (a b)"), in_=pS[:], func=AF.Exp, scale=XS),
                      reads=[pk], writes=[ptk])
                    for mt in range(2):
                        A("pe", lambda e, mt=mt, pt=pt, c=c, g=g: e.matmul((psO if g % 2 == 0 else psT_f)[:, 0:512], lhsT=vmx[:, mt, c * 128:(c + 1) * 128], rhs=pt[:, mt, :],
                                                                     start=(mt == 0), stop=(mt == 1)), reads=["vmx", ptk], writes=["psO" if g % 2 == 0 else "psT"])
                    for mt in range(2):
                        A("pe", lambda e, mt=mt, pt=pt: e.matmul(psB[:], lhsT=onesb[:], rhs=pt[:, mt, :], start=(mt == 0), stop=(mt == 1)),
                          reads=["onesb", ptk], writes=["psB"])
                    A("dve", lambda e: e.reciprocal(out=recx, in_=psB[:]), reads=["psB"], writes=["recx"])
                    A("dve", lambda e, g=g: e.tensor_tensor(out=t1x, in0=(psO if g % 2 == 0 else psT_f)[:, 0:512], in1=recx, op=ALU.mult),
                      reads=["psO" if g % 2 == 0 else "psT", "recx"], writes=["t1x"])
                    A("pool", lambda e, g=g, c=c: e.tensor_tensor(out=mixT[:, 8 + c, g * 512:(g + 1) * 512], in0=t1x,
                                                                 in1=gcT[:, g * 512:(g + 1) * 512], op=ALU.mult),
                      reads=["t1x", ("gcT", g)], writes=[("mixT", 8 + c)])

        def phase3():
            scr_reset()
            tri8 = carve([128, 1024], BF16)
            rmask = carve([128, 512], F32)
            tri8f = carve([128, 1024], F32)
            dma(tri8f, tri8_d, writes=["tri8f"])
            A("dve", lambda e: e.tensor_copy(out=tri8, in_=tri8f), reads=["tri8f"], writes=["tri8"])
            dma(rmask, rmask_d, writes=["rmask"])
            scr_off[0] -= 4096
            S.barrier()
            bufs = []
            for pr in range(2):
                b = {}
                b["wg"] = carve([128, 8, 784], BF16)
                b["zt"] = carve([16, 512], F32)
                b["tmpA"] = carve([128, 512], F32)
                b["Bg"] = carve([128, 512], F32)
                b["tmpE"] = carve([128, 512], F32)
                b["sdg"] = carve([128, 512], F32)
                b["ktg"] = carve([128, 512], BF16)
                b["khg"] = carve([128, 512], BF16)
                b["qtg"] = carve([128, 512], BF16)
                b["dg"] = carve([128, 4], F32)
                b["khtok"] = carve([128, 4, 128], BF16)
                b["vbg"] = carve([128, 4, 256], BF16)
                b["Sbg"] = carve([128, 4, 128], BF16)
                b["Sst"] = carve([128, 128], F32)
                b["attm"] = carve([128, 2, 4, 128], BF16)
                b["gbg"] = carve([128, 512], BF16)
                bufs.append(b)
            wa2b = carve([16, 256], BF16)
            A("dve", lambda e: e.tensor_copy(out=wa2b, in_=wa2[0:16, :]), reads=["wa2"], writes=["wa2b"])
            for pr in range(2):
                wg = bufs[pr]["wg"]
                wk_ = ("wg", pr)
                load_w(w_in, [(QB + pr * 128, 128), (KB + pr * 128, 128)], nwc, dst=wg[:, :, 0:256], dkey=wk_)
                load_w(w_in, [(VB + pr * 256, 256)], nwc, dst=wg[:, :, 256:512], dkey=wk_)
                load_w(w_in, [(GB + pr * 256, 256)], nwc, dst=wg[:, :, 512:768], dkey=wk_)
                load_w(w_in, [(ZB, 16)], nwc, dst=wg[:, :, 768:784], dkey=wk_)
                A("dve", lambda e, pr=pr: e.memset(bufs[pr]["Sst"], 0.0), writes=[("Sst", pr)])

            def group_body(pr, g):
                b = bufs[pr]
                wg, zt, tmpA, Bg, tmpE, sdg = b["wg"], b["zt"], b["tmpA"], b["Bg"], b["tmpE"], b["sdg"]
                ktg, khg, qtg, dg, khtok, vbg, Sbg, Sst, attm, gbg = (b["ktg"], b["khg"], b["qtg"], b["dg"], b["khtok"], b["vbg"],
                                                                     b["Sbg"], b["Sst"], b["attm"], b["gbg"])

                def k(name):
                    return (name, pr)
                wk_ = k("wg")
                own = g >= 4
                t0 = g * 512
                zt = bufs[g % 2]["zt"].bitcast(BF16)[:, 0:512]
                zkey = ("ztS", g % 2)
                if pr == 0:
                    proj_F(wg, wk_, 768, 16, uT, "uT", t0, 512, psA[0:16, :], "psA")
                    A("act", lambda e: e.activation(out=zt, in_=psA[0:16, :], func=AF.Copy), reads=["psA"], writes=[zkey])
                yield
                A("pe", lambda e: e.matmul(psB[:], lhsT=wa2b[0:16, pr * 128:(pr + 1) * 128], rhs=zt, start=True, stop=True),
                  reads=["wa2b", zkey], writes=["psB"])
                A("act", lambda e: e.activation(out=tmpA, in_=psB[:], func=AF.Exp, scale=-1.0, bias=nbal[:, pr:pr + 1]),
                  reads=["psB", "nbal"], writes=[k("tmpA")])
                yield
                A("act", lambda e: e.activation(out=tmpA, in_=tmpA, func=AF.Ln, bias=1.0), reads=[k("tmpA")], writes=[k("tmpA")])
                A("dve", lambda e: e.tensor_tensor_scan(out=Bg, data0=rmask, data1=tmpA, initial=0.0, op0=ALU.mult, op1=ALU.add),
                  reads=[k("tmpA"), "rmask"], writes=[k("Bg")])
                yield
                proj_F(wg, wk_, 128, 128, uT, "uT", t0, 512, psA[:], "psA")
                A("act", lambda e: e.activation(out=tmpE, in_=Bg, func=AF.Exp, scale=1.0 / 16), reads=[k("Bg")], writes=[k("tmpE")])
                A("dve", lambda e: e.tensor_tensor(out=ktg, in0=psA[:], in1=tmpE, op=ALU.mult), reads=["psA", k("tmpE")], writes=[k("ktg")])
                A("act", lambda e: e.activation(out=dg, in_=Bg.rearrange("p (a b) -> p a b", a=4)[:, :, 127], func=AF.Exp, scale=-1.0 / 16),
                  reads=[k("Bg")], writes=[k("dg")])
                yield
                for ch in range(4):
                    A("dve", lambda e, ch=ch: e.tensor_scalar(out=khg[:, ch * 128:(ch + 1) * 128], in0=ktg[:, ch * 128:(ch + 1) * 128],
                                                              scalar1=dg[:, ch:ch + 1], scalar2=None, op0=ALU.mult),
                      reads=[k("ktg"), k("dg")], writes=[k("khg")])
                for ch in range(4):
                    A("pe", lambda e, ch=ch: e.transpose(out=psT[:, ch * 128:(ch + 1) * 128], in_=khg[:, ch * 128:(ch + 1) * 128],
                                                         identity=identb[:]), reads=[k("khg"), "identb"], writes=["psT"])
                A("dve", lambda e: e.tensor_copy(out=khtok, in_=psT[:, 0:512].rearrange("p (a b) -> p a b", a=4)),
                  reads=["psT"], writes=[k("khtok")])
                yield
                for ch in range(4):
                    proj_T(wg, wk_, 256, 256, uT, "uT", t0 + ch * 128, psS[0][:, ch * 256:(ch + 1) * 256], "psS0")
                A("act", lambda e: e.activation(out=vbg.rearrange("p a b -> p (a b)"), in_=psS[0][:], func=AF.Copy),
                  reads=["psS0"], writes=[k("vbg")])
                yield
                for ch in range(4):
                    A("pe", lambda e, ch=ch: e.matmul(psS[1][:, ch * 256:(ch + 1) * 256], lhsT=khtok[:, ch, :], rhs=vbg[:, ch, :],
                                                      start=True, stop=True), reads=[k("khtok"), k("vbg")], writes=["psS1"])
                for ch in range(4):
                    if own:
                        A("pool", lambda e, ch=ch: e.tensor_copy(out=Sbg[:, ch, :], in_=Sst), reads=[k("Sst")], writes=[k("Sbg")])
                    for hh in range(2):
                        r0 = hh * 64
                        A("dve", lambda e, ch=ch, hh=hh, r0=r0: e.scalar_tensor_tensor(
                            out=Sst[r0:r0 + 64, :], in0=Sst[r0:r0 + 64, :], scalar=dg[r0:r0 + 64, ch:ch + 1],
                            in1=psS[1][r0:r0 + 64, ch * 256 + hh * 128:ch * 256 + (hh + 1) * 128], op0=ALU.mult, op1=ALU.add),
                          reads=[k("Sst"), k("dg"), "psS1"], writes=[k("Sst")])
                yield
                if not own:
                    return
                proj_F(wg, wk_, 0, 128, uT, "uT", t0, 512, psA[:], "psA")
                A("act", lambda e: e.activation(out=tmpE, in_=Bg, func=AF.Exp, scale=-1.0 / 16), reads=[k("Bg")], writes=[k("tmpE")])
                A("dve", lambda e: e.scalar_tensor_tensor(out=qtg, in0=psA[:], scalar=0.125, in1=tmpE, op0=ALU.mult, op1=ALU.mult),
                  reads=["psA", k("tmpE")], writes=[k("qtg")])
                yield
                for hh in range(2):
                    r0 = hh * 64
                    for ch in range(4):
                        A("pe", lambda e, hh=hh, ch=ch, r0=r0: e.matmul(
                            psS[0][:, (hh * 4 + ch) * 128:(hh * 4 + ch + 1) * 128], lhsT=ktg[r0:r0 + 64, ch * 128:(ch + 1) * 128],
                            rhs=qtg[r0:r0 + 64, ch * 128:(ch + 1) * 128], start=True, stop=True),
                          reads=[k("ktg"), k("qtg")], writes=["psS0"])
                A("dve", lambda e: e.tensor_tensor(out=attm.rearrange("p a b c -> p (a b c)"), in0=psS[0][:], in1=tri8, op=ALU.mult),
                  reads=["psS0", "tri8"], writes=[k("attm")])
                yield
                for hh in range(2):
                    r0 = hh * 64
                    hd = pr * 2 + hh
                    pO, pOk = (psO, "psO") if hh == 0 else (psT_f, "psT")
                    for ch in range(4):
                        A("pe", lambda e, hh=hh, ch=ch, r0=r0, pO=pO: e.matmul(
                            pO[:, ch * 128:(ch + 1) * 128], lhsT=Sbg[r0:r0 + 64, ch, :], rhs=qtg[r0:r0 + 64, ch * 128:(ch + 1) * 128],
                            start=True, stop=False), reads=[k("Sbg"), k("qtg")], writes=[pOk])
                        A("pe", lambda e, hh=hh, ch=ch, pO=pO: e.matmul(
                            pO[:, ch * 128:(ch + 1) * 128], lhsT=vbg[:, ch, hh * 128:(hh + 1) * 128], rhs=attm[:, hh, ch, :],
                            start=False, stop=True), reads=[k("vbg"), k("attm")], writes=[pOk])
                    A("act", lambda e, pO=pO: e.activation(out=khg, in_=pO[:, 0:512], func=AF.Square), reads=[pOk], writes=[k("khg")])
                    A("pe", lambda e: e.matmul(psB[:], lhsT=onesb[:], rhs=khg, start=True, stop=True), reads=["onesb", k("khg")], writes=["psB"])
                    A("act", lambda e: e.activation(out=sdg, in_=psB[:], func=AF.Sqrt, scale=1.0 / 128, bias=epsc[:]),
                      reads=["psB", "epsc"], writes=[k("sdg")])
                    A("dve", lambda e: e.reciprocal(out=sdg, in_=sdg), reads=[k("sdg")], writes=[k("sdg")])
                    A("dve", lambda e, pO=pO: e.scalar_tensor_tensor(out=tmpE, in0=pO[:, 0:512], scalar=gnwc[:, 0:1], in1=sdg, op0=ALU.mult, op1=ALU.mult),
                      reads=[pOk, "gnwc", k("sdg")], writes=[k("tmpE")])
                    proj_F(wg, wk_, 512 + hh * 128, 128, uT, "uT", t0, 512, psA[:], "psA")
                    A("act", lambda e: e.activation(out=gbg, in_=psA[:], func=AF.Silu), reads=["psA"], writes=[k("gbg")])
                    A("pool", lambda e, hd=hd: e.tensor_tensor(out=mixT[:, 4 + hd, (g - 4) * 512:(g - 3) * 512], in0=tmpE, in1=gbg, op=ALU.mult),
                      reads=[k("tmpE"), k("gbg")], writes=[("mixT", 4 + hd)])
                    yield

            for g in range(8):
                gens = [group_body(0, g), group_body(1, g)]
                while gens:
                    for gen in list(gens):
                        try:
                            next(gen)
                        except StopIteration:
                            gens.remove(gen)

        def phase4():
            scr_reset()
            Kaug = [carve([80, T_ALL], BF16) for _ in range(2)]
            Qaug = [carve([80, T_OWN], BF16) for _ in range(2)]
            V3 = carve([128, 32, 192], BF16)
            gaT = carve([128, T_OWN], BF16)
            ptm = [carve([128, 2, 512], BF16) for _ in range(3)]
            sbias = carve([128, 2, 512], F32)
            bDP = [carve([128, 4, 256], F32) for _ in range(2)]
            bD = [t_[:, 0:2, :] for t_ in bDP]
            bP = [t_[:, 2:4, :] for t_ in bDP]
            diagM = carve([128, 2, 256], F32)
            gmask = carve([128, 16, 16], F32)
            dforce = carve([128, 16, 16], F32)
            gm = carve([128, 16, 16], F32)
            m8 = carve([128, 16, 8], F32)
            thrc = carve([128, 16], F32)
            Fsel = carve([128, 16, 16], F32)
            FTin = carve([128, 16, 80], BF16)
            recm = carve([128, 512], F32)
            t1m = carve([128, 512], F32)
            ksum = carve([128, 16], F32)
            ksumB = carve([128, 16], F32)
            kmT = [carve([64, 16], BF16) for _ in range(2)]
            ohst = sbias.rearrange("p a b -> p (a b)")[0:80]
            dma(diagM.rearrange("p a b -> p (a b)"), diagM_d, writes=["diagM"])
            dma(gmask.rearrange("p a b -> p (a b)"), gmask_d, writes=["gmask"])
            dma(dforce.rearrange("p a b -> p (a b)"), dforce_d, writes=["dforce"])
            for q4 in range(4):
                dma(ohst[64:80, :], onehot_d[:, q4 * 1024:(q4 + 1) * 1024], writes=[("sbias", 0), ("sbias", 256)])
                for hh in range(2):
                    A("dve", lambda e, hh=hh, q4=q4: e.tensor_copy(out=Kaug[hh][64:80, q4 * 1024:(q4 + 1) * 1024], in_=ohst[64:80, :]),
                      reads=[("sbias", 0), ("sbias", 256)], writes=[("Koh", hh)])
            MS = os.environ.get("DBG_MSENG", "pool")
            A(MS, lambda e: e.memset(V3[:, :, 64:128], 1.0), writes=["V3ones"])
            A(MS, lambda e: e.memset(FTin.rearrange("p a b -> p (a b)"), 0.0), writes=["FTin"])
            for p in range(int(os.environ.get('DBG_NPAIR', '4'))):
                PARTS = os.environ.get('DBG_P4PARTS', 'kqvg')
                wt, wk = load_w(w_in, [(KA + p * 128, 128), (QA + p * 128, 128)], nwc)
                for g in range(8 if 'k' in PARTS else 0):
                    pst, pk = (psA, "psA") if g % 2 == 0 else (psB, "psB")
                    proj_F(wt, wk, 0, 128, uT, "uT", g * 512, 512, pst[:], pk)
                    for blk in range(2):
                        cs = slice(blk * 256, (blk + 1) * 256)
                        ts = slice(g * 512 + blk * 256, g * 512 + (blk + 1) * 256)
                        col = g * 2 + blk
                        A("act", lambda e, pst=pst, cs=cs, ts=ts, col=col: e.activation(
                            out=Kaug[0][0:64, ts], in_=pst[0:64, cs], func=AF.Copy, accum_out=ksum[0:64, col:col + 1]),
                          reads=[pk], writes=[("Kaug", 0), ("ksA", col)])
                        A("dve", lambda e, pst=pst, cs=cs, ts=ts, col=col: e.tensor_scalar(
                            out=Kaug[1][0:64, ts], in0=pst[64:128, cs], scalar1=1.0, scalar2=0.0, op0=ALU.mult, op1=ALU.add,
                            accum_out=ksumB[0:64, col:col + 1]), reads=[pk], writes=[("Kaug", 1), ("ksB", col)])
                A("dve", lambda e: e.tensor_scalar(out=kmT[0][:], in0=ksum[0:64, :], scalar1=1.0 / 256, scalar2=None, op0=ALU.mult),
                  reads=[("ksA", c_) for c_ in range(16)], writes=[("kmT", 0)])
                A("dve", lambda e: e.tensor_scalar(out=kmT[1][:], in0=ksumB[0:64, :], scalar1=1.0 / 256, scalar2=None, op0=ALU.mult),
                  reads=[("ksB", c_) for c_ in range(16)], writes=[("kmT", 1)])
                for g in range(4 if 'q' in PARTS else 0):
                    pst, pk = (psA, "psA") if g % 2 == 0 else (psB, "psB")
                    proj_F(wt, wk, 128, 128, uT, "uT", T_OWN + g * 512, 512, pst[:], pk)
                    A("act", lambda e, g=g, pst=pst: e.activation(out=Qaug[0][0:64, g * 512:(g + 1) * 512], in_=pst[0:64, :], func=AF.Copy, scale=0.125),
                      reads=[pk], writes=[("Qaug", 0)])
                    A("dve", lambda e, g=g, pst=pst: e.tensor_scalar(out=Qaug[1][0:64, g * 512:(g + 1) * 512], in0=pst[64:128, :], scalar1=0.125,
                                                                     scalar2=None, op0=ALU.mult), reads=[pk], writes=[("Qaug", 1)])
                wt, wk = load_w(w_in, [(VA + p * 128, 128), (GA + p * 128, 128)], nwc)
                def gen_vg():
                    for g4 in range(8 if 'v' in PARTS else 0):
                        pst, pk = (psA, "psA") if g4 % 2 == 0 else (psB, "psB")
                        for ti in range(4):
                            proj_T(wt, wk, 0, 128, uT, "uT", (g4 * 4 + ti) * 128, pst[:, ti * 128:(ti + 1) * 128], pk)
                        pv = pst[:].rearrange("p (a b) -> p a b", a=4)
                        A("act", lambda e, g4=g4, pv=pv: e.activation(out=V3[:, g4 * 4:(g4 + 1) * 4, 0:64], in_=pv[:, :, 0:64], func=AF.Copy),
                          reads=[pk], writes=["V3"])
                        A("dve", lambda e, g4=g4, pv=pv: e.tensor_copy(out=V3[:, g4 * 4:(g4 + 1) * 4, 128:192], in_=pv[:, :, 64:128]),
                          reads=[pk], writes=["V3"])
                        yield
                    for g in range(4 if 'g' in PARTS else 0):
                        pst, pk = (psA, "psA") if g % 2 == 0 else (psB, "psB")
                        proj_F(wt, wk, 128, 128, uT, "uT", T_OWN + g * 512, 512, pst[:], pk)
                        A("act", lambda e, g=g, pst=pst: e.activation(out=gaT[:, g * 512:(g + 1) * 512], in_=pst[:], func=AF.Silu),
                          reads=[pk], writes=["gaT"])
                        yield
                        yield
                def gen_gate():
                    for hh in range(int(os.environ.get('DBG_NHH', '2'))):
                        h = p * 2 + hh
                        dma(bDP[hh].rearrange("p a b -> p (a b)"), biasDP_d[h], writes=[("bD", hh), ("bP", hh)])
                        A("dve", lambda e, hh=hh, h=h: e.scalar_tensor_tensor(
                            out=bD[hh], in0=bD[hh], scalar=c31[:, h:h + 1], in1=diagM, op0=ALU.subtract, op1=ALU.add),
                          reads=[("bD", hh), "c31", "diagM"], writes=[("bD", hh)])
                        A("dve", lambda e, hh=hh, h=h: e.tensor_scalar(
                            out=bP[hh], in0=bP[hh], scalar1=c31[:, h:h + 1], scalar2=None, op0=ALU.subtract), reads=[("bP", hh), "c31"], writes=[("bP", hh)])
                        yield
                        gps = psS[0][:, 0:256].rearrange("p (a b) -> p a b", a=16)
                        for ti in range(16):
                            A("pe", lambda e, ti=ti, hh=hh: e.matmul(psS[0][:, ti * 16:(ti + 1) * 16], lhsT=Qaug[hh][0:64, ti * 128:(ti + 1) * 128],
                                                                     rhs=kmT[hh][:], start=True, stop=True),
                              reads=[("Qaug", hh), ("kmT", hh)], writes=["psS0"])
                        A("dve", lambda e: e.tensor_tensor(out=gm, in0=gps, in1=gmask, op=ALU.add), reads=["psS0", "gmask"], writes=["gm"])
                        for ti in range(16):
                            A("dve", lambda e, ti=ti: e.max(out=m8[:, ti, :], in_=gm[:, ti, :]), reads=["gm"], writes=["m8"])
                        A("dve", lambda e: e.tensor_scalar(out=thrc, in0=m8[:, :, 2], scalar1=-1e29, scalar2=None, op0=ALU.max),
                          reads=["m8"], writes=["thrc"])
                        yield
                        for ti in range(16):
                            A("dve", lambda e, ti=ti: e.tensor_scalar(out=Fsel[:, ti, :], in0=gm[:, ti, :], scalar1=thrc[:, ti:ti + 1],
                                                                      scalar2=None, op0=ALU.is_ge), reads=["gm", "thrc"], writes=["Fsel"])
                        A("dve", lambda e: e.tensor_scalar(out=Fsel, in0=Fsel, scalar1=-1.0, scalar2=-NEGM, op0=ALU.add, op1=ALU.mult),
                          reads=["Fsel"], writes=["Fsel"])
                        A("dve", lambda e: e.tensor_tensor(out=Fsel, in0=Fsel, in1=dforce, op=ALU.add), reads=["Fsel", "dforce"], writes=["Fsel"])
                        A("dve", lambda e, h=h: e.tensor_scalar(out=FTin[:, :, 64:80], in0=Fsel, scalar1=0.0, scalar2=c31[:, h:h + 1],
                                                                op0=ALU.min, op1=ALU.add), reads=["Fsel", "c31"], writes=["FTin"])
                        for half in range(2):
                            for t8 in range(8):
                                ti = half * 8 + t8
                                A("pe", lambda e, ti=ti, t8=t8: e.transpose(out=psT[0:80, t8 * 128:(t8 + 1) * 128], in_=FTin[:, ti, :],
                                                                            identity=identb[:]), reads=["FTin", "identb"], writes=["psT"])
                            A("dve", lambda e, half=half, hh=hh: e.tensor_copy(out=Qaug[hh][64:80, half * 1024:(half + 1) * 1024], in_=psT[64:80, :]),
                              reads=["psT"], writes=[("Qaug", hh)])
                            yield
                            yield
                gens_ = [gen_vg(), gen_gate()]
                while gens_:
                    for gen_ in list(gens_):
                        try:
                            next(gen_)
                        except StopIteration:
                            gens_.remove(gen_)
                for hh in range(int(os.environ.get('DBG_NHH', '2'))):
                    h = p * 2 + hh
                    orow = hh * 64
                    drow = 64 - hh * 64
                    slots_all = []
                    for qt in range(int(os.environ.get('DBG_NQT', '4'))):
                        l0 = 2 * qt
                        slots = list(range(8)) + [8 + m for m in range(l0 + 2)]
                        for si, n in enumerate(slots):
                            last = (n == 8 + l0 + 1)
                            c0 = 256 if last else 0
                            kind_l = kind_r = "far"
                            if n == 8 + l0 + 1:
                                kind_l, kind_r = "skip", "diag"
                            elif n == 8 + l0:
                                kind_l, kind_r = "diag", "prev"
                            elif n == 8 + l0 - 1:
                                kind_l = "prev"
                            gi = len(slots_all)
                            slots_all.append(dict(qt=qt, si=si, n=n, nslot=len(slots), c0=c0, NQ=512 - c0, q0=qt * 512 + c0,
                                                  kl=kind_l, kr=kind_r, b=gi % 3, bt=gi % 3))

                    psS3 = [psS[0], psS[1], psAB]
                    psK3 = [["psS0"], ["psS1"], ["psA", "psB"]]

                    def emit_S(sl):
                        pS, pk = psS3[sl["b"]], psK3[sl["b"]]
                        n, c0, q0, NQ = sl["n"], sl["c0"], sl["q0"], sl["NQ"]
                        for kt in range(2):
                            A("pe", lambda e, kt=kt, pS=pS, n=n, q0=q0, NQ=NQ, c0=c0, hh=hh: e.matmul(
                                pS[:, kt * 512 + c0:kt * 512 + 512], lhsT=Kaug[hh][0:80, n * 256 + kt * 128:n * 256 + (kt + 1) * 128],
                                rhs=Qaug[hh][0:80, q0:q0 + NQ], start=True, stop=True),
                              reads=[("Kaug", hh), ("Koh", hh), ("Qaug", hh)], writes=pk)

                    def emit_exp(sl):
                        pS, pk = psS3[sl["b"]], psK3[sl["b"]]
                        pt, ptk = ptm[sl["bt"]], ("ptm", sl["bt"])
                        pS3 = pS[:].rearrange("p (a b) -> p a b", a=2)
                        if sl["kl"] == "far" and sl["kr"] == "far":
                            A("act", lambda e, pS=pS, pt=pt: e.activation(out=pt.rearrange("p a b -> p (a b)"), in_=pS[:], func=AF.Exp),
                              reads=pk, writes=[ptk])
                            return
                        for (kind, cs) in ((sl["kl"], 0), (sl["kr"], 256)):
                            if kind == "skip":
                                continue
                            if kind == "far":
                                A("act", lambda e, pS3=pS3, pt=pt, cs=cs: e.activation(out=pt[:, :, cs:cs + 256], in_=pS3[:, :, cs:cs + 256],
                                                                                       func=AF.Exp), reads=pk, writes=[ptk])
                                continue
                            bt, bk = (bD[hh], ("bD", hh)) if kind == "diag" else (bP[hh], ("bP", hh))
                            A("dve", lambda e, pS3=pS3, cs=cs, bt=bt: e.tensor_tensor(out=sbias[:, :, cs:cs + 256], in0=pS3[:, :, cs:cs + 256],
                                                                                      in1=bt, op=ALU.add), reads=pk + [bk], writes=[("sbias", cs)])
                            A("act", lambda e, pt=pt, cs=cs: e.activation(out=pt[:, :, cs:cs + 256], in_=sbias[:, :, cs:cs + 256], func=AF.Exp),
                              reads=[("sbias", cs)], writes=[ptk])

                    def emit_PV(sl):
                        pt, ptk = ptm[sl["bt"]], ("ptm", sl["bt"])
                        n, c0, si, nslot, qt = sl["n"], sl["c0"], sl["si"], sl["nslot"], sl["qt"]
                        pO, pOk = (psO, "psO") if qt % 2 == 0 else (psT_f, "psT")
                        for kt in range(2):
                            A("pe", lambda e, kt=kt, pt=pt, n=n, c0=c0, si=si, nslot=nslot, hh=hh: e.matmul(
                                pO[:, c0:512], lhsT=V3[:, n * 2 + kt, hh * 64:hh * 64 + 128], rhs=pt[:, kt, c0:512],
                                start=(si == 0 and kt == 0), stop=(si == nslot - 1 and kt == 1)),
                              reads=["V3", "V3ones", ptk], writes=[pOk])
                        if si != nslot - 1:
                            return
                        A("dve", lambda e, orow=orow, drow=drow: e.reciprocal(out=recm[orow:orow + 64, :], in_=pO[drow:drow + 64, :]),
                          reads=[pOk], writes=["recm"])
                        A("dve", lambda e, orow=orow: e.tensor_tensor(out=t1m[orow:orow + 64, :], in0=pO[orow:orow + 64, :],
                                                                      in1=recm[orow:orow + 64, :], op=ALU.mult), reads=[pOk, "recm"], writes=["t1m"])
                        A("pool", lambda e, orow=orow, qt=qt, p=p: e.tensor_tensor(
                            out=mixT[orow:orow + 64, p, qt * 512:(qt + 1) * 512], in0=t1m[orow:orow + 64, :],
                            in1=gaT[orow:orow + 64, qt * 512:(qt + 1) * 512], op=ALU.mult), reads=["t1m", "gaT"], writes=[("mixT", p)])

                    for sl in slots_all[0:3]:
                        emit_S(sl)
                    for gi, sl in enumerate(slots_all):
                        emit_exp(sl)
                        if gi + 3 < len(slots_all):
                            emit_S(slots_all[gi + 3])
                        emit_PV(sl)

        def phase5():
            scr_reset()
            wo = carve([128, 12, 1024], BF16)
            fnw = carve([128, 1024], F32)
            xs5 = [carve([128, 2, 1024], F32) for _ in range(2)]
            hres = carve([128, 1024], F32)
            junk5 = carve([128, 1024], BF16)
            ot = [carve([128, 2, 1024], F32) for _ in range(2)]
            dma(fnw, fnw_d, writes=["fnw"])
            for cc in range(4):
                load_w(w_out, [(cc * 256, 256)], None, kc0=0, nkc=8, dst=wo[:, 0:8, cc * 256:(cc + 1) * 256], dkey="wo")
                load_w(w_out, [(cc * 256, 256)], None, kc0=8, nkc=4, dst=wo[:, 8:12, cc * 256:(cc + 1) * 256], dkey="wo")
            mix_keys = [("mixT", k) for k in range(12)]
            finals = []
            for ip in range(8):
                bx = ip % 2
                dma(xs5[bx], xall[T_OWN + ip * 256:T_OWN + (ip + 1) * 256, :].rearrange("(t p) d -> p t d", p=128), writes=[("xs5", bx)])
                for t in range(2):
                    i = ip * 2 + t
                    for half in range(2):
                        pst, pk = (psA, "psA") if half == 0 else (psB, "psB")
                        for kt in range(12):
                            A("pe", lambda e, kt=kt, half=half, pst=pst, i=i: e.matmul(
                                pst[:], lhsT=mixT[:, kt, i * 128:(i + 1) * 128], rhs=wo[:, kt, half * 512:(half + 1) * 512],
                                start=(kt == 0), stop=(kt == 11)), reads=mix_keys + ["wo"], writes=[pk])
                        A("dve", lambda e, half=half, pst=pst, bx=bx, t=t: e.tensor_tensor(
                            out=hres[:, half * 512:(half + 1) * 512], in0=pst[:], in1=xs5[bx][:, t, half * 512:(half + 1) * 512], op=ALU.add),
                          reads=[pk, ("xs5", bx)], writes=["hres"])
                    col = i
                    A("act", lambda e, col=col: e.activation(out=junk5, in_=hres, func=AF.Square, accum_out=stat[:, 0, col:col + 1]),
                      reads=["hres"], writes=["junk5", ("ss", col)])
                    A("act", lambda e, col=col: e.activation(out=stat[:, 1, col:col + 1], in_=stat[:, 0, col:col + 1], func=AF.Sqrt,
                                                             scale=1.0 / 1024, bias=epsc[:]), reads=[("ss", col), "epsc"], writes=[("sd", col)])
                    A("dve", lambda e, col=col: e.reciprocal(out=stat[:, 2, col:col + 1], in_=stat[:, 1, col:col + 1]),
                      reads=[("sd", col)], writes=[("rs", col)])
                    A("dve", lambda e, bx=bx, t=t, col=col: e.scalar_tensor_tensor(out=ot[bx][:, t, :], in0=hres, scalar=stat[:, 2, col:col + 1],
                                                                                  in1=fnw, op0=ALU.mult, op1=ALU.mult),
                      reads=["hres", ("rs", col), "fnw"], writes=[("ot", bx)])
                finals.append(dma(out_d[ip * 256:(ip + 1) * 256, :].rearrange("(t p) d -> p t d", p=128), ot[bx], reads=[("ot", bx)]))
            return finals

        def phase6():
            S.nolimit = True
            scr_reset()
            stg = [carve([128, 512], F32) for _ in range(2)]
            fin = []
            for kt in range(4, 12):
                for c in range(4):
                    k = (kt * 4 + c) % 2
                    A("dve", lambda e, kt=kt, c=c, k=k: e.tensor_copy(out=stg[k], in_=mixT[:, kt, c * 512:(c + 1) * 512]),
                      reads=[("mixT", kt)], writes=[("stg", k)])
                    fin.append(dma(mixo_d[:, (kt - 4) * T_OWN + c * 512:(kt - 4) * T_OWN + (c + 1) * 512], stg[k], reads=[("stg", k)]))
            return fin

        def phase7():
            scr_reset()
            stg = [carve([128, 512], F32) for _ in range(2)]
            for kt in range(4, 12):
                for c in range(4):
                    k = (kt * 4 + c) % 2
                    dma(stg[k], mixi_d[:, (kt - 4) * T_OWN + c * 512:(kt - 4) * T_OWN + (c + 1) * 512], writes=[("stg", k)])
                    A("dve", lambda e, kt=kt, c=c, k=k: e.tensor_copy(out=mixT[:, kt, c * 512:(c + 1) * 512], in_=stg[k]),
                      reads=[("stg", k)], writes=[("mixT", kt)])

        def phase8():
            for k in range(int(os.environ.get('DBG_PAD', '1000'))):
                if os.environ.get('DBG_PADENG', 'pool') == 'pe':
                    A("pe", lambda e: e.transpose(out=psT[:, 0:128], in_=identb[:], identity=identb[:]), writes=[("padjunk", k)])
                elif os.environ.get('DBG_PADENG', 'pool') == 'act':
                    A("act", lambda e: e.activation(out=stat[:, 2, 39:40], in_=stat[:, 2, 38:39], func=AF.Copy), writes=[("padjunk", k)])
                else:
                    A(os.environ.get('DBG_PADENG', 'pool'), lambda e: e.memset(stat[:, 2, 39:40], 0.0), writes=[("padjunk", k)])

        fns = {2: phase2, 3: phase3, 4: phase4, 5: phase5, 6: phase6, 7: phase7, 8: phase8}
        finals = None
        for ph in phases:
            if ph in fns:
                r_ = fns[ph]()
                if ph in (5, 6):
                    finals = r_
        S.emit(final_waits=finals)
    return nc


def _t5_bucket(rel):
    n = np.maximum(rel, 0).astype(np.int32)
    is_small = n < 16
    nf = np.maximum(n, 16).astype(np.float32)
    large = 16 + (np.log(nf / np.float32(16.0)) / np.float32(math.log(128 / 16)) * np.float32(16)).astype(np.int32)
    large = np.minimum(large, 31)
    return np.where(is_small, n, large)


_NC_CACHE = {}


def _prep(x, mem, norm_w, w_in, w_alpha2, b_alpha, gla_norm_w, mem_norm_w, w_mem_kv, w_out, rel_bias, final_norm_w):
    f32 = np.float32
    x = np.asarray(x, f32)
    mem = np.asarray(mem, f32)
    rel_bias = np.asarray(rel_bias, f32)

    p = np.arange(128)[:, None, None]
    kt = np.arange(2)[None, :, None]
    q = np.arange(256)[None, None, :]
    kloc = kt * 128 + p
    relD = q - kloc
    bktD = _t5_bucket(relD)
    bktP = _t5_bucket(256 + relD)
    biasD = np.ascontiguousarray(np.transpose(rel_bias[bktD], (3, 0, 1, 2))).reshape(8, 128, 512).astype(f32)
    biasP = np.ascontiguousarray(np.transpose(rel_bias[bktP], (3, 0, 1, 2))).reshape(8, 128, 512).astype(f32)
    diagM = np.where(relD >= 0, 0.0, NEGM).astype(f32).reshape(128, 512)
    c31 = np.ascontiguousarray(np.broadcast_to(rel_bias[31][None, :], (128, 8))).astype(f32)
    ident = np.eye(128, dtype=f32)
    onehot = (np.arange(T_ALL)[None, :] // 256 == np.arange(16)[:, None]).astype(f32)
    s_ = np.arange(128)[:, None]
    t_ = np.arange(128)[None, :]
    tri8 = np.tile((s_ <= t_).astype(f32), (1, 8))
    rmask = np.tile((np.arange(512) % 128 != 0).astype(f32)[None, :], (128, 1))
    common = {
        "w_in": np.ascontiguousarray(np.asarray(w_in, f32)[0]),
        "w_mem": np.ascontiguousarray(np.asarray(w_mem_kv, f32)[0]),
        "w_out": np.ascontiguousarray(np.asarray(w_out, f32)[0]),
        "nwc": np.ascontiguousarray(np.asarray(norm_w, f32)[0].reshape(8, 128).T),
        "mnwc": np.ascontiguousarray(np.asarray(mem_norm_w, f32)[0].reshape(8, 128).T),
        "fnw": np.ascontiguousarray(np.broadcast_to(np.asarray(final_norm_w, f32)[None, :], (128, 1024))),
        "balc": np.ascontiguousarray(np.asarray(b_alpha, f32)[0].reshape(2, 128).T),
        "wa2": np.ascontiguousarray(np.asarray(w_alpha2, f32)[0]),
        "gnwc": np.ascontiguousarray(np.asarray(gla_norm_w, f32)[0].reshape(128, 1)),
        "biasDP": np.ascontiguousarray(np.concatenate([biasD, biasP], axis=2)), "c31": c31, "ident": ident, "diagM": diagM,
        "onehot": onehot, "tri8": tri8, "rmask": rmask,
    }
    in_maps = []
    for core in range(8):
        b, j = core // 2, core % 2
        own = x[b, j * T_OWN:(j + 1) * T_OWN]
        prev = x[b, 0:T_OWN] if j == 1 else np.zeros((T_OWN, 1024), f32)
        gm = np.full((16, 16), -1e30, f32)
        df = np.zeros((16, 16), f32)
        for ti in range(16):
            l = ti // 2
            for n in range(16):
                valid = (n < 8 and j == 1) or (n >= 8 and (n - 8) < l)
                if valid:
                    gm[ti, n] = 0.0
            df[ti, 8 + l] = -NEGM
        m = dict(common)
        m["xall"] = np.ascontiguousarray(np.concatenate([prev, own], axis=0))
        m["memb"] = np.ascontiguousarray(mem[b])
        m["gmask"] = np.ascontiguousarray(np.broadcast_to(gm.reshape(1, 256), (128, 256)))
        m["dforce"] = np.ascontiguousarray(np.broadcast_to(df.reshape(1, 256), (128, 256)))
        in_maps.append(m)
    return in_maps


def kernel(x, mem, norm_w, w_in, w_alpha2, b_alpha, gla_norm_w, mem_norm_w, w_mem_kv, w_out, rel_bias, final_norm_w):
    f32 = np.float32
    in_maps = _prep(x, mem, norm_w, w_in, w_alpha2, b_alpha, gla_norm_w, mem_norm_w, w_mem_kv, w_out, rel_bias, final_norm_w)
    if "nc" not in _NC_CACHE:
        _NC_CACHE["nc"] = build_program((1, 2, 3, 4, 5))
    res = run_bass_kernel_spmd(_NC_CACHE["nc"], in_maps, core_ids=list(range(8)))
    out = np.empty((4, 4096, 1024), f32)
    for core in range(8):
        b, j = core // 2, core % 2
        out[b, j * T_OWN:(j + 1) * T_OWN] = res.results[core]["out"]
    return out
```
